# Optimizing a Trainium2 kernel written in Bass

```python
import jax, jax.numpy as jnp
from jax import lax
import numpy as np

D_MODEL = 2048
BATCH = 4
SEQ = 2048
DEPTH = 4

GRID_W = 64
CTX_LEN = 256
NH_M = 4
DH_M = 256
D_M = NH_M * DH_M
NG_F = 4
DG_F = 128
D_F = NG_F * DG_F
D_C = 512
D_FF = 5632
CHUNK = 128
CONV_W = 3
EPS = 1e-6
N_BRANCH = 3

OFF_GATES = 2 * D_M
N_STATE_COLS = 2 * D_M + 4 * NH_M
OFF_Q = N_STATE_COLS
OFF_O = OFF_Q + D_M
OFF_F = OFF_O + D_M
OFF_C = OFF_F + D_F
OFF_G = OFF_C + 3 * D_C
N_IN = OFF_G + N_BRANCH * D_MODEL

kernel_name = "hybrid_mlstm_fourier_shortconv_dit_block"


def rmsnorm(x, w):
    xf = x.astype(jnp.float32)
    y = xf * lax.rsqrt(jnp.mean(xf * xf, axis=-1, keepdims=True) + EPS)
    return (y * w.astype(jnp.float32)).astype(x.dtype)


def modulation(cond, w_mod, b_mod):
    m = (jax.nn.silu(cond) @ w_mod + b_mod).reshape(-1, 1, 6 * D_MODEL)
    return jnp.split(m, 6, axis=-1)


def modulate(x, w, shift, scale):
    return rmsnorm(x, w) * (1.0 + scale) + shift


def dwconv3(x, w, axis):
    n = x.shape[axis]
    half = CONV_W // 2
    pad = [(0, 0)] * x.ndim
    pad[axis] = (half, half)
    xp = jnp.pad(x, pad)
    out = lax.slice_in_dim(xp, 0, n, axis=axis) * w[0]
    for j in range(1, CONV_W):
        out = out + lax.slice_in_dim(xp, j, j + n, axis=axis) * w[j]
    return out


def conv_mix(x, w, grid, grid_axis):
    if grid is None:
        return dwconv3(x, w, 1)
    B, T, C = x.shape
    return dwconv3(x.reshape(B, grid[0], grid[1], C), w, grid_axis).reshape(B, T, C)


def mlstm_scan(q, k, v, i_pre, f_pre, state):
    B, T, H, Dh = k.shape
    nc = T // CHUNK

    def chunks(a):
        a = a.astype(jnp.float32).reshape((B, nc, CHUNK) + a.shape[2:])
        return jnp.swapaxes(jnp.moveaxis(a, 1, 0), 2, 3)

    with_h = q is not None
    xs = (chunks(k), chunks(v), chunks(i_pre), chunks(jax.nn.log_sigmoid(f_pre.astype(jnp.float32))))
    if with_h:
        xs = xs + (chunks(q),)
    causal = jnp.tril(jnp.ones((CHUNK, CHUNK), dtype=bool))

    def step(carry, inp):
        C, n, m = carry
        kc, vc, ic, lfc = inp[:4]
        b = jnp.cumsum(lfc, axis=-1)
        b_end = b[..., -1]
        a_end = b_end[..., None] - b + ic
        m_new = jnp.maximum(b_end + m, jnp.max(a_end, axis=-1))
        w_end = jnp.exp(a_end - m_new[..., None])
        keep = jnp.exp(b_end + m - m_new)
        C_new = keep[..., None, None] * C + jnp.einsum("bhsk,bhsv->bhkv", kc * w_end[..., None], vc)
        n_new = keep[..., None] * n + jnp.einsum("bhsk,bhs->bhk", kc, w_end)
        if not with_h:
            return (C_new, n_new, m_new), None
        qc = inp[4]
        log_d = jnp.where(causal, b[..., :, None] - b[..., None, :] + ic[..., None, :], -jnp.inf)
        inter = b + m[..., None]
        m_t = jnp.maximum(inter, jnp.max(log_d, axis=-1))
        decay = jnp.exp(log_d - m_t[..., None])
        g_inter = jnp.exp(inter - m_t)
        s = jnp.einsum("bhtk,bhsk->bhts", qc, kc) * decay
        num = g_inter[..., None] * jnp.einsum("bhtk,bhkv->bhtv", qc, C) + jnp.einsum("bhts,bhsv->bhtv", s, vc)
        den = g_inter * jnp.einsum("bhtk,bhk->bht", qc, n) + jnp.sum(s, axis=-1)
        h = num / jnp.maximum(jnp.abs(den), jnp.exp(-m_t))[..., None]
        return (C_new, n_new, m_new), h

    state, h = lax.scan(step, state, xs)
    if with_h:
        h = jnp.transpose(h, (1, 0, 3, 2, 4)).reshape(B, T, H * Dh)
    return h, state


def bidir_mlstm(q, k, v, gates, init_states):
    B, T, H, Dh = k.shape
    if init_states is None:
        zero = (jnp.zeros((B, H, Dh, Dh), jnp.float32), jnp.zeros((B, H, Dh), jnp.float32),
                jnp.zeros((B, H), jnp.float32))
        init_states = (zero, zero)
    rev = lambda a: None if a is None else jnp.flip(a, axis=1)
    h_f, st_f = mlstm_scan(q, k, v, gates[:, :, 0], gates[:, :, 1], init_states[0])
    h_b, st_b = mlstm_scan(rev(q), rev(k), rev(v), rev(gates[:, :, 2]), rev(gates[:, :, 3]), init_states[1])
    h = None if q is None else h_f + rev(h_b)
    return h, (st_f, st_b)


def mlstm_inputs(p_state, conv_k_w):
    B, T, _ = p_state.shape
    k = jax.nn.silu(dwconv3(p_state[..., :D_M], conv_k_w, 1)).reshape(B, T, NH_M, DH_M)
    v = p_state[..., D_M:2 * D_M].reshape(B, T, NH_M, DH_M)
    gates = p_state[..., OFF_GATES:N_STATE_COLS].reshape(B, T, 4, NH_M)
    return k, v, gates


def fourier_mix(xf):
    B, T, _ = xf.shape
    a = xf.astype(jnp.float32).reshape(B, T, NG_F, DG_F)
    y = jnp.fft.fft2(a, axes=(1, 3), norm="ortho").real
    return y.reshape(B, T, D_F).astype(xf.dtype)


def mixer(xn, p, grid, init_states):
    B, T, _ = xn.shape
    proj = xn @ p["w_in"] + p["b_in"]
    k, v, gates = mlstm_inputs(proj[..., :N_STATE_COLS], p["conv_k_w"])
    q = jax.nn.silu(dwconv3(proj[..., OFF_Q:OFF_O], p["conv_q_w"], 1)).reshape(B, T, NH_M, DH_M) * (DH_M ** -0.5)
    o_gate = proj[..., OFF_O:OFF_F]
    xf = proj[..., OFF_F:OFF_C]
    cb = proj[..., OFF_C:OFF_C + D_C]
    cc = proj[..., OFF_C + D_C:OFF_C + 2 * D_C]
    cx = proj[..., OFF_C + 2 * D_C:OFF_G]
    gm = proj[..., OFF_G:]

    h, states = bidir_mlstm(q, k, v, gates, init_states)
    h = rmsnorm(h.astype(xn.dtype).reshape(B, T, NH_M, DH_M), p["mlstm_norm_w"].reshape(NH_M, DH_M)).reshape(B, T, D_M)
    y_m = (jax.nn.sigmoid(o_gate) * h) @ p["w_pm"]
    y_f = fourier_mix(xf) @ p["w_pf"]
    y_c = (cb * conv_mix(cc * cx, p["conv_c_w"], grid, 2)) @ p["w_pc"]
    g = jax.nn.sigmoid(gm).reshape(B, T, N_BRANCH, D_MODEL)
    merged = g[..., 0, :] * y_m + g[..., 1, :] * y_f + g[..., 2, :] * y_c
    return merged @ p["w_o"], states


def ctx_states(xn_c, p):
    p_state = xn_c @ p["w_in"][:, :N_STATE_COLS] + p["b_in"][:N_STATE_COLS]
    k, v, gates = mlstm_inputs(p_state, p["conv_k_w"])
    _, states = bidir_mlstm(None, k, v, gates, None)
    return states


def conv_ffn(xn, p, grid):
    u = xn @ p["w_up"]
    a = conv_mix(u[..., :D_FF], p["conv_ff_w"], grid, 1)
    return (jax.nn.silu(a) * u[..., D_FF:]) @ p["w_down"]


def setup_inputs(seed: int = 0) -> dict:
    key = jax.random.key(seed)
    ks = jax.random.split(key, 24)
    D = D_MODEL
    nrm = lambda k, shape, scale: scale * jax.random.normal(k, shape, jnp.float32)
    b_in = nrm(ks[9], (DEPTH, N_IN), 0.02)
    b_in = b_in.at[:, OFF_GATES + NH_M:OFF_GATES + 2 * NH_M].add(3.0)
    b_in = b_in.at[:, OFF_GATES + 3 * NH_M:OFF_GATES + 4 * NH_M].add(3.0)
    return {
        "x": nrm(ks[0], (BATCH, SEQ, D), 1.0),
        "c": nrm(ks[1], (BATCH, D), 1.0),
        "ctx": nrm(ks[2], (BATCH, CTX_LEN, D), 1.0),
        "c_ctx": nrm(ks[3], (D,), 1.0),
        "w_mod": nrm(ks[4], (DEPTH, D, 6 * D), 0.5 * D ** -0.5),
        "b_mod": nrm(ks[5], (DEPTH, 6 * D), 0.02),
        "norm1_w": 1.0 + nrm(ks[6], (DEPTH, D), 0.02),
        "norm2_w": 1.0 + nrm(ks[7], (DEPTH, D), 0.02),
        "w_in": nrm(ks[8], (DEPTH, D, N_IN), D ** -0.5),
        "b_in": b_in,
        "conv_q_w": nrm(ks[10], (DEPTH, CONV_W, D_M), CONV_W ** -0.5),
        "conv_k_w": nrm(ks[11], (DEPTH, CONV_W, D_M), CONV_W ** -0.5),
        "mlstm_norm_w": 1.0 + nrm(ks[12], (DEPTH, D_M), 0.02),
        "w_pm": nrm(ks[13], (DEPTH, D_M, D), D_M ** -0.5),
        "w_pf": nrm(ks[14], (DEPTH, D_F, D), D_F ** -0.5),
        "w_pc": nrm(ks[15], (DEPTH, D_C, D), D_C ** -0.5),
        "conv_c_w": nrm(ks[16], (DEPTH, CONV_W, D_C), CONV_W ** -0.5),
        "w_o": nrm(ks[17], (DEPTH, D, D), D ** -0.5),
        "w_up": nrm(ks[18], (DEPTH, D, 2 * D_FF), D ** -0.5),
        "conv_ff_w": nrm(ks[19], (DEPTH, CONV_W, D_FF), CONV_W ** -0.5),
        "w_down": nrm(ks[20], (DEPTH, D_FF, D), D_FF ** -0.5),
        "final_norm_w": 1.0 + nrm(ks[21], (D,), 0.02),
    }


def reference(x, c, ctx, c_ctx, w_mod, b_mod, norm1_w, norm2_w, w_in, b_in, conv_q_w, conv_k_w,
              mlstm_norm_w, w_pm, w_pf, w_pc, conv_c_w, w_o, w_up, conv_ff_w, w_down, final_norm_w):
    rows = x.shape[1] // GRID_W
    grid = (rows, GRID_W)
    h, hc = x, ctx
    for l in range(DEPTH):
        p = {"w_in": w_in[l], "b_in": b_in[l], "conv_q_w": conv_q_w[l], "conv_k_w": conv_k_w[l],
             "mlstm_norm_w": mlstm_norm_w[l], "w_pm": w_pm[l], "w_pf": w_pf[l], "w_pc": w_pc[l],
             "conv_c_w": conv_c_w[l], "w_o": w_o[l], "w_up": w_up[l], "conv_ff_w": conv_ff_w[l],
             "w_down": w_down[l]}
        sh1, sc1, g1, sh2, sc2, g2 = modulation(c, w_mod[l], b_mod[l])
        csh1, csc1, cg1, csh2, csc2, cg2 = modulation(c_ctx, w_mod[l], b_mod[l])
        xn_c = modulate(hc, norm1_w[l], csh1, csc1)
        if l == DEPTH - 1:
            states = ctx_states(xn_c, p)
        else:
            out_c, states = mixer(xn_c, p, None, None)
            hc = hc + cg1 * out_c
            hc = hc + cg2 * conv_ffn(modulate(hc, norm2_w[l], csh2, csc2), p, None)
        out, _ = mixer(modulate(h, norm1_w[l], sh1, sc1), p, grid, states)
        h = h + g1 * out
        h = h + g2 * conv_ffn(modulate(h, norm2_w[l], sh2, sc2), p, grid)
    return rmsnorm(h, final_norm_w)
```

```python
import contextlib
import numpy as np
import concourse.bass as bass
import concourse.mybir as mybir
from concourse.bass_utils import run_bass_kernel_spmd

F32 = mybir.dt.float32
BF16 = mybir.dt.bfloat16
AF = mybir.ActivationFunctionType
ALU = mybir.AluOpType

ENGS = ["pe", "act", "dve", "pool", "sp"]

D = 2048
L_CTX = 256
SEQ = 2048
NT = L_CTX + SEQ
NCH = NT // 128
DEPTH = 4
D_M = 1024
D_F = 512
D_C = 512
D_FF = 5632
NFF = D_FF // 128
OFF_GATES = 2048
OFF_Q = 2064
OFF_O = OFF_Q + 1024
OFF_F = OFF_O + 1024
OFF_C = OFF_F + 512
OFF_G = OFF_C + 3 * 512
N_IN = OFF_G + 3 * D
EPS = 1e-6
TT = [(0, 256), (256, 768), (768, 1280), (1280, 1792), (1792, 2304)]
NEG = -30000.0

V_N1, V_N2, V_BMOD, V_BIN, V_CQ, V_CK, V_MN, V_CC, V_CFF, V_FN = 0, 16, 32, 128, 216, 240, 264, 272, 284, 416
NV = 432
FM = []
for i in range(8):
    FM.append(("k", i, 0 + 128 * i))
for i in range(8):
    FM.append(("q", i, OFF_Q + 128 * i))
for i in range(8):
    FM.append(("o", i, OFF_O + 128 * i))
for i in range(4):
    FM.append(("xf", i, OFF_F + 128 * i))
for i in range(4):
    FM.append(("cb", i, OFF_C + 128 * i))
for i in range(4):
    FM.append(("cc", i, OFF_C + 512 + 128 * i))
for i in range(4):
    FM.append(("cx", i, OFF_C + 1024 + 128 * i))
for i in range(48):
    FM.append(("gm", i, OFF_G + 128 * i))
FMIDX = {(k, i): n for n, (k, i, _) in enumerate(FM)}
C_ID, C_MF, C_MB, C_SEL, C_I4, C_DFT, C_OH = 0, 128, 256, 384, 896, 900, 1156
NCST = 1156 + 512 + 128


class Buf:
    __slots__ = ("name", "w", "r")

    def __init__(self, name=""):
        self.name = name
        self.w = None
        self.r = {}


class Prog:
    def __init__(self, ndma=28):
        self.nc = bass.Bass("TRN2", target_bir_lowering=False)
        self.ndma = ndma
        self.streams = {e: [] for e in ENGS}
        self.cnt = {e: 0 for e in ENGS}
        self.seen = {e: {} for e in ENGS}
        self.snap = {}
        self.dma_cnt = [0] * ndma
        self.dma_rr = 0
        self.dma_rr2 = 0
        self.waited = {}
        self.es = contextlib.ExitStack()
        self.n_ops = 0

    def sb(self, name, shape, dt):
        return self.es.enter_context(self.nc.sbuf_tensor("sb_" + name, list(shape), dt))

    def ps(self, name, shape, dt=F32):
        return self.es.enter_context(self.nc.psum_tensor("pp_" + name, list(shape), dt))

    def dram(self, name, shape, dt, kind="Internal"):
        return self.nc.dram_tensor(name, list(shape), dt, kind=kind)

    def op(self, eng, fn, reads=(), writes=(), dma=False):
        deps = {}

        def add(d):
            if d is None:
                return
            tl, s = d
            if deps.get(tl, 0) < s:
                deps[tl] = s

        for b in reads:
            add(b.w)
        for b in writes:
            add(b.w)
            for tl, s in b.r.items():
                add((tl, s))
        k = None
        if dma:
            if eng == "pool":
                k = self.ndma - 8 + self.dma_rr2
                self.dma_rr2 = (self.dma_rr2 + 1) % 8
            else:
                k = self.dma_rr
                self.dma_rr = (k + 1) % (self.ndma - 8)
            if self.dma_cnt[k] > 0:
                add(("d%d" % k, self.dma_cnt[k]))
        seen = self.seen[eng]
        waits = []
        for tl, s in deps.items():
            if tl == eng and eng == "pe":
                continue
            if seen.get(tl, 0) >= s:
                continue
            waits.append((tl, s))
        for tl, s in waits:
            sn = self.snap.get((tl, s))
            if sn:
                for t2, s2 in sn.items():
                    if seen.get(t2, 0) < s2:
                        seen[t2] = s2
            if seen.get(tl, 0) < s:
                seen[tl] = s
            self.waited.setdefault(tl, set()).add(s)
        if dma:
            self.dma_cnt[k] += 1
            done = ("d%d" % k, self.dma_cnt[k])
        else:
            self.cnt[eng] += 1
            done = (eng, self.cnt[eng])
        self.snap[done] = dict(seen)
        self.streams[eng].append((waits, fn, done))
        for b in reads:
            if b.r.get(done[0], 0) < done[1]:
                b.r[done[0]] = done[1]
        for b in writes:
            b.w = done
            b.r = {}
        self.n_ops += 1
        return done

    def barrier(self, engs=("pe", "act", "dve", "sp", "pool")):
        targets = []
        for k in range(self.ndma):
            if self.dma_cnt[k] > 0:
                targets.append(("d%d" % k, self.dma_cnt[k]))
        for e in ENGS:
            if self.cnt[e] > 0:
                targets.append((e, self.cnt[e]))
        for eng in engs:
            seen = self.seen[eng]
            waits = []
            for tl, s in targets:
                if tl == eng or seen.get(tl, 0) >= s:
                    continue
                waits.append((tl, s))
                seen[tl] = s
                if not tl.startswith("d"):
                    self.waited.setdefault(tl, set()).add(s)
            for tl, s in waits:
                sn = self.snap.get((tl, s))
                if sn:
                    for t2, s2 in sn.items():
                        if seen.get(t2, 0) < s2:
                            seen[t2] = s2
            if eng != "pe":
                seen[eng] = self.cnt[eng]
            self.streams[eng].append((waits, None, None))

    def emit(self):
        nc = self.nc
        self.snap = None
        sems = {}
        for e in ENGS:
            sems[e] = self.es.enter_context(nc.semaphore("s_" + e))
        for k in range(self.ndma):
            sems["d%d" % k] = self.es.enter_context(nc.semaphore("s_d%d" % k))
        rank = {}
        for tl, ss in self.waited.items():
            if tl.startswith("d"):
                continue
            rank[tl] = {s: i + 1 for i, s in enumerate(sorted(ss))}

        def val(tl, s):
            if tl.startswith("d"):
                return 16 * s
            return rank[tl][s]

        def replay(ename, eng):
            for waits, fn, done in self.streams[ename]:
                for tl, s in waits:
                    eng.wait_ge(sems[tl], val(tl, s))
                if fn is None:
                    continue
                inst = fn(eng)
                tl, s = done
                if tl.startswith("d"):
                    inst.then_inc(sems[tl], 16)
                elif tl in rank and s in rank[tl]:
                    inst.then_inc(sems[tl], 1)

        block = self.es.enter_context(nc.Block())

        @block.tensor
        def _(eng):
            replay("pe", eng)

        @block.scalar
        def _(eng):
            replay("act", eng)

        @block.vector
        def _(eng):
            replay("dve", eng)

        @block.gpsimd
        def _(eng):
            replay("pool", eng)

        @block.sync
        def _(eng):
            replay("sp", eng)

        self.es.close()
        return nc


class Ring:
    def __init__(self, tiles):
        self.tiles = tiles
        self.bufs = [Buf() for _ in tiles]
        self.i = 0

    def get(self):
        i = self.i
        self.i = (i + 1) % len(self.tiles)
        return self.tiles[i], self.bufs[i]


def build(n_layers=DEPTH, debug=False, stop_after=None):
    p = Prog()
    nc = p.nc
    okind = "ExternalOutput" if debug else "Internal"

    def din(name, shape):
        return p.dram(name, shape, F32, kind="ExternalInput").ap()

    hT0 = din("hT0", [D, NT])
    ccp = din("ccp", [128, 32])
    vec_d = din("vec", [DEPTH, 128, NV])
    bvrep_d = din("bvrep", [DEPTH, 128, 1024])
    bgt_d = din("bgt", [DEPTH, 4, 4])
    cst_d = din("cst", [128, NCST])
    dftl_d = din("dft_lat", [2, SEQ, SEQ])
    dftc_d = din("dft_ctx", [2, L_CTX, L_CTX])
    w_mod = din("w_mod", [DEPTH, D, 6 * D])
    w_in = din("w_in", [DEPTH, D, N_IN])
    w_pm = din("w_pm", [DEPTH, D_M, D])
    w_pf = din("w_pf", [DEPTH, D_F, D])
    w_pc = din("w_pc", [DEPTH, D_C, D])
    w_o = din("w_o", [DEPTH, D, D])
    w_up = din("w_up", [DEPTH, D, 2 * D_FF])
    w_down = din("w_down", [DEPTH, D_FF, D])
    out_d = p.dram("outT", [D, SEQ], F32, kind="ExternalOutput").ap()

    def scratch(name, shape, dt):
        return p.dram(name, shape, dt, kind=okind).ap()

    d_h = scratch("d_h", [16, 128, NT], F32)
    d_kT = scratch("d_kT", [8, 128, NT], BF16)
    d_qT = scratch("d_qT", [8, 128, NT], BF16)
    d_og = scratch("d_og", [8, 128, NT], BF16)
    d_ktok = scratch("d_ktok", [NCH, 128, 1024], BF16)
    d_v = scratch("d_v", [NCH, 128, 4 * 257], BF16)
    d_xf = scratch("d_xf", [4, 128, NT], BF16)
    d_yc = scratch("d_yc", [4, 128, NT], BF16)
    d_yf = scratch("d_yf", [4, 128, NT], BF16)
    d_g = scratch("d_g", [48, 128, NT], BF16)
    d_hs = scratch("d_hs", [NCH, 128, 1024], F32)
    d_hm = scratch("d_hm", [8, 128, NT], BF16)
    d_mg = scratch("d_mg", [16, 128, NT], BF16)
    d_hid = scratch("d_hid", [NFF, 128, NT], BF16)
    if debug:
        d_xn = scratch("d_xn", [16, 128, NT], BF16)
        d_rows = scratch("d_rows", [16, 4, NT], F32)
        d_mod = scratch("d_mod", [128, 192], F32)
    B_dh = [[Buf() for _ in TT] for _ in range(16)]

    cst = p.sb("cst", [128, NCST], F32)
    vec = p.sb("vec", [128, DEPTH, NV], F32)
    ones_bf = p.sb("ones_bf", [128, 128], BF16)
    id_bf = p.sb("id_bf", [128, 128], BF16)
    dft_cs = p.sb("dft_cs", [128, 256], BF16)
    dftc_sb = p.sb("dftc_sb", [128, 2, 2, 256], BF16)
    scc = p.sb("scc", [128, 16, 2], F32)
    modc = p.sb("modc", [128, 96, 2], F32)
    amod = p.sb("amod", [128, 2, 16, 2], F32)
    B_cst, B_vec, B_ones, B_idbf, B_dftcs, B_dftc, B_scc, B_modc, B_amod = (Buf() for _ in range(9))
    WR = Ring([p.sb("wr%d" % i, [128, 16, 512], BF16) for i in range(3)])
    PS = Ring([p.ps("ps%d" % i, [128, 512]) for i in range(7)])
    psT = p.ps("psT", [128, 8, 128], BF16)
    B_psT = Buf()

    def dma(q, out, in_, reads=(), writes=(), **kw):
        p.op(q, lambda e: e.dma_start(out=out, in_=in_, **kw), reads, writes, dma=True)

    def load_w(pieces):
        t, b = WR.get()
        for src, k0, c0 in pieces:
            nk = src.shape[0] // 128
            w = src.shape[1]
            dma("pool", t[:, k0:k0 + nk, c0:c0 + w], src.rearrange("(k p) w -> p k w", p=128), writes=[b])
        return t, b

    def mm(ps_ap, pairs, reads, psb):
        n = len(pairs)
        for i, (l, r) in enumerate(pairs):
            p.op("pe", lambda e, l=l, r=r, i=i: e.matmul(ps_ap, lhsT=l, rhs=r, start=(i == 0), stop=(i == n - 1)),
                 reads=reads, writes=[psb])

    def act(out, in_, func, reads, writes, bias=None, scale=None):
        kw = {}
        if bias is not None:
            kw["bias"] = bias
        if scale is not None:
            kw["scale"] = scale
        p.op("act", lambda e: e.activation(out=out, in_=in_, func=func, **kw), reads, writes)

    def dve(name, reads, writes, **kw):
        p.op("dve", lambda e: getattr(e, name)(**kw), reads, writes)

    dma("sp", cst[:], cst_d, writes=[B_cst])
    dma("sp", vec[:], vec_d.rearrange("l p v -> p l v"), writes=[B_vec])
    dma("sp", scc[:], ccp.rearrange("p (k c) -> p k c", c=2), writes=[B_scc])
    act(scc[:], scc[:], AF.Silu, [B_scc], [B_scc])
    maskbf = p.sb("maskbf", [128, 256], BF16)
    B_maskbf = Buf()
    dve("tensor_copy", [B_cst], [B_maskbf], out=maskbf[:], in_=cst[:, C_MF:C_MF + 256])
    scc_bf = p.sb("scc_bf", [128, 16, 2], BF16)
    B_sccbf = Buf()
    dve("tensor_copy", [B_scc], [B_sccbf], out=scc_bf[:], in_=scc[:])
    dve("memset", [], [B_ones], ap=ones_bf[:], constant=1.0)
    dve("tensor_copy", [B_cst], [B_idbf], out=id_bf[:], in_=cst[:, C_ID:C_ID + 128])
    dve("tensor_copy", [B_cst], [B_dftcs], out=dft_cs[:], in_=cst[:, C_DFT:C_DFT + 256])
    for j in range(2):
        dma("pool", dftc_sb[:, j, :, :], dftc_d[j].rearrange("(k p) w -> p k w", p=128), writes=[B_dftc])
    for k in range(16):
        for ti, (t0, t1) in enumerate(TT):
            dma("sp", d_h[k, :, t0:t1], hT0[k * 128:(k + 1) * 128, t0:t1], writes=[B_dh[k][ti]])

    ident = cst[:, C_ID:C_ID + 128]

    def phase_mod(l):
        if True:
            pm, pmb = PS.get()
            for grp in range(24):
                t, b = load_w([(w_mod[l, :, grp * 512:(grp + 1) * 512], 0, 0)])
                for j4 in range(4):
                    j = grp * 4 + j4
                    mm(pm[:, 2 * j:2 * j + 2], [(t[:, k, j4 * 128:(j4 + 1) * 128], scc_bf[:, k, :]) for k in range(16)],
                       [b, B_sccbf], pmb)
            dve("tensor_tensor", [pmb, B_vec], [B_modc], out=modc[:],
                in0=pm[:, 0:192].rearrange("p (j c) -> p j c", c=2),
                in1=vec[:, l, V_BMOD:V_BMOD + 96].unsqueeze(2).to_broadcast([128, 96, 2]), op=ALU.add)
            for w in range(2):
                sc = modc[:, 16 + 48 * w:32 + 48 * w, :]
                nw = vec[:, l, (V_N1 if w == 0 else V_N2):(V_N1 if w == 0 else V_N2) + 16]
                dve("scalar_tensor_tensor", [B_modc, B_vec], [B_amod], out=amod[:, w, :, :], in0=sc, scalar=1.0,
                    in1=nw.unsqueeze(2).to_broadcast([128, 16, 2]), op0=ALU.add, op1=ALU.mult)
            if debug:
                dma("sp", d_mod, modc[:].rearrange("p j c -> p (j c)"), reads=[B_modc])
            p.barrier()

    def phase_norm(l, w, xn, B_xn, final=False):
        fx = "F" if final else ""
        hA = [nc.sbuf_tensor("hA%s%d_%d_%d" % (fx, l, w, i), [128, 16, 256], F32) for i in range(3)]
        sqt = [nc.sbuf_tensor("sq%s%d_%d_%d" % (fx, l, w, i), [128, 16, 256], BF16) for i in range(2)]
        rst = [nc.sbuf_tensor("rs%s%d_%d_%d" % (fx, l, w, i), [128, 256], F32) for i in range(2)]
        with hA[0] as h0, hA[1] as h1, hA[2] as h2, sqt[0] as sq0, sqt[1] as sq1, rst[0] as rs0, rst[1] as rs1:
            ring = Ring([h0, h1, h2])
            sqr = Ring([sq0, sq1])
            rsr = Ring([rs0, rs1])
            n = 256
            subs = [s9 for s9 in range(NT // 256) if not (final and s9 == 0)]

            def stage1(s9):
                t0, t1 = 256 * s9, 256 * s9 + 256
                ti = 0 if s9 == 0 else 1 + (s9 - 1) // 2
                t, b = ring.get()
                sq, B_sq = sqr.get()
                rs, B_rs = rsr.get()
                dma("sp", t[:, :, :n], d_h[:, :, t0:t1].rearrange("k p t -> p k t"),
                    reads=[B_dh[k][ti] for k in range(16)], writes=[b])
                act(sq[:, :, :n], t[:, :, :n], AF.Square, [b], [B_sq])
                ps, psb = PS.get()
                mm(ps[:, :n], [(ones_bf[:, :], sq[:, k, :n]) for k in range(16)], [B_ones, B_sq], psb)
                act(rs[:, :n], ps[:, :n], AF.Sqrt, [psb], [B_rs], bias=EPS, scale=1.0 / D)
                dve("reciprocal", [B_rs], [B_rs], out=rs[:, :n], in_=rs[:, :n])
                return (s9, t, b, rs, B_rs)

            def stage2(st):
                s9, t, b, rs, B_rs = st
                t0, t1 = 256 * s9, 256 * s9 + 256
                ti = 0 if s9 == 0 else 1 + (s9 - 1) // 2
                seg = 1 if ti == 0 else 0
                for k in range(16):
                    if final:
                        sc_ap = vec[:, 0, V_FN + k:V_FN + k + 1]
                    else:
                        sc_ap = amod[:, w, k, seg:seg + 1]
                    dve("scalar_tensor_tensor", [b, B_rs, B_amod, B_vec], [b], out=t[:, k, :n], in0=t[:, k, :n],
                        scalar=sc_ap, in1=rs[:, :n], op0=ALU.mult, op1=ALU.mult)
                    if not final:
                        act(xn[:, k, t0:t1], t[:, k, :n], AF.Identity, [b, B_modc], [B_xn[ti]],
                            bias=modc[:, 48 * w + k, seg:seg + 1])
                if final:
                    dma("sp", out_d.rearrange("(k p) t -> p k t", p=128)[:, :, t0 - L_CTX:t1 - L_CTX], t[:, :, :n], reads=[b])

            prev = None
            for s9 in subs:
                cur = stage1(s9)
                if prev is not None:
                    stage2(prev)
                prev = cur
            stage2(prev)
            if debug and not final and w == 0:
                dma("sp", d_xn.rearrange("k p t -> p k t"), xn[:], reads=B_xn)
            p.barrier()

    def conv3(O, U, wcols, segs, shift, B_O, B_U):
        dve("tensor_scalar", [B_U, B_vec], [B_O], out=O[:, :], in0=U[:, :], scalar1=wcols[1], scalar2=None, op0=ALU.mult)
        for (a, b_, rows) in segs:
            if rows is None:
                dve("scalar_tensor_tensor", [B_U, B_vec, B_O], [B_O], out=O[:, a + shift:b_], in0=U[:, a:b_ - shift],
                    scalar=wcols[0], in1=O[:, a + shift:b_], op0=ALU.mult, op1=ALU.add)
                dve("scalar_tensor_tensor", [B_U, B_vec, B_O], [B_O], out=O[:, a:b_ - shift], in0=U[:, a + shift:b_],
                    scalar=wcols[2], in1=O[:, a:b_ - shift], op0=ALU.mult, op1=ALU.add)
            else:
                Ov = O[:, a:b_].rearrange("p (r c) -> p r c", c=rows)
                Uv = U[:, a:b_].rearrange("p (r c) -> p r c", c=rows)
                dve("scalar_tensor_tensor", [B_U, B_vec, B_O], [B_O], out=Ov[:, :, 1:rows], in0=Uv[:, :, 0:rows - 1],
                    scalar=wcols[0], in1=Ov[:, :, 1:rows], op0=ALU.mult, op1=ALU.add)
                dve("scalar_tensor_tensor", [B_U, B_vec, B_O], [B_O], out=Ov[:, :, 0:rows - 1], in0=Uv[:, :, 1:rows],
                    scalar=wcols[2], in1=Ov[:, :, 0:rows - 1], op0=ALU.mult, op1=ALU.add)

    SEG_SEQ = [(0, L_CTX, None), (L_CTX, NT, None)]

    def phase_inproj(l, xn, B_xn):
        U = [nc.sbuf_tensor("U%d_%d" % (l, i), [128, NT], F32) for i in range(3)]
        R = [nc.sbuf_tensor("R%d_%d" % (l, i), [128, NT], BF16) for i in range(2)]
        Tt = nc.sbuf_tensor("Tt%d" % l, [128, NCH, 128], BF16)
        Vs = [nc.sbuf_tensor("Vs%d_%d" % (l, i), [128, 2, 257], BF16) for i in range(3)]
        bv = nc.sbuf_tensor("bv%d" % l, [128, 1024], F32)
        wg = nc.sbuf_tensor("wg%d" % l, [128, 16, 16], BF16)
        bg = nc.sbuf_tensor("bg%d" % l, [4, 4], F32)
        grow = nc.sbuf_tensor("grow%d" % l, [4, NT], F32)
        with U[0] as U0, U[1] as U1, U[2] as U2, R[0] as R0, R[1] as R1, Tt as Tt_, Vs[0] as V0, Vs[1] as V1, Vs[2] as V2, \
                bv as bv_, wg as wg_, bg as bg_, grow as grow_:
            Ur = Ring([U0, U1, U2])
            Rr = Ring([R0, R1])
            Vr = Ring([V0, V1, V2])
            B_Tt, B_bv, B_wg, B_bg, B_grow = Buf(), Buf(), Buf(), Buf(), Buf()
            dma("sp", bv_[:], bvrep_d[l], writes=[B_bv])
            dma("sp", bg_[:], bgt_d[l], writes=[B_bg])
            dma("pool", wg_[:], w_in[l, :, OFF_GATES:OFF_GATES + 16].rearrange("(k p) g -> p k g", p=128), writes=[B_wg])
            for vt, vb in zip(Vr.tiles, Vr.bufs):
                dve("memset", [], [vb], ap=vt[:, :, 256:257], constant=1.0)

            def proj_rows(wt, wb, c0, bias_idx, dst, dstb, func=AF.Identity, scale=None):
                for ti, (t0, t1) in enumerate(TT):
                    n = t1 - t0
                    ps, psb = PS.get()
                    mm(ps[:, :n], [(wt[:, k, c0:c0 + 128], xn[:, k, t0:t1]) for k in range(16)], [wb, B_xn[ti]], psb)
                    act(dst[:, t0:t1], ps[:, :n], func, [psb, B_vec], [dstb],
                        bias=vec[:, l, V_BIN + bias_idx:V_BIN + bias_idx + 1], scale=scale)

            for g in range(4):
                for ti, (t0, t1) in enumerate(TT):
                    n = t1 - t0
                    ps, psb = PS.get()
                    mm(ps[0:4, :n], [(wg_[:, k, 4 * g:4 * g + 4], xn[:, k, t0:t1]) for k in range(16)], [B_wg, B_xn[ti]], psb)
                    act(grow_[:, t0:t1], ps[0:4, :n], AF.Identity, [psb, B_bg], [B_grow], bias=bg_[:, g:g + 1])
                dma("sp", d_gr[g], grow_[:], reads=[B_grow])

            for kind, dT, cw, qs in (("k", d_kT, V_CK, None), ("q", d_qT, V_CQ, 1.0 / 16.0)):
                for half in range(2):
                    off0 = (0 if kind == "k" else OFF_Q) + 512 * half
                    wt, wb = load_w([(w_in[l, :, off0:off0 + 512], 0, 0)])
                    for i4 in range(4):
                        i = half * 4 + i4
                        Ut, Ub = Ur.get()
                        proj_rows(wt, wb, 128 * i4, FMIDX[(kind, i)], Ut, Ub)
                        Ot, Ob = Ur.get()
                        wc = [vec[:, l, cw + 8 * j + i:cw + 8 * j + i + 1] for j in range(3)]
                        conv3(Ot, Ut, wc, SEG_SEQ, 1, Ob, Ub)
                        Rt, Rb = Rr.get()
                        if qs is None:
                            act(Rt[:, :], Ot[:, :], AF.Silu, [Ob], [Rb])
                        else:
                            act(Ot[:, :], Ot[:, :], AF.Silu, [Ob], [Ob])
                            act(Rt[:, :], Ot[:, :], AF.Copy, [Ob], [Rb], scale=qs)
                        dma("sp", dT[i], Rt[:, :], reads=[Rb])
                        if kind == "k":
                            for c8 in range(0, NCH, 8):
                                nn = min(8, NCH - c8)
                                for c in range(c8, c8 + nn):
                                    p.op("pe", lambda e, c=c, c8=c8, Rt=Rt: e.transpose(out=psT[:, c - c8, :], in_=Rt[:, c * 128:(c + 1) * 128],
                                                                                         identity=id_bf[:, :]),
                                         reads=[Rb, B_idbf], writes=[B_psT])
                                dve("tensor_copy", [B_psT], [B_Tt], out=Tt_[:, c8:c8 + nn, :], in_=psT[:, 0:nn, :])
                            dma("sp", d_ktok[:, :, 128 * i:128 * (i + 1)].rearrange("c p f -> p c f"), Tt_[:, :, :], reads=[B_Tt])

            for half in range(2):
                wt, wb = load_w([(w_in[l, :, 1024 + 512 * half:1536 + 512 * half], 0, 0)])
                for c in range(NCH):
                    ti = 0 if c < 2 else 1 + (c - 2) // 4
                    ps, psb = PS.get()
                    mm(ps[:, :], [(xn[:, k, c * 128:(c + 1) * 128], wt[:, k, :]) for k in range(16)], [wb, B_xn[ti]], psb)
                    vt, vb = Vr.get()
                    dve("tensor_tensor", [psb, B_bv], [vb], out=vt[:, :, 0:256], in0=ps[:, :].rearrange("p (h d) -> p h d", d=256),
                        in1=bv_[:, 512 * half:512 * half + 512].rearrange("p (h d) -> p h d", d=256), op=ALU.add)
                    dma("sp", d_v[c].rearrange("p (h d) -> p h d", d=257)[:, 2 * half:2 * half + 2, :], vt[:, :, :], reads=[vb])

            for half in range(2):
                wt, wb = load_w([(w_in[l, :, OFF_O + 512 * half:OFF_O + 512 * half + 512], 0, 0)])
                for i4 in range(4):
                    i = half * 4 + i4
                    Rt, Rb = Rr.get()
                    proj_rows(wt, wb, 128 * i4, FMIDX[("o", i)], Rt, Rb, func=AF.Sigmoid)
                    dma("sp", d_og[i], Rt[:, :], reads=[Rb])
            wt, wb = load_w([(w_in[l, :, OFF_F:OFF_F + 512], 0, 0)])
            for i in range(4):
                Rt, Rb = Rr.get()
                proj_rows(wt, wb, 128 * i, FMIDX[("xf", i)], Rt, Rb)
                dma("sp", d_xf[i], Rt[:, :], reads=[Rb])
            for i in range(4):
                wt, wb = load_w([(w_in[l, :, OFF_C + 512 * j + 128 * i:OFF_C + 512 * j + 128 * i + 128], 0, 128 * j) for j in range(3)])
                Ucc, Bcc = Ur.get()
                proj_rows(wt, wb, 128, FMIDX[("cc", i)], Ucc, Bcc)
                Ucx, Bcx = Ur.get()
                proj_rows(wt, wb, 256, FMIDX[("cx", i)], Ucx, Bcx)
                dve("tensor_tensor", [Bcc, Bcx], [Bcx], out=Ucx[:, :], in0=Ucx[:, :], in1=Ucc[:, :], op=ALU.mult)
                wc = [vec[:, l, V_CC + 4 * j + i:V_CC + 4 * j + i + 1] for j in range(3)]
                conv3(Ucc, Ucx, wc, [(0, L_CTX, None), (L_CTX, NT, 64)], 1, Bcc, Bcx)
                Ucb, Bcb = Ur.get()
                proj_rows(wt, wb, 0, FMIDX[("cb", i)], Ucb, Bcb)
                Rt, Rb = Rr.get()
                dve("tensor_tensor", [Bcc, Bcb], [Rb], out=Rt[:, :], in0=Ucc[:, :], in1=Ucb[:, :], op=ALU.mult)
                dma("sp", d_yc[i], Rt[:, :], reads=[Rb])
            for grp in range(12):
                wt, wb = load_w([(w_in[l, :, OFF_G + 512 * grp:OFF_G + 512 * grp + 512], 0, 0)])
                for i4 in range(4):
                    i = grp * 4 + i4
                    Rt, Rb = Rr.get()
                    proj_rows(wt, wb, 128 * i4, FMIDX[("gm", i)], Rt, Rb, func=AF.Sigmoid)
                    dma("sp", d_g[i], Rt[:, :], reads=[Rb])
            p.barrier()

    d_gr = scratch("d_gr", [4, 4, NT], F32)

    def phase_mlstm(l):
        es = contextlib.ExitStack()
        es_pre = contextlib.ExitStack()
        T = {}
        Bn = {}

        def alloc(stack, nm, shp, dt):
            T[nm] = stack.enter_context(nc.sbuf_tensor("%s_L%d" % (nm, l), shp, dt))
            Bn[nm] = Buf(nm)
        for d_ in range(2):
            for nm in ("Xa", "Xm"):
                alloc(es, nm + str(d_), [4, NT], F32)
        alloc(es, "colsb", [128, 432], F32)
        alloc(es, "keepb", [128, 144], F32)
        for nm in ("X1", "X2", "X3", "X4"):
            alloc(es_pre, nm, [4, NT], F32)
        alloc(es_pre, "mrow", [4, 2, 20], F32)
        alloc(es_pre, "mlast", [4, 2, 18], F32)
        alloc(es_pre, "klog", [4, 2, 18], F32)
        if True:
            colsb, keepb, mrow, mlast, klog = T["colsb"], T["keepb"], T["mrow"], T["mlast"], T["klog"]
            X1, X2, X3, X4 = T["X1"], T["X2"], T["X3"], T["X4"]
            sel4 = cst[0:4, C_SEL:C_SEL + 512].rearrange("p (h s) -> p h s", s=128)
            oh4 = cst[0:4, C_OH:C_OH + 512].rearrange("p (h s) -> p h s", s=128)
            id4 = cst[0:4, C_I4:C_I4 + 4]
            pcol, pcolb = PS.get()
            pkeep, pkeepb = PS.get()
            dve("memset", [], [Bn["mrow"]], ap=mrow[:], constant=0.0)
            orders = [list(range(NCH)), [1, 0] + list(range(NCH - 1, 1, -1))]
            for d_ in range(2):
                Xa, Xm = T["Xa%d" % d_], T["Xm%d" % d_]
                Ba, Bm = Bn["Xa%d" % d_], Bn["Xm%d" % d_]
                B1, B2, B3, B4 = Bn["X1"], Bn["X2"], Bn["X3"], Bn["X4"]
                dma("sp", Xa[:], d_gr[2 * d_], writes=[Ba])
                dma("sp", X1[:], d_gr[2 * d_ + 1], writes=[B1])
                act(X1[:], X1[:], AF.Exp, [B1], [B1], scale=-1.0)
                act(X1[:], X1[:], AF.Ln, [B1], [B1], bias=1.0)
                dve("tensor_scalar", [B1], [B1], out=X1[:], in0=X1[:], scalar1=-1.0, scalar2=None, op0=ALU.mult)
                dve("memset", [], [B3], ap=X3[:], constant=1.0)

                def sl(c):
                    if d_ == 0:
                        return slice(c * 128, (c + 1) * 128)
                    return slice((c + 1) * 128 - 1, (c * 128 - 1) if c > 0 else None, -1)

                def last(c):
                    t = (c + 1) * 128 - 1 if d_ == 0 else c * 128
                    return slice(t, t + 1)
                for c in range(NCH):
                    dve("tensor_tensor_scan", [B1, B3], [B2], out=X2[:, sl(c)], data0=X3[:, sl(c)], data1=X1[:, sl(c)],
                        initial=0.0, op0=ALU.mult, op1=ALU.add)
                dve("tensor_tensor", [Ba, B2], [Ba], out=Xa[:], in0=Xa[:], in1=X2[:], op=ALU.subtract)
                for c in range(NCH):
                    dve("tensor_tensor_scan", [Ba], [Bm], out=Xm[:, sl(c)], data0=Xa[:, sl(c)], data1=Xa[:, sl(c)],
                        initial=-1e30, op0=ALU.max, op1=ALU.max)
                for j, c in enumerate(orders[d_]):
                    cs = slice(c * 128, (c + 1) * 128)
                    mp = mrow[:, d_, j:j + 1]
                    dve("tensor_scalar", [Bm, Bn["mrow"]], [Bm], out=Xm[:, cs], in0=Xm[:, cs], scalar1=mp, scalar2=None, op0=ALU.max)
                    dve("tensor_tensor", [Bm, B2], [Bn["mrow"]], out=mrow[:, d_, j + 1:j + 2], in0=Xm[:, last(c)], in1=X2[:, last(c)], op=ALU.add)
                    dve("tensor_tensor", [Bm, Bn["mrow"]], [Bn["klog"]], out=klog[:, d_, c:c + 1], in0=mp, in1=Xm[:, last(c)], op=ALU.subtract)
                    dve("tensor_scalar", [Bm], [Bn["mlast"]], out=mlast[:, d_, c:c + 1], in0=Xm[:, last(c)], scalar1=-1.0, scalar2=None, op0=ALU.mult)
                    act(X4[:, cs], Xm[:, cs], AF.Exp, [Bm, Bn["mrow"]], [B4], bias=mp, scale=-1.0)
                    act(X3[:, cs], Xa[:, cs], AF.Exp, [Ba, Bn["mlast"]], [B3], bias=mlast[:, d_, c:c + 1])
                dve("tensor_tensor", [B2, Bm], [B2], out=X2[:], in0=X2[:], in1=Xm[:], op=ALU.add)
                act(X2[:], X2[:], AF.Exp, [B2], [B2], scale=-1.0)
                act(klog[:, d_, :], klog[:, d_, :], AF.Exp, [Bn["klog"]], [Bn["klog"]])
                dve("tensor_scalar", [Bm], [Bm], out=Xm[:], in0=Xm[:], scalar1=-1.0, scalar2=None, op0=ALU.mult)
                if debug:
                    for qi, (xt, xb) in enumerate(((Xa, Ba), (Xm, Bm), (X4, B4), (X2, B2), (X3, B3))):
                        dma("sp", d_rows[d_ * 5 + qi], xt[:], reads=[xb])
                for qi, (xt, xb) in enumerate(((X4, B4), (X2, B2), (X3, B3))):
                    for c in range(NCH):
                        o = ((qi * 2 + d_) * NCH + c) * 4
                        p.op("pe", lambda e, xt=xt, c=c, o=o: e.matmul(pcol[:, o:o + 4], lhsT=xt[:, c * 128:(c + 1) * 128], rhs=id4,
                                                                       start=True, stop=True), reads=[xb, B_cst], writes=[pcolb])
                for h in range(4):
                    o = (d_ * 4 + h) * NCH
                    p.op("pe", lambda e, h=h, o=o, d_=d_: e.matmul(pkeep[:, o:o + NCH], lhsT=sel4[:, h, :], rhs=klog[:, d_, :], start=True, stop=True),
                         reads=[Bn["klog"], B_cst], writes=[pkeepb])
            act(colsb[:], pcol[:, 0:432], AF.Copy, [pcolb], [Bn["colsb"]])
            act(keepb[:], pkeep[:, 0:144], AF.Copy, [pkeepb], [Bn["keepb"]])

            def col(qi, d_, c, h):
                o = ((qi * 2 + d_) * NCH + c) * 4 + h
                return colsb[:, o:o + 1]

            p.barrier()
            es_pre.close()
            per = [("qTc", [128, 8, 128], BF16, 2), ("kTc", [128, 8, 128], BF16, 2), ("ktk", [128, 1024], BF16, 2), ("vc", [128, 4, 257], BF16, 2),
                   ("ogc", [128, 8, 128], BF16, 2), ("hsf", [128, 1024], F32, 3), ("hsb", [128, 1024], F32, 2), ("Dt", [128, 512], F32, 2),
                   ("SdT", [128, 512], BF16, 2), ("t1", [128, 257], F32, 8), ("nd", [128, 257], F32, 8), ("kw", [128, 256], BF16, 8),
                   ("hn", [128, 1024], BF16, 2), ("hmc", [128, 8, 128], BF16, 2), ("sm", [128, 8], F32, 4), ("am4", [4, 4, 128], F32, 2)]
            rg = {}
            for nm, shp, dt, dep in per:
                tl = []
                for i in range(dep):
                    alloc(es, "%s%d" % (nm, i), shp, dt)
                    tl.append(T["%s%d" % (nm, i)])
                rg[nm] = Ring(tl)
            alloc(es, "Cst", [128, 4, 2, 257], F32)
            alloc(es, "Cbf", [128, 4, 2, 257], BF16)
            Cst, Cbf = T["Cst"], T["Cbf"]
            BCs = [[Buf(), Buf()] for _ in range(4)]
            BCb = [[Buf(), Buf()] for _ in range(4)]
            B_dhs = [Buf() for _ in range(NCH)]
            PSl = PS
            for d_ in range(2):
                Xa, Xm = T["Xa%d" % d_], T["Xm%d" % d_]
                Ba, Bm = Bn["Xa%d" % d_], Bn["Xm%d" % d_]
                mask = cst[:, (C_MF if d_ == 0 else C_MB):(C_MF if d_ == 0 else C_MB) + 128]
                dve("memset", [], [x for y in BCs for x in y], ap=Cst[:], constant=0.0)
                dve("memset", [], [x for y in BCb for x in y], ap=Cbf[:], constant=0.0)
                def issue_loads(c_):
                    cs_ = slice(c_ * 128, (c_ + 1) * 128)
                    L = {}
                    L["q"] = rg["qTc"].get()
                    L["k"] = rg["kTc"].get()
                    L["kt"] = rg["ktk"].get()
                    L["v"] = rg["vc"].get()
                    dma("sp", L["q"][0][:], d_qT[:, :, cs_].rearrange("f p t -> p f t"), writes=[L["q"][1]])
                    dma("sp", L["k"][0][:], d_kT[:, :, cs_].rearrange("f p t -> p f t"), writes=[L["k"][1]])
                    dma("sp", L["kt"][0][:], d_ktok[c_], writes=[L["kt"][1]])
                    dma("sp", L["v"][0][:], d_v[c_].rearrange("p (h d) -> p h d", d=257), writes=[L["v"][1]])
                    if d_ == 1:
                        L["hsf"] = rg["hsf"].get()
                        dma("sp", L["hsf"][0][:], d_hs[c_], reads=[B_dhs[c_]], writes=[L["hsf"][1]])
                        L["og"] = rg["ogc"].get()
                        dma("sp", L["og"][0][:], d_og[:, :, cs_].rearrange("f p t -> p f t"), writes=[L["og"][1]])
                    return L
                def front(j):
                    c = orders[d_][j]
                    cs = slice(c * 128, (c + 1) * 128)
                    L = issue_loads(c)
                    qTc, Bq = L["q"]
                    kTc, Bk = L["k"]
                    ktk, Bkt = L["kt"]
                    am4, Bam = rg["am4"].get()
                    dve("tensor_tensor", [Ba, B_cst], [Bam], out=am4[:], in0=Xa[:, cs].unsqueeze(1).to_broadcast([4, 4, 128]), in1=oh4, op=ALU.mult)
                    pSa, pSb_ = PSl.get()
                    for h in range(4):
                        mm(pSa[:, 128 * h:128 * h + 128], [(kTc[:, 2 * h + kc, :], qTc[:, 2 * h + kc, :]) for kc in range(2)], [Bk, Bq], pSb_)
                    pEa, pEb_ = PSl.get()
                    for h in range(4):
                        mm(pEa[:, 128 * h:128 * h + 128], [(am4[:, h, :], oh_ones), (sel4[:, h, :], Xm[:, cs]), (id_bf[:, :], mask_bf)],
                           [Bam, Bm, B_cst, B_idbf, B_maskbf], pEb_)
                    Dta, BDa = rg["Dt"].get()
                    act(Dta[:], pEa[:, :], AF.Exp, [pEb_], [BDa])
                    SdTa, BSa = rg["SdT"].get()
                    dve("tensor_tensor", [pSb_, BDa], [BSa], out=SdTa[:], in0=pSa[:, :], in1=Dta[:], op=ALU.mult)
                    kw = {}
                    for h in range(4):
                        kw[h] = rg["kw"].get()
                        dve("tensor_scalar", [Bkt, Bn["colsb"]], [kw[h][1]], out=kw[h][0][:], in0=ktk[:, 256 * h:256 * h + 256],
                            scalar1=col(2, d_, c, h), scalar2=None, op0=ALU.mult)
                    L["SdT"] = (SdTa, BSa)
                    L["kw"] = kw
                    return L

                def midback(j, L):
                    c = orders[d_][j]
                    cs = slice(c * 128, (c + 1) * 128)
                    qTc, Bq = L["q"]
                    vc, Bv = L["v"]
                    SdTa, BSa = L["SdT"]
                    kw = L["kw"]
                    H = range(4)
                    pN, pI, t1, nd = {}, {}, {}, {}
                    for h in H:
                        pN[h] = PSl.get()
                        mm(pN[h][0][:, 0:257], [(SdTa[:, 128 * h:128 * h + 128], vc[:, h, :])], [BSa, Bv], pN[h][1])
                        pI[h] = PSl.get()
                        mm(pI[h][0][:, 0:257], [(qTc[:, 2 * h + kc, :], Cbf[:, h, kc, :]) for kc in range(2)], [Bq, BCb[h][0], BCb[h][1]], pI[h][1])
                        t1[h] = rg["t1"].get()
                        act(t1[h][0][:], pI[h][0][:, 0:257], AF.Identity, [pI[h][1], Bn["colsb"]], [t1[h][1]], scale=col(0, d_, c, h))
                        nd[h] = rg["nd"].get()
                        dve("tensor_tensor", [pN[h][1], t1[h][1]], [nd[h][1]], out=nd[h][0][:], in0=pN[h][0][:, 0:257], in1=t1[h][0][:], op=ALU.add)
                    for h in H:
                        for kc in range(2):
                            pC, pCb = PSl.get()
                            mm(pC[:, 0:257], [(kw[h][0][:, 128 * kc:128 * kc + 128], vc[:, h, :])], [kw[h][1], Bv], pCb)
                            ko = (d_ * 4 + h) * NCH + c
                            dve("scalar_tensor_tensor", [BCs[h][kc], Bn["keepb"], pCb], [BCs[h][kc]], out=Cst[:, h, kc, :], in0=Cst[:, h, kc, :],
                                scalar=keepb[:, ko:ko + 1], in1=pC[:, 0:257], op0=ALU.mult, op1=ALU.add)
                            act(Cbf[:, h, kc, :], Cst[:, h, kc, :], AF.Copy, [BCs[h][kc]], [BCb[h][kc]])
                    sm, Bsm = rg["sm"].get()
                    hs, Bhs = rg["hsf" if d_ == 0 else "hsb"].get()
                    for h in H:
                        act(sm[:, h:h + 1], nd[h][0][:, 256:257], AF.Abs, [nd[h][1]], [Bsm])
                    for h in H:
                        dve("tensor_scalar", [Bsm, Bn["colsb"]], [Bsm], out=sm[:, h:h + 1], in0=sm[:, h:h + 1],
                            scalar1=col(1, d_, c, h), scalar2=None, op0=ALU.max)
                    dve("reciprocal", [Bsm], [Bsm], out=sm[:, 0:4], in_=sm[:, 0:4])
                    if d_ == 0:
                        for h in H:
                            dve("tensor_scalar", [nd[h][1], Bsm], [Bhs], out=hs[:, 256 * h:256 * h + 256], in0=nd[h][0][:, 0:256],
                                scalar1=sm[:, h:h + 1], scalar2=None, op0=ALU.mult)
                        dma("sp", d_hs[c], hs[:], reads=[Bhs], writes=[B_dhs[c]])
                    else:
                        hsf, Bhsf = L["hsf"]
                        ogc, Bog = L["og"]
                        for h in H:
                            dve("scalar_tensor_tensor", [nd[h][1], Bsm, Bhsf], [Bhs], out=hs[:, 256 * h:256 * h + 256], in0=nd[h][0][:, 0:256],
                                scalar=sm[:, h:h + 1], in1=hsf[:, 256 * h:256 * h + 256], op0=ALU.mult, op1=ALU.add)
                        sm2, Bsm2 = rg["sm"].get()
                        hn, Bhn = rg["hn"].get()
                        for h in range(4):
                            act(hn[:, 256 * h:256 * h + 256], hs[:, 256 * h:256 * h + 256], AF.Square, [Bhs], [Bhn])
                        dve("tensor_reduce", [Bhn], [Bsm2], out=sm2[:, 4:8], in_=hn[:, :].rearrange("p (h d) -> p h d", d=256),
                            axis=mybir.AxisListType.X, op=ALU.add)
                        act(sm2[:, 4:8], sm2[:, 4:8], AF.Sqrt, [Bsm2], [Bsm2], bias=EPS, scale=1.0 / 256)
                        dve("reciprocal", [Bsm2], [Bsm2], out=sm2[:, 4:8], in_=sm2[:, 4:8])
                        for h in range(4):
                            dve("tensor_scalar", [Bhs, Bsm2], [Bhn], out=hn[:, 256 * h:256 * h + 256], in0=hs[:, 256 * h:256 * h + 256],
                                scalar1=sm2[:, 4 + h:5 + h], scalar2=None, op0=ALU.mult)
                        hmc, Bhm = rg["hmc"].get()
                        for f in range(8):
                            p.op("pe", lambda e, f=f, hn=hn: e.transpose(out=psT[:, f, :], in_=hn[:, 128 * f:128 * f + 128], identity=id_bf[:, :]),
                                 reads=[Bhn, B_idbf], writes=[B_psT])
                        for f in range(8):
                            dve("scalar_tensor_tensor", [B_psT, B_vec, Bog], [Bhm], out=hmc[:, f, :], in0=psT[:, f, :],
                                scalar=vec[:, l, V_MN + f:V_MN + f + 1], in1=ogc[:, f, :], op0=ALU.mult, op1=ALU.mult)
                        dma("sp", d_hm[:, :, cs].rearrange("f p t -> p f t"), hmc[:], reads=[Bhm])

                mask_bf = maskbf[:, 128 * d_:128 * d_ + 128]
                cur = front(0)
                for j in range(NCH):
                    nxtL = front(j + 1) if j + 1 < NCH else None
                    midback(j, cur)
                    cur = nxtL
            p.barrier()
            es.close()

    oh_ones = cst[0:4, C_SEL + 0:C_SEL + 128]

    def phase_fourier(l):
        xf = nc.sbuf_tensor("xfa%d" % l, [128, 4, NT], BF16)
        pq = nc.sbuf_tensor("pq%d" % l, [128, NCH, 4, 256], BF16)
        yr = [nc.sbuf_tensor("yr%d_%d" % (l, i), [128, 512], BF16) for i in range(2)]
        with xf as xf_, pq as pq_, yr[0] as y0, yr[1] as y1:
            Yr = Ring([y0, y1])
            B_xf, B_pq = Buf(), [Buf() for _ in range(NCH)]
            dma("sp", xf_[:], d_xf.rearrange("g p t -> p g t"), writes=[B_xf])
            for c in range(NCH):
                for g2 in range(2):
                    ps, psb = PS.get()
                    for gg in range(2):
                        g = 2 * g2 + gg
                        mm(ps[:, 256 * gg:256 * gg + 256], [(xf_[:, g, c * 128:(c + 1) * 128], dft_cs[:, :])], [B_xf, B_dftcs], psb)
                    act(pq_[:, c, 2 * g2:2 * g2 + 2, :], ps[:, :].rearrange("p (g w) -> p g w", w=256), AF.Copy, [psb], [B_pq[c]])
            for g in range(4):
                ps, psb = PS.get()
                pairs = []
                for tc in range(2):
                    pairs.append((pq_[:, tc, g, 0:128], dftc_sb[:, 0, tc, :]))
                    pairs.append((pq_[:, tc, g, 128:256], dftc_sb[:, 1, tc, :]))
                mm(ps[:, 0:256], pairs, [B_pq[0], B_pq[1], B_dftc], psb)
                yt, yb = Yr.get()
                act(yt[:, 0:256], ps[:, 0:256], AF.Copy, [psb], [yb])
                dma("sp", d_yf[g, :, 0:256], yt[:, 0:256], reads=[yb])
            for j in range(4):
                wc, wcb = load_w([(dftl_d[0, :, 512 * j:512 * j + 512], 0, 0)])
                ws, wsb = load_w([(dftl_d[1, :, 512 * j:512 * j + 512], 0, 0)])
                for g in range(4):
                    ps, psb = PS.get()
                    pairs = []
                    for tc in range(16):
                        pairs.append((pq_[:, 2 + tc, g, 0:128], wc[:, tc, :]))
                        pairs.append((pq_[:, 2 + tc, g, 128:256], ws[:, tc, :]))
                    mm(ps[:, :], pairs, B_pq[2:] + [wcb, wsb], psb)
                    yt, yb = Yr.get()
                    act(yt[:, :], ps[:, :], AF.Copy, [psb], [yb])
                    dma("sp", d_yf[g, :, L_CTX + 512 * j:L_CTX + 512 * j + 512], yt[:, :], reads=[yb])
            p.barrier()

    def phase_merge(l):
        br = nc.sbuf_tensor("br%d" % l, [128, 16, NT], BF16)
        gr = [nc.sbuf_tensor("gr%d_%d" % (l, i), [128, 3, NT], BF16) for i in range(2)]
        mr = [nc.sbuf_tensor("mr%d_%d" % (l, i), [128, NT], BF16) for i in range(2)]
        tm = [nc.sbuf_tensor("tm%d_%d" % (l, i), [128, 512], F32) for i in range(4)]
        with br as br_, gr[0] as g0, gr[1] as g1, mr[0] as m0, mr[1] as m1, tm[0] as a0, tm[1] as a1, tm[2] as a2, tm[3] as a3:
            Gr, Mr, Tm = Ring([g0, g1]), Ring([m0, m1]), Ring([a0, a1, a2, a3])
            B_br = Buf()
            dma("sp", br_[:, 0:8, :], d_hm.rearrange("f p t -> p f t"), writes=[B_br])
            dma("sp", br_[:, 8:12, :], d_yf.rearrange("f p t -> p f t"), writes=[B_br])
            dma("sp", br_[:, 12:16, :], d_yc.rearrange("f p t -> p f t"), writes=[B_br])
            for dg in range(4):
                wt, wb = load_w([(w_pm[l, :, 512 * dg:512 * dg + 512], 0, 0), (w_pf[l, :, 512 * dg:512 * dg + 512], 8, 0),
                                 (w_pc[l, :, 512 * dg:512 * dg + 512], 12, 0)])
                for d4 in range(4):
                    dch = dg * 4 + d4
                    gt, gb = Gr.get()
                    for bi in range(3):
                        dma("sp", gt[:, bi, :], d_g[16 * bi + dch], writes=[gb])
                    mt, mb = Mr.get()
                    for ti, (t0, t1) in enumerate(TT):
                        n = t1 - t0
                        acc = None
                        for bi, (k0, k1) in enumerate(((0, 8), (8, 12), (12, 16))):
                            ps, psb = PS.get()
                            mm(ps[:, :n], [(wt[:, k, 128 * d4:128 * d4 + 128], br_[:, k, t0:t1]) for k in range(k0, k1)], [wb, B_br], psb)
                            tt, tb = Tm.get()
                            dve("tensor_tensor", [psb, gb], [tb], out=tt[:, :n], in0=ps[:, :n], in1=gt[:, bi, t0:t1], op=ALU.mult)
                            if acc is not None:
                                at, ab = acc
                                if bi == 2:
                                    dve("tensor_tensor", [tb, ab], [mb], out=mt[:, t0:t1], in0=tt[:, :n], in1=at[:, :n], op=ALU.add)
                                else:
                                    dve("tensor_tensor", [tb, ab], [tb], out=tt[:, :n], in0=tt[:, :n], in1=at[:, :n], op=ALU.add)
                            acc = (tt, tb)
                    dma("sp", d_mg[dch], mt[:, :], reads=[mb])
            p.barrier()

    def phase_resid_proj(l, src_d, wmat, gate_off, nm):
        xa = nc.sbuf_tensor("xa%s%d" % (nm, l), [128, 16, NT], BF16)
        hr = [nc.sbuf_tensor("hr%s%d_%d" % (nm, l, i), [128, NT], F32) for i in range(2)]
        with xa as xa_, hr[0] as h0, hr[1] as h1:
            Hr = Ring([h0, h1])
            B_xa = [Buf() for _ in TT]
            for ti, (t0, t1) in enumerate(TT):
                dma("sp", xa_[:, :, t0:t1], src_d[:, :, t0:t1].rearrange("f p t -> p f t"), writes=[B_xa[ti]])
            for dg in range(4):
                wt, wb = load_w([(wmat[l, :, 512 * dg:512 * dg + 512], 0, 0)])
                for d4 in range(4):
                    dch = dg * 4 + d4
                    ht, hb = Hr.get()
                    dma("sp", ht[:, :], d_h[dch], reads=B_dh[dch], writes=[hb])
                    for ti, (t0, t1) in enumerate(TT):
                        n = t1 - t0
                        seg = 1 if ti == 0 else 0
                        ps, psb = PS.get()
                        mm(ps[:, :n], [(wt[:, k, 128 * d4:128 * d4 + 128], xa_[:, k, t0:t1]) for k in range(16)], [wb, B_xa[ti]], psb)
                        dve("scalar_tensor_tensor", [psb, B_modc, hb], [hb], out=ht[:, t0:t1], in0=ps[:, :n],
                            scalar=modc[:, gate_off + dch, seg:seg + 1], in1=ht[:, t0:t1], op0=ALU.mult, op1=ALU.add)
                    dma("sp", d_h[dch], ht[:, :], reads=[hb], writes=B_dh[dch])
            p.barrier()

    def phase_ffn_up(l, xn, B_xn):
        U = [nc.sbuf_tensor("Uf%d_%d" % (l, i), [128, NT], F32) for i in range(4)]
        G = [nc.sbuf_tensor("Gf%d_%d" % (l, i), [128, NT], BF16) for i in range(2)]
        R = [nc.sbuf_tensor("Rf%d_%d" % (l, i), [128, NT], BF16) for i in range(2)]
        with U[0] as U0, U[1] as U1, U[2] as U2, U[3] as U3, G[0] as G0, G[1] as G1, R[0] as R0, R[1] as R1:
            Ur, Gr, Rr = Ring([U0, U1, U2, U3]), Ring([G0, G1]), Ring([R0, R1])
            for j4 in range(NFF // 4):
                wt, wb = load_w([(w_up[l, :, 512 * j4:512 * j4 + 512], 0, 0)])
                wt2, wb2 = load_w([(w_up[l, :, D_FF + 512 * j4:D_FF + 512 * j4 + 512], 0, 0)])
                for jj in range(4):
                    j = 4 * j4 + jj
                    Ut, Ub = Ur.get()
                    Gt, Gb = Gr.get()
                    for ti, (t0, t1) in enumerate(TT):
                        n = t1 - t0
                        ps, psb = PS.get()
                        mm(ps[:, :n], [(wt[:, k, 128 * jj:128 * jj + 128], xn[:, k, t0:t1]) for k in range(16)], [wb, B_xn[ti]], psb)
                        act(Ut[:, t0:t1], ps[:, :n], AF.Copy, [psb], [Ub])
                        ps2, psb2 = PS.get()
                        mm(ps2[:, :n], [(wt2[:, k, 128 * jj:128 * jj + 128], xn[:, k, t0:t1]) for k in range(16)], [wb2, B_xn[ti]], psb2)
                        act(Gt[:, t0:t1], ps2[:, :n], AF.Copy, [psb2], [Gb])
                    Ot, Ob = Ur.get()
                    wc = [vec[:, l, V_CFF + NFF * q + j:V_CFF + NFF * q + j + 1] for q in range(3)]
                    dve("tensor_scalar", [Ub, B_vec], [Ob], out=Ot[:, :], in0=Ut[:, :], scalar1=wc[1], scalar2=None, op0=ALU.mult)
                    for (a, b_, sh) in ((0, L_CTX, 1), (L_CTX, NT, 64)):
                        dve("scalar_tensor_tensor", [Ub, B_vec, Ob], [Ob], out=Ot[:, a + sh:b_], in0=Ut[:, a:b_ - sh],
                            scalar=wc[0], in1=Ot[:, a + sh:b_], op0=ALU.mult, op1=ALU.add)
                        dve("scalar_tensor_tensor", [Ub, B_vec, Ob], [Ob], out=Ot[:, a:b_ - sh], in0=Ut[:, a + sh:b_],
                            scalar=wc[2], in1=Ot[:, a:b_ - sh], op0=ALU.mult, op1=ALU.add)
                    act(Ot[:, :], Ot[:, :], AF.Silu, [Ob], [Ob])
                    Rt, Rb = Rr.get()
                    dve("tensor_tensor", [Ob, Gb], [Rb], out=Rt[:, :], in0=Ot[:, :], in1=Gt[:, :], op=ALU.mult)
                    dma("sp", d_hid[j], Rt[:, :], reads=[Rb])
            p.barrier()

    def phase_ffn_down(l):
        hd = [nc.sbuf_tensor("hd%d_%d" % (l, i), [128, NFF, 512], BF16) for i in range(2)]
        hr = [nc.sbuf_tensor("hq%d_%d" % (l, i), [128, 512], F32) for i in range(4)]
        with hd[0] as hd0, hd[1] as hd1, hr[0] as q0, hr[1] as q1, hr[2] as q2, hr[3] as q3:
            Hq = Ring([q0, q1, q2, q3])
            hds = [hd0, hd1]
            B_hd = [Buf(), Buf()]
            for pr in ((0, 1), (2, 3), (4,)):
                for i, ti in enumerate(pr):
                    t0, t1 = TT[ti]
                    dma("sp", hds[i][:, :, :t1 - t0], d_hid[:, :, t0:t1].rearrange("f p t -> p f t"), writes=[B_hd[i]])
                for dch in range(16):
                    wt, wb = WR.get()
                    wv = wt[:].rearrange("p k c -> p (k c)")[:, 0:NFF * 128].rearrange("p (k c) -> p k c", c=128)
                    dma("pool", wv, w_down[l, :, 128 * dch:128 * dch + 128].rearrange("(k p) w -> p k w", p=128), writes=[wb])
                    for i, ti in enumerate(pr):
                        t0, t1 = TT[ti]
                        n = t1 - t0
                        seg = 1 if ti == 0 else 0
                        ht, hb = Hq.get()
                        dma("sp", ht[:, :n], d_h[dch, :, t0:t1], reads=[B_dh[dch][ti]], writes=[hb])
                        ps, psb = PS.get()
                        mm(ps[:, :n], [(wv[:, k, :], hds[i][:, k, :n]) for k in range(NFF)], [wb, B_hd[i]], psb)
                        dve("scalar_tensor_tensor", [psb, B_modc, hb], [hb], out=ht[:, :n], in0=ps[:, :n],
                            scalar=modc[:, 80 + dch, seg:seg + 1], in1=ht[:, :n], op0=ALU.mult, op1=ALU.add)
                        dma("sp", d_h[dch, :, t0:t1], ht[:, :n], reads=[hb], writes=[B_dh[dch][ti]])
            p.barrier()

    oh_ones = cst[0:4, NCST - 128:NCST]
    stages = []
    for l in range(n_layers):
        phase_mod(l)
        xn_c = nc.sbuf_tensor("xn%d" % l, [128, 16, NT], BF16)
        with xn_c as xn:
            B_xn = [Buf() for _ in TT]
            phase_norm(l, 0, xn, B_xn)
            if stop_after == "norm":
                break
            phase_inproj(l, xn, B_xn)
        if stop_after == "inproj":
            break
        phase_mlstm(l)
        if stop_after == "mlstm":
            break
        phase_fourier(l)
        phase_merge(l)
        phase_resid_proj(l, d_mg, w_o, 32, "o")
        if stop_after == "mixer":
            break
        xn_c = nc.sbuf_tensor("xm%d" % l, [128, 16, NT], BF16)
        with xn_c as xn:
            B_xn = [Buf() for _ in TT]
            phase_norm(l, 1, xn, B_xn)
            phase_ffn_up(l, xn, B_xn)
        phase_ffn_down(l)
    if stop_after is None:
        phase_norm(0, 0, None, None, final=True)
    p.barrier()
    return p.emit()


def _host_consts():
    cst = np.zeros((128, NCST), np.float32)
    cst[:, C_ID:C_ID + 128] = np.eye(128, dtype=np.float32)
    s = np.arange(128)[:, None]
    t = np.arange(128)[None, :]
    cst[:, C_MF:C_MF + 128] = np.where(s <= t, 0.0, NEG)
    cst[:, C_MB:C_MB + 128] = np.where(s >= t, 0.0, NEG)
    for h in range(4):
        cst[h, C_SEL + 128 * h:C_SEL + 128 * h + 128] = 1.0
        cst[h, C_OH + 128 * h:C_OH + 128 * h + 128] = 1.0
    cst[0:4, C_I4:C_I4 + 4] = np.eye(4, dtype=np.float32)
    ang = 2 * np.pi * (np.arange(128)[:, None] * np.arange(128)[None, :] % 128) / 128.0
    cst[:, C_DFT:C_DFT + 128] = np.cos(ang)
    cst[:, C_DFT + 128:C_DFT + 256] = np.sin(ang)
    cst[0:4, NCST - 128:NCST] = 1.0

    def seq_dft(T):
        idx = (np.arange(T, dtype=np.int64)[:, None] * np.arange(T, dtype=np.int64)[None, :]) % T
        a = 2 * np.pi * idx / T
        sc = 1.0 / np.sqrt(T * 128.0)
        return np.stack([np.cos(a) * sc, -np.sin(a) * sc]).astype(np.float32)
    return cst, seq_dft(SEQ), seq_dft(L_CTX)


def _pcols(v):
    return np.ascontiguousarray(v.reshape(-1, 128).T)


def _host_vec(inp):
    vec = np.zeros((DEPTH, 128, NV), np.float32)
    for l in range(DEPTH):
        vec[l, :, V_N1:V_N1 + 16] = _pcols(inp["norm1_w"][l])
        vec[l, :, V_N2:V_N2 + 16] = _pcols(inp["norm2_w"][l])
        vec[l, :, V_BMOD:V_BMOD + 96] = _pcols(inp["b_mod"][l])
        for n, (kind, i, off) in enumerate(FM):
            vec[l, :, V_BIN + n] = inp["b_in"][l, off:off + 128]
        for j in range(3):
            vec[l, :, V_CQ + 8 * j:V_CQ + 8 * j + 8] = _pcols(inp["conv_q_w"][l, j])
            vec[l, :, V_CK + 8 * j:V_CK + 8 * j + 8] = _pcols(inp["conv_k_w"][l, j])
            vec[l, :, V_CC + 4 * j:V_CC + 4 * j + 4] = _pcols(inp["conv_c_w"][l, j])
            vec[l, :, V_CFF + NFF * j:V_CFF + NFF * j + NFF] = _pcols(inp["conv_ff_w"][l, j])
        vec[l, :, V_MN:V_MN + 8] = _pcols(inp["mlstm_norm_w"][l])
        vec[l, :, V_FN:V_FN + 16] = _pcols(inp["final_norm_w"])
    bvrep = np.ascontiguousarray(np.broadcast_to(inp["b_in"][:, None, 1024:2048], (DEPTH, 128, 1024))).astype(np.float32)
    bgt = np.ascontiguousarray(inp["b_in"][:, OFF_GATES:OFF_GATES + 16].reshape(DEPTH, 4, 4).transpose(0, 2, 1)).astype(np.float32)
    return vec, bvrep, bgt


def make_in_maps(inp, cores):
    cst, dftl, dftc = _host_consts()
    vec, bvrep, bgt = _host_vec(inp)
    shared = {"vec": vec, "bvrep": bvrep, "bgt": bgt, "cst": cst, "dft_lat": dftl, "dft_ctx": dftc}
    for k in ("w_mod", "w_in", "w_pm", "w_pf", "w_pc", "w_o", "w_up", "w_down"):
        shared[k] = np.ascontiguousarray(inp[k], dtype=np.float32)
    maps = []
    for b in cores:
        hT0 = np.ascontiguousarray(np.concatenate([inp["ctx"][b], inp["x"][b]], axis=0).T)
        cc2 = np.stack([inp["c"][b], inp["c_ctx"]], axis=1)
        ccp = np.ascontiguousarray(cc2.reshape(16, 128, 2).transpose(1, 0, 2).reshape(128, 32))
        m = dict(shared)
        m["hT0"] = hT0.astype(np.float32)
        m["ccp"] = ccp.astype(np.float32)
        maps.append(m)
    return maps


def kernel(**inputs):
    inp = {k: np.asarray(v) for k, v in inputs.items()}
    nc = build()
    cores = [0, 1, 2, 3, 0, 1, 2, 3]
    maps = make_in_maps(inp, cores)
    res = run_bass_kernel_spmd(nc, maps, core_ids=list(range(8)))
    out = np.stack([res.results[b]["outT"].T for b in range(4)], axis=0)
    return np.ascontiguousarray(out.astype(np.float32))
```

```python
import contextlib
import numpy as np
import concourse.bass as bass
import concourse.mybir as mybir
from concourse.bass_utils import run_bass_kernel_spmd

F32 = mybir.dt.float32
BF16 = mybir.dt.bfloat16
AF = mybir.ActivationFunctionType
ALU = mybir.AluOpType

ENGS = ["pe", "act", "dve", "pool", "sp"]

D = 2048
L_CTX = 256
SEQ = 2048
NT = L_CTX + SEQ
NCH = NT // 128
DEPTH = 4
D_M = 1024
D_F = 512
D_C = 512
D_FF = 5632
NFF = D_FF // 128
OFF_GATES = 2048
OFF_Q = 2064
OFF_O = OFF_Q + 1024
OFF_F = OFF_O + 1024
OFF_C = OFF_F + 512
OFF_G = OFF_C + 3 * 512
N_IN = OFF_G + 3 * D
EPS = 1e-6
TT = [(0, 256), (256, 768), (768, 1280), (1280, 1792), (1792, 2304)]
NEG = -30000.0

V_N1, V_N2, V_BMOD, V_BIN, V_CQ, V_CK, V_MN, V_CC, V_CFF, V_FN = 0, 16, 32, 128, 216, 240, 264, 272, 284, 416
NV = 432
FM = []
for i in range(8):
    FM.append(("k", i, 0 + 128 * i))
for i in range(8):
    FM.append(("q", i, OFF_Q + 128 * i))
for i in range(8):
    FM.append(("o", i, OFF_O + 128 * i))
for i in range(4):
    FM.append(("xf", i, OFF_F + 128 * i))
for i in range(4):
    FM.append(("cb", i, OFF_C + 128 * i))
for i in range(4):
    FM.append(("cc", i, OFF_C + 512 + 128 * i))
for i in range(4):
    FM.append(("cx", i, OFF_C + 1024 + 128 * i))
for i in range(48):
    FM.append(("gm", i, OFF_G + 128 * i))
FMIDX = {(k, i): n for n, (k, i, _) in enumerate(FM)}
C_ID, C_MF, C_MB, C_SEL, C_I4, C_DFT, C_OH = 0, 128, 256, 384, 896, 900, 1156
NCST = 1156 + 512 + 128


class Buf:
    __slots__ = ("name", "w", "r")

    def __init__(self, name=""):
        self.name = name
        self.w = None
        self.r = {}


class Prog:
    def __init__(self, ndma=28):
        self.nc = bass.Bass("TRN2", target_bir_lowering=False)
        self.ndma = ndma
        self.streams = {e: [] for e in ENGS}
        self.cnt = {e: 0 for e in ENGS}
        self.seen = {e: {} for e in ENGS}
        self.snap = {}
        self.dma_cnt = [0] * ndma
        self.dma_rr = 0
        self.dma_rr2 = 0
        self.waited = {}
        self.es = contextlib.ExitStack()
        self.n_ops = 0

    def sb(self, name, shape, dt):
        return self.es.enter_context(self.nc.sbuf_tensor("sb_" + name, list(shape), dt))

    def ps(self, name, shape, dt=F32):
        return self.es.enter_context(self.nc.psum_tensor("pp_" + name, list(shape), dt))

    def dram(self, name, shape, dt, kind="Internal"):
        return self.nc.dram_tensor(name, list(shape), dt, kind=kind)

    def op(self, eng, fn, reads=(), writes=(), dma=False):
        deps = {}

        def add(d):
            if d is None:
                return
            tl, s = d
            if deps.get(tl, 0) < s:
                deps[tl] = s

        for b in reads:
            add(b.w)
        for b in writes:
            add(b.w)
            for tl, s in b.r.items():
                add((tl, s))
        k = None
        if dma:
            if eng == "pool":
                k = self.ndma - 8 + self.dma_rr2
                self.dma_rr2 = (self.dma_rr2 + 1) % 8
            else:
                k = self.dma_rr
                self.dma_rr = (k + 1) % (self.ndma - 8)
            if self.dma_cnt[k] > 0:
                add(("d%d" % k, self.dma_cnt[k]))
        seen = self.seen[eng]
        waits = []
        for tl, s in deps.items():
            if tl == eng and eng == "pe":
                continue
            if seen.get(tl, 0) >= s:
                continue
            waits.append((tl, s))
        for tl, s in waits:
            sn = self.snap.get((tl, s))
            if sn:
                for t2, s2 in sn.items():
                    if seen.get(t2, 0) < s2:
                        seen[t2] = s2
            if seen.get(tl, 0) < s:
                seen[tl] = s
            self.waited.setdefault(tl, set()).add(s)
        if dma:
            self.dma_cnt[k] += 1
            done = ("d%d" % k, self.dma_cnt[k])
        else:
            self.cnt[eng] += 1
            done = (eng, self.cnt[eng])
        self.snap[done] = dict(seen)
        self.streams[eng].append((waits, fn, done))
        for b in reads:
            if b.r.get(done[0], 0) < done[1]:
                b.r[done[0]] = done[1]
        for b in writes:
            b.w = done
            b.r = {}
        self.n_ops += 1
        return done

    def barrier(self, engs=("pe", "act", "dve", "sp", "pool")):
        targets = []
        for k in range(self.ndma):
            if self.dma_cnt[k] > 0:
                targets.append(("d%d" % k, self.dma_cnt[k]))
        for e in ENGS:
            if self.cnt[e] > 0:
                targets.append((e, self.cnt[e]))
        for eng in engs:
            seen = self.seen[eng]
            waits = []
            for tl, s in targets:
                if tl == eng or seen.get(tl, 0) >= s:
                    continue
                waits.append((tl, s))
                seen[tl] = s
                if not tl.startswith("d"):
                    self.waited.setdefault(tl, set()).add(s)
            for tl, s in waits:
                sn = self.snap.get((tl, s))
                if sn:
                    for t2, s2 in sn.items():
                        if seen.get(t2, 0) < s2:
                            seen[t2] = s2
            if eng != "pe":
                seen[eng] = self.cnt[eng]
            self.streams[eng].append((waits, None, None))

    def emit(self):
        nc = self.nc
        self.snap = None
        sems = {}
        for e in ENGS:
            sems[e] = self.es.enter_context(nc.semaphore("s_" + e))
        for k in range(self.ndma):
            sems["d%d" % k] = self.es.enter_context(nc.semaphore("s_d%d" % k))
        rank = {}
        for tl, ss in self.waited.items():
            if tl.startswith("d"):
                continue
            rank[tl] = {s: i + 1 for i, s in enumerate(sorted(ss))}

        def val(tl, s):
            if tl.startswith("d"):
                return 16 * s
            return rank[tl][s]

        def replay(ename, eng):
            for waits, fn, done in self.streams[ename]:
                for tl, s in waits:
                    eng.wait_ge(sems[tl], val(tl, s))
                if fn is None:
                    continue
                inst = fn(eng)
                tl, s = done
                if tl.startswith("d"):
                    inst.then_inc(sems[tl], 16)
                elif tl in rank and s in rank[tl]:
                    inst.then_inc(sems[tl], 1)

        block = self.es.enter_context(nc.Block())

        @block.tensor
        def _(eng):
            replay("pe", eng)

        @block.scalar
        def _(eng):
            replay("act", eng)

        @block.vector
        def _(eng):
            replay("dve", eng)

        @block.gpsimd
        def _(eng):
            replay("pool", eng)

        @block.sync
        def _(eng):
            replay("sp", eng)

        self.es.close()
        return nc


class Ring:
    def __init__(self, tiles):
        self.tiles = tiles
        self.bufs = [Buf() for _ in tiles]
        self.i = 0

    def get(self):
        i = self.i
        self.i = (i + 1) % len(self.tiles)
        return self.tiles[i], self.bufs[i]


def build(n_layers=DEPTH, debug=False, stop_after=None):
    p = Prog()
    nc = p.nc
    okind = "ExternalOutput" if debug else "Internal"

    def din(name, shape):
        return p.dram(name, shape, F32, kind="ExternalInput").ap()

    hT0 = din("hT0", [D, NT])
    ccp = din("ccp", [128, 32])
    vec_d = din("vec", [DEPTH, 128, NV])
    bvrep_d = din("bvrep", [DEPTH, 128, 1024])
    bgt_d = din("bgt", [DEPTH, 4, 4])
    cst_d = din("cst", [128, NCST])
    dftl_d = din("dft_lat", [2, SEQ, SEQ])
    dftc_d = din("dft_ctx", [2, L_CTX, L_CTX])
    w_mod = din("w_mod", [DEPTH, D, 6 * D])
    w_in = din("w_in", [DEPTH, D, N_IN])
    w_pm = din("w_pm", [DEPTH, D_M, D])
    w_pf = din("w_pf", [DEPTH, D_F, D])
    w_pc = din("w_pc", [DEPTH, D_C, D])
    w_o = din("w_o", [DEPTH, D, D])
    w_up = din("w_up", [DEPTH, D, 2 * D_FF])
    w_down = din("w_down", [DEPTH, D_FF, D])
    out_d = p.dram("outT", [D, SEQ], F32, kind="ExternalOutput").ap()

    def scratch(name, shape, dt):
        return p.dram(name, shape, dt, kind=okind).ap()

    d_h = scratch("d_h", [16, 128, NT], F32)
    d_kT = scratch("d_kT", [8, 128, NT], BF16)
    d_qT = scratch("d_qT", [8, 128, NT], BF16)
    d_og = scratch("d_og", [8, 128, NT], BF16)
    d_ktok = scratch("d_ktok", [NCH, 128, 1024], BF16)
    d_v = scratch("d_v", [NCH, 128, 4 * 257], BF16)
    d_xf = scratch("d_xf", [4, 128, NT], BF16)
    d_yc = scratch("d_yc", [4, 128, NT], BF16)
    d_yf = scratch("d_yf", [4, 128, NT], BF16)
    d_g = scratch("d_g", [48, 128, NT], BF16)
    d_hs = scratch("d_hs", [NCH, 128, 1024], F32)
    d_hm = scratch("d_hm", [8, 128, NT], BF16)
    d_mg = scratch("d_mg", [16, 128, NT], BF16)
    d_hid = scratch("d_hid", [NFF, 128, NT], BF16)
    if debug:
        d_xn = scratch("d_xn", [16, 128, NT], BF16)
        d_rows = scratch("d_rows", [16, 4, NT], F32)
        d_mod = scratch("d_mod", [128, 192], F32)
    B_dh = [[Buf() for _ in TT] for _ in range(16)]

    cst = p.sb("cst", [128, NCST], F32)
    vec = p.sb("vec", [128, DEPTH, NV], F32)
    ones_bf = p.sb("ones_bf", [128, 128], BF16)
    id_bf = p.sb("id_bf", [128, 128], BF16)
    dft_cs = p.sb("dft_cs", [128, 256], BF16)
    dftc_sb = p.sb("dftc_sb", [128, 2, 2, 256], BF16)
    scc = p.sb("scc", [128, 16, 2], F32)
    modc = p.sb("modc", [128, 96, 2], F32)
    amod = p.sb("amod", [128, 2, 16, 2], F32)
    B_cst, B_vec, B_ones, B_idbf, B_dftcs, B_dftc, B_scc, B_modc, B_amod = (Buf() for _ in range(9))
    WR = Ring([p.sb("wr%d" % i, [128, 16, 512], BF16) for i in range(3)])
    PS = Ring([p.ps("ps%d" % i, [128, 512]) for i in range(7)])
    psT = p.ps("psT", [128, 8, 128], BF16)
    B_psT = Buf()

    def dma(q, out, in_, reads=(), writes=(), **kw):
        p.op(q, lambda e: e.dma_start(out=out, in_=in_, **kw), reads, writes, dma=True)

    def load_w(pieces):
        t, b = WR.get()
        for src, k0, c0 in pieces:
            nk = src.shape[0] // 128
            w = src.shape[1]
            dma("pool", t[:, k0:k0 + nk, c0:c0 + w], src.rearrange("(k p) w -> p k w", p=128), writes=[b])
        return t, b

    def mm(ps_ap, pairs, reads, psb):
        n = len(pairs)
        for i, (l, r) in enumerate(pairs):
            p.op("pe", lambda e, l=l, r=r, i=i: e.matmul(ps_ap, lhsT=l, rhs=r, start=(i == 0), stop=(i == n - 1)),
                 reads=reads, writes=[psb])

    def act(out, in_, func, reads, writes, bias=None, scale=None):
        kw = {}
        if bias is not None:
            kw["bias"] = bias
        if scale is not None:
            kw["scale"] = scale
        p.op("act", lambda e: e.activation(out=out, in_=in_, func=func, **kw), reads, writes)

    def dve(name, reads, writes, **kw):
        p.op("dve", lambda e: getattr(e, name)(**kw), reads, writes)

    dma("sp", cst[:], cst_d, writes=[B_cst])
    dma("sp", vec[:], vec_d.rearrange("l p v -> p l v"), writes=[B_vec])
    dma("sp", scc[:], ccp.rearrange("p (k c) -> p k c", c=2), writes=[B_scc])
    act(scc[:], scc[:], AF.Silu, [B_scc], [B_scc])
    maskbf = p.sb("maskbf", [128, 256], BF16)
    B_maskbf = Buf()
    dve("tensor_copy", [B_cst], [B_maskbf], out=maskbf[:], in_=cst[:, C_MF:C_MF + 256])
    scc_bf = p.sb("scc_bf", [128, 16, 2], BF16)
    B_sccbf = Buf()
    dve("tensor_copy", [B_scc], [B_sccbf], out=scc_bf[:], in_=scc[:])
    dve("memset", [], [B_ones], ap=ones_bf[:], constant=1.0)
    dve("tensor_copy", [B_cst], [B_idbf], out=id_bf[:], in_=cst[:, C_ID:C_ID + 128])
    dve("tensor_copy", [B_cst], [B_dftcs], out=dft_cs[:], in_=cst[:, C_DFT:C_DFT + 256])
    for j in range(2):
        dma("pool", dftc_sb[:, j, :, :], dftc_d[j].rearrange("(k p) w -> p k w", p=128), writes=[B_dftc])
    for k in range(16):
        for ti, (t0, t1) in enumerate(TT):
            dma("sp", d_h[k, :, t0:t1], hT0[k * 128:(k + 1) * 128, t0:t1], writes=[B_dh[k][ti]])

    ident = cst[:, C_ID:C_ID + 128]

    def phase_mod(l):
        if True:
            pm, pmb = PS.get()
            for grp in range(24):
                t, b = load_w([(w_mod[l, :, grp * 512:(grp + 1) * 512], 0, 0)])
                for j4 in range(4):
                    j = grp * 4 + j4
                    mm(pm[:, 2 * j:2 * j + 2], [(t[:, k, j4 * 128:(j4 + 1) * 128], scc_bf[:, k, :]) for k in range(16)],
                       [b, B_sccbf], pmb)
            dve("tensor_tensor", [pmb, B_vec], [B_modc], out=modc[:],
                in0=pm[:, 0:192].rearrange("p (j c) -> p j c", c=2),
                in1=vec[:, l, V_BMOD:V_BMOD + 96].unsqueeze(2).to_broadcast([128, 96, 2]), op=ALU.add)
            for w in range(2):
                sc = modc[:, 16 + 48 * w:32 + 48 * w, :]
                nw = vec[:, l, (V_N1 if w == 0 else V_N2):(V_N1 if w == 0 else V_N2) + 16]
                dve("scalar_tensor_tensor", [B_modc, B_vec], [B_amod], out=amod[:, w, :, :], in0=sc, scalar=1.0,
                    in1=nw.unsqueeze(2).to_broadcast([128, 16, 2]), op0=ALU.add, op1=ALU.mult)
            if debug:
                dma("sp", d_mod, modc[:].rearrange("p j c -> p (j c)"), reads=[B_modc])
            p.barrier()

    def phase_norm(l, w, xn, B_xn, final=False):
        fx = "F" if final else ""
        hA = [nc.sbuf_tensor("hA%s%d_%d_%d" % (fx, l, w, i), [128, 16, 256], F32) for i in range(3)]
        sqt = [nc.sbuf_tensor("sq%s%d_%d_%d" % (fx, l, w, i), [128, 16, 256], BF16) for i in range(2)]
        rst = [nc.sbuf_tensor("rs%s%d_%d_%d" % (fx, l, w, i), [128, 256], F32) for i in range(2)]
        with hA[0] as h0, hA[1] as h1, hA[2] as h2, sqt[0] as sq0, sqt[1] as sq1, rst[0] as rs0, rst[1] as rs1:
            ring = Ring([h0, h1, h2])
            sqr = Ring([sq0, sq1])
            rsr = Ring([rs0, rs1])
            n = 256
            subs = [s9 for s9 in range(NT // 256) if not (final and s9 == 0)]

            def stage1(s9):
                t0, t1 = 256 * s9, 256 * s9 + 256
                ti = 0 if s9 == 0 else 1 + (s9 - 1) // 2
                t, b = ring.get()
                sq, B_sq = sqr.get()
                rs, B_rs = rsr.get()
                dma("sp", t[:, :, :n], d_h[:, :, t0:t1].rearrange("k p t -> p k t"),
                    reads=[B_dh[k][ti] for k in range(16)], writes=[b])
                act(sq[:, :, :n], t[:, :, :n], AF.Square, [b], [B_sq])
                ps, psb = PS.get()
                mm(ps[:, :n], [(ones_bf[:, :], sq[:, k, :n]) for k in range(16)], [B_ones, B_sq], psb)
                act(rs[:, :n], ps[:, :n], AF.Ln, [psb], [B_rs], bias=EPS, scale=1.0 / D)
                act(rs[:, :n], rs[:, :n], AF.Exp, [B_rs], [B_rs], scale=-0.5)
                return (s9, t, b, rs, B_rs)

            def stage2(st):
                s9, t, b, rs, B_rs = st
                t0, t1 = 256 * s9, 256 * s9 + 256
                ti = 0 if s9 == 0 else 1 + (s9 - 1) // 2
                seg = 1 if ti == 0 else 0
                for k in range(16):
                    if final:
                        sc_ap = vec[:, 0, V_FN + k:V_FN + k + 1]
                    else:
                        sc_ap = amod[:, w, k, seg:seg + 1]
                    dve("scalar_tensor_tensor", [b, B_rs, B_amod, B_vec], [b], out=t[:, k, :n], in0=t[:, k, :n],
                        scalar=sc_ap, in1=rs[:, :n], op0=ALU.mult, op1=ALU.mult)
                    if not final:
                        act(xn[:, k, t0:t1], t[:, k, :n], AF.Identity, [b, B_modc], [B_xn[ti]],
                            bias=modc[:, 48 * w + k, seg:seg + 1])
                if final:
                    dma("sp", out_d.rearrange("(k p) t -> p k t", p=128)[:, :, t0 - L_CTX:t1 - L_CTX], t[:, :, :n], reads=[b])

            prev = None
            for s9 in subs:
                cur = stage1(s9)
                if prev is not None:
                    stage2(prev)
                prev = cur
            stage2(prev)
            if debug and not final and w == 0:
                dma("sp", d_xn.rearrange("k p t -> p k t"), xn[:], reads=B_xn)
            p.barrier()

    def conv3(O, U, wcols, segs, shift, B_O, B_U):
        dve("tensor_scalar", [B_U, B_vec], [B_O], out=O[:, :], in0=U[:, :], scalar1=wcols[1], scalar2=None, op0=ALU.mult)
        for (a, b_, rows) in segs:
            if rows is None:
                dve("scalar_tensor_tensor", [B_U, B_vec, B_O], [B_O], out=O[:, a + shift:b_], in0=U[:, a:b_ - shift],
                    scalar=wcols[0], in1=O[:, a + shift:b_], op0=ALU.mult, op1=ALU.add)
                dve("scalar_tensor_tensor", [B_U, B_vec, B_O], [B_O], out=O[:, a:b_ - shift], in0=U[:, a + shift:b_],
                    scalar=wcols[2], in1=O[:, a:b_ - shift], op0=ALU.mult, op1=ALU.add)
            else:
                Ov = O[:, a:b_].rearrange("p (r c) -> p r c", c=rows)
                Uv = U[:, a:b_].rearrange("p (r c) -> p r c", c=rows)
                dve("scalar_tensor_tensor", [B_U, B_vec, B_O], [B_O], out=Ov[:, :, 1:rows], in0=Uv[:, :, 0:rows - 1],
                    scalar=wcols[0], in1=Ov[:, :, 1:rows], op0=ALU.mult, op1=ALU.add)
                dve("scalar_tensor_tensor", [B_U, B_vec, B_O], [B_O], out=Ov[:, :, 0:rows - 1], in0=Uv[:, :, 1:rows],
                    scalar=wcols[2], in1=Ov[:, :, 0:rows - 1], op0=ALU.mult, op1=ALU.add)

    SEG_SEQ = [(0, L_CTX, None), (L_CTX, NT, None)]

    def phase_inproj(l, xn, B_xn):
        U = [nc.sbuf_tensor("U%d_%d" % (l, i), [128, NT], F32) for i in range(3)]
        R = [nc.sbuf_tensor("R%d_%d" % (l, i), [128, NT], BF16) for i in range(2)]
        Tt = nc.sbuf_tensor("Tt%d" % l, [128, NCH, 128], BF16)
        Vs = [nc.sbuf_tensor("Vs%d_%d" % (l, i), [128, 2, 257], BF16) for i in range(3)]
        bv = nc.sbuf_tensor("bv%d" % l, [128, 1024], F32)
        wg = nc.sbuf_tensor("wg%d" % l, [128, 16, 16], BF16)
        bg = nc.sbuf_tensor("bg%d" % l, [4, 4], F32)
        grow = nc.sbuf_tensor("grow%d" % l, [4, NT], F32)
        with U[0] as U0, U[1] as U1, U[2] as U2, R[0] as R0, R[1] as R1, Tt as Tt_, Vs[0] as V0, Vs[1] as V1, Vs[2] as V2, \
                bv as bv_, wg as wg_, bg as bg_, grow as grow_:
            Ur = Ring([U0, U1, U2])
            Rr = Ring([R0, R1])
            Vr = Ring([V0, V1, V2])
            B_Tt, B_bv, B_wg, B_bg, B_grow = Buf(), Buf(), Buf(), Buf(), Buf()
            dma("sp", bv_[:], bvrep_d[l], writes=[B_bv])
            dma("sp", bg_[:], bgt_d[l], writes=[B_bg])
            dma("pool", wg_[:], w_in[l, :, OFF_GATES:OFF_GATES + 16].rearrange("(k p) g -> p k g", p=128), writes=[B_wg])
            for vt, vb in zip(Vr.tiles, Vr.bufs):
                dve("memset", [], [vb], ap=vt[:, :, 256:257], constant=1.0)

            def proj_rows(wt, wb, c0, bias_idx, dst, dstb, func=AF.Identity, scale=None):
                for ti, (t0, t1) in enumerate(TT):
                    n = t1 - t0
                    ps, psb = PS.get()
                    mm(ps[:, :n], [(wt[:, k, c0:c0 + 128], xn[:, k, t0:t1]) for k in range(16)], [wb, B_xn[ti]], psb)
                    act(dst[:, t0:t1], ps[:, :n], func, [psb, B_vec], [dstb],
                        bias=vec[:, l, V_BIN + bias_idx:V_BIN + bias_idx + 1], scale=scale)

            for g in range(4):
                for ti, (t0, t1) in enumerate(TT):
                    n = t1 - t0
                    ps, psb = PS.get()
                    mm(ps[0:4, :n], [(wg_[:, k, 4 * g:4 * g + 4], xn[:, k, t0:t1]) for k in range(16)], [B_wg, B_xn[ti]], psb)
                    act(grow_[:, t0:t1], ps[0:4, :n], AF.Identity, [psb, B_bg], [B_grow], bias=bg_[:, g:g + 1])
                dma("sp", d_gr[g], grow_[:], reads=[B_grow])

            for kind, dT, cw, qs in (("k", d_kT, V_CK, None), ("q", d_qT, V_CQ, 1.0 / 16.0)):
                for half in range(2):
                    off0 = (0 if kind == "k" else OFF_Q) + 512 * half
                    wt, wb = load_w([(w_in[l, :, off0:off0 + 512], 0, 0)])
                    for i4 in range(4):
                        i = half * 4 + i4
                        Ut, Ub = Ur.get()
                        proj_rows(wt, wb, 128 * i4, FMIDX[(kind, i)], Ut, Ub)
                        Ot, Ob = Ur.get()
                        wc = [vec[:, l, cw + 8 * j + i:cw + 8 * j + i + 1] for j in range(3)]
                        conv3(Ot, Ut, wc, SEG_SEQ, 1, Ob, Ub)
                        Rt, Rb = Rr.get()
                        if qs is None:
                            act(Rt[:, :], Ot[:, :], AF.Silu, [Ob], [Rb])
                        else:
                            act(Ot[:, :], Ot[:, :], AF.Silu, [Ob], [Ob])
                            act(Rt[:, :], Ot[:, :], AF.Copy, [Ob], [Rb], scale=qs)
                        dma("sp", dT[i], Rt[:, :], reads=[Rb])
                        if kind == "k":
                            for c8 in range(0, NCH, 8):
                                nn = min(8, NCH - c8)
                                for c in range(c8, c8 + nn):
                                    p.op("pe", lambda e, c=c, c8=c8, Rt=Rt: e.transpose(out=psT[:, c - c8, :], in_=Rt[:, c * 128:(c + 1) * 128],
                                                                                         identity=id_bf[:, :]),
                                         reads=[Rb, B_idbf], writes=[B_psT])
                                dve("tensor_copy", [B_psT], [B_Tt], out=Tt_[:, c8:c8 + nn, :], in_=psT[:, 0:nn, :])
                            dma("sp", d_ktok[:, :, 128 * i:128 * (i + 1)].rearrange("c p f -> p c f"), Tt_[:, :, :], reads=[B_Tt])

            for half in range(2):
                wt, wb = load_w([(w_in[l, :, 1024 + 512 * half:1536 + 512 * half], 0, 0)])
                for c in range(NCH):
                    ti = 0 if c < 2 else 1 + (c - 2) // 4
                    ps, psb = PS.get()
                    mm(ps[:, :], [(xn[:, k, c * 128:(c + 1) * 128], wt[:, k, :]) for k in range(16)], [wb, B_xn[ti]], psb)
                    vt, vb = Vr.get()
                    dve("tensor_tensor", [psb, B_bv], [vb], out=vt[:, :, 0:256], in0=ps[:, :].rearrange("p (h d) -> p h d", d=256),
                        in1=bv_[:, 512 * half:512 * half + 512].rearrange("p (h d) -> p h d", d=256), op=ALU.add)
                    dma("sp", d_v[c].rearrange("p (h d) -> p h d", d=257)[:, 2 * half:2 * half + 2, :], vt[:, :, :], reads=[vb])

            for half in range(2):
                wt, wb = load_w([(w_in[l, :, OFF_O + 512 * half:OFF_O + 512 * half + 512], 0, 0)])
                for i4 in range(4):
                    i = half * 4 + i4
                    Rt, Rb = Rr.get()
                    proj_rows(wt, wb, 128 * i4, FMIDX[("o", i)], Rt, Rb, func=AF.Sigmoid)
                    dma("sp", d_og[i], Rt[:, :], reads=[Rb])
            wt, wb = load_w([(w_in[l, :, OFF_F:OFF_F + 512], 0, 0)])
            for i in range(4):
                Rt, Rb = Rr.get()
                proj_rows(wt, wb, 128 * i, FMIDX[("xf", i)], Rt, Rb)
                dma("sp", d_xf[i], Rt[:, :], reads=[Rb])
            for i in range(4):
                wt, wb = load_w([(w_in[l, :, OFF_C + 512 * j + 128 * i:OFF_C + 512 * j + 128 * i + 128], 0, 128 * j) for j in range(3)])
                Ucc, Bcc = Ur.get()
                proj_rows(wt, wb, 128, FMIDX[("cc", i)], Ucc, Bcc)
                Ucx, Bcx = Ur.get()
                proj_rows(wt, wb, 256, FMIDX[("cx", i)], Ucx, Bcx)
                dve("tensor_tensor", [Bcc, Bcx], [Bcx], out=Ucx[:, :], in0=Ucx[:, :], in1=Ucc[:, :], op=ALU.mult)
                wc = [vec[:, l, V_CC + 4 * j + i:V_CC + 4 * j + i + 1] for j in range(3)]
                conv3(Ucc, Ucx, wc, [(0, L_CTX, None), (L_CTX, NT, 64)], 1, Bcc, Bcx)
                Ucb, Bcb = Ur.get()
                proj_rows(wt, wb, 0, FMIDX[("cb", i)], Ucb, Bcb)
                Rt, Rb = Rr.get()
                dve("tensor_tensor", [Bcc, Bcb], [Rb], out=Rt[:, :], in0=Ucc[:, :], in1=Ucb[:, :], op=ALU.mult)
                dma("sp", d_yc[i], Rt[:, :], reads=[Rb])
            for grp in range(12):
                wt, wb = load_w([(w_in[l, :, OFF_G + 512 * grp:OFF_G + 512 * grp + 512], 0, 0)])
                for i4 in range(4):
                    i = grp * 4 + i4
                    Rt, Rb = Rr.get()
                    proj_rows(wt, wb, 128 * i4, FMIDX[("gm", i)], Rt, Rb, func=AF.Sigmoid)
                    dma("sp", d_g[i], Rt[:, :], reads=[Rb])
            p.barrier()

    d_gr = scratch("d_gr", [4, 4, NT], F32)

    def phase_mlstm(l):
        es = contextlib.ExitStack()
        es_pre = contextlib.ExitStack()
        T = {}
        Bn = {}

        def alloc(stack, nm, shp, dt):
            T[nm] = stack.enter_context(nc.sbuf_tensor("%s_L%d" % (nm, l), shp, dt))
            Bn[nm] = Buf(nm)
        for d_ in range(2):
            for nm in ("Xa", "Xm"):
                alloc(es, nm + str(d_), [4, NT], F32)
        alloc(es, "colsb", [128, 432], F32)
        alloc(es, "keepb", [128, 144], F32)
        for nm in ("X1", "X2", "X3", "X4"):
            alloc(es_pre, nm, [4, NT], F32)
        alloc(es_pre, "mrow", [4, 2, 20], F32)
        alloc(es_pre, "mlast", [4, 2, 18], F32)
        alloc(es_pre, "klog", [4, 2, 18], F32)
        if True:
            colsb, keepb, mrow, mlast, klog = T["colsb"], T["keepb"], T["mrow"], T["mlast"], T["klog"]
            X1, X2, X3, X4 = T["X1"], T["X2"], T["X3"], T["X4"]
            sel4 = cst[0:4, C_SEL:C_SEL + 512].rearrange("p (h s) -> p h s", s=128)
            oh4 = cst[0:4, C_OH:C_OH + 512].rearrange("p (h s) -> p h s", s=128)
            id4 = cst[0:4, C_I4:C_I4 + 4]
            pcol, pcolb = PS.get()
            pkeep, pkeepb = PS.get()
            dve("memset", [], [Bn["mrow"]], ap=mrow[:], constant=0.0)
            orders = [list(range(NCH)), [1, 0] + list(range(NCH - 1, 1, -1))]
            for d_ in range(2):
                Xa, Xm = T["Xa%d" % d_], T["Xm%d" % d_]
                Ba, Bm = Bn["Xa%d" % d_], Bn["Xm%d" % d_]
                B1, B2, B3, B4 = Bn["X1"], Bn["X2"], Bn["X3"], Bn["X4"]
                dma("sp", Xa[:], d_gr[2 * d_], writes=[Ba])
                dma("sp", X1[:], d_gr[2 * d_ + 1], writes=[B1])
                act(X1[:], X1[:], AF.Exp, [B1], [B1], scale=-1.0)
                act(X1[:], X1[:], AF.Ln, [B1], [B1], bias=1.0)
                dve("tensor_scalar", [B1], [B1], out=X1[:], in0=X1[:], scalar1=-1.0, scalar2=None, op0=ALU.mult)
                dve("memset", [], [B3], ap=X3[:], constant=1.0)

                def sl(c):
                    if d_ == 0:
                        return slice(c * 128, (c + 1) * 128)
                    return slice((c + 1) * 128 - 1, (c * 128 - 1) if c > 0 else None, -1)

                def last(c):
                    t = (c + 1) * 128 - 1 if d_ == 0 else c * 128
                    return slice(t, t + 1)
                for c in range(NCH):
                    dve("tensor_tensor_scan", [B1, B3], [B2], out=X2[:, sl(c)], data0=X3[:, sl(c)], data1=X1[:, sl(c)],
                        initial=0.0, op0=ALU.mult, op1=ALU.add)
                dve("tensor_tensor", [Ba, B2], [Ba], out=Xa[:], in0=Xa[:], in1=X2[:], op=ALU.subtract)
                for c in range(NCH):
                    dve("tensor_tensor_scan", [Ba], [Bm], out=Xm[:, sl(c)], data0=Xa[:, sl(c)], data1=Xa[:, sl(c)],
                        initial=-1e30, op0=ALU.max, op1=ALU.max)
                for j, c in enumerate(orders[d_]):
                    cs = slice(c * 128, (c + 1) * 128)
                    mp = mrow[:, d_, j:j + 1]
                    dve("tensor_scalar", [Bm, Bn["mrow"]], [Bm], out=Xm[:, cs], in0=Xm[:, cs], scalar1=mp, scalar2=None, op0=ALU.max)
                    dve("tensor_tensor", [Bm, B2], [Bn["mrow"]], out=mrow[:, d_, j + 1:j + 2], in0=Xm[:, last(c)], in1=X2[:, last(c)], op=ALU.add)
                    dve("tensor_tensor", [Bm, Bn["mrow"]], [Bn["klog"]], out=klog[:, d_, c:c + 1], in0=mp, in1=Xm[:, last(c)], op=ALU.subtract)
                    dve("tensor_scalar", [Bm], [Bn["mlast"]], out=mlast[:, d_, c:c + 1], in0=Xm[:, last(c)], scalar1=-1.0, scalar2=None, op0=ALU.mult)
                    act(X4[:, cs], Xm[:, cs], AF.Exp, [Bm, Bn["mrow"]], [B4], bias=mp, scale=-1.0)
                    act(X3[:, cs], Xa[:, cs], AF.Exp, [Ba, Bn["mlast"]], [B3], bias=mlast[:, d_, c:c + 1])
                dve("tensor_tensor", [B2, Bm], [B2], out=X2[:], in0=X2[:], in1=Xm[:], op=ALU.add)
                act(X2[:], X2[:], AF.Exp, [B2], [B2], scale=-1.0)
                act(klog[:, d_, :], klog[:, d_, :], AF.Exp, [Bn["klog"]], [Bn["klog"]])
                dve("tensor_scalar", [Bm], [Bm], out=Xm[:], in0=Xm[:], scalar1=-1.0, scalar2=None, op0=ALU.mult)
                if debug:
                    for qi, (xt, xb) in enumerate(((Xa, Ba), (Xm, Bm), (X4, B4), (X2, B2), (X3, B3))):
                        dma("sp", d_rows[d_ * 5 + qi], xt[:], reads=[xb])
                for qi, (xt, xb) in enumerate(((X4, B4), (X2, B2), (X3, B3))):
                    for c in range(NCH):
                        o = ((qi * 2 + d_) * NCH + c) * 4
                        p.op("pe", lambda e, xt=xt, c=c, o=o: e.matmul(pcol[:, o:o + 4], lhsT=xt[:, c * 128:(c + 1) * 128], rhs=id4,
                                                                       start=True, stop=True), reads=[xb, B_cst], writes=[pcolb])
                for h in range(4):
                    o = (d_ * 4 + h) * NCH
                    p.op("pe", lambda e, h=h, o=o, d_=d_: e.matmul(pkeep[:, o:o + NCH], lhsT=sel4[:, h, :], rhs=klog[:, d_, :], start=True, stop=True),
                         reads=[Bn["klog"], B_cst], writes=[pkeepb])
            act(colsb[:], pcol[:, 0:432], AF.Copy, [pcolb], [Bn["colsb"]])
            act(keepb[:], pkeep[:, 0:144], AF.Copy, [pkeepb], [Bn["keepb"]])

            def col(qi, d_, c, h):
                o = ((qi * 2 + d_) * NCH + c) * 4 + h
                return colsb[:, o:o + 1]

            p.barrier()
            es_pre.close()
            per = [("qTc", [128, 8, 128], BF16, 2), ("kTc", [128, 8, 128], BF16, 2), ("ktk", [128, 1024], BF16, 2), ("vc", [128, 4, 257], BF16, 2),
                   ("ogc", [128, 8, 128], BF16, 3), ("hsf", [128, 1024], F32, 3), ("hsb", [128, 1024], F32, 2), ("Dt", [128, 512], F32, 2),
                   ("SdT", [128, 512], BF16, 2), ("t1", [128, 257], F32, 8), ("nd", [128, 257], F32, 8), ("kw", [128, 256], BF16, 8),
                   ("hn", [128, 1024], BF16, 2), ("hmc", [128, 8, 128], BF16, 2), ("sm", [128, 8], F32, 6), ("am4", [4, 4, 128], F32, 2)]
            rg = {}
            for nm, shp, dt, dep in per:
                tl = []
                for i in range(dep):
                    alloc(es, "%s%d" % (nm, i), shp, dt)
                    tl.append(T["%s%d" % (nm, i)])
                rg[nm] = Ring(tl)
            alloc(es, "Cst", [128, 4, 2, 257], F32)
            alloc(es, "Cbf", [128, 4, 2, 257], BF16)
            Cst, Cbf = T["Cst"], T["Cbf"]
            BCs = [[Buf(), Buf()] for _ in range(4)]
            BCb = [[Buf(), Buf()] for _ in range(4)]
            B_dhs = [Buf() for _ in range(NCH)]
            PSl = PS
            for d_ in range(2):
                Xa, Xm = T["Xa%d" % d_], T["Xm%d" % d_]
                Ba, Bm = Bn["Xa%d" % d_], Bn["Xm%d" % d_]
                mask = cst[:, (C_MF if d_ == 0 else C_MB):(C_MF if d_ == 0 else C_MB) + 128]
                dve("memset", [], [x for y in BCs for x in y], ap=Cst[:], constant=0.0)
                dve("memset", [], [x for y in BCb for x in y], ap=Cbf[:], constant=0.0)
                def issue_loads(c_):
                    cs_ = slice(c_ * 128, (c_ + 1) * 128)
                    L = {}
                    L["q"] = rg["qTc"].get()
                    L["k"] = rg["kTc"].get()
                    L["kt"] = rg["ktk"].get()
                    L["v"] = rg["vc"].get()
                    dma("sp", L["q"][0][:], d_qT[:, :, cs_].rearrange("f p t -> p f t"), writes=[L["q"][1]])
                    dma("sp", L["k"][0][:], d_kT[:, :, cs_].rearrange("f p t -> p f t"), writes=[L["k"][1]])
                    dma("sp", L["kt"][0][:], d_ktok[c_], writes=[L["kt"][1]])
                    dma("sp", L["v"][0][:], d_v[c_].rearrange("p (h d) -> p h d", d=257), writes=[L["v"][1]])
                    if d_ == 1:
                        L["hsf"] = rg["hsf"].get()
                        dma("sp", L["hsf"][0][:], d_hs[c_], reads=[B_dhs[c_]], writes=[L["hsf"][1]])
                        L["og"] = rg["ogc"].get()
                        dma("sp", L["og"][0][:], d_og[:, :, cs_].rearrange("f p t -> p f t"), writes=[L["og"][1]])
                    return L
                def front(j):
                    c = orders[d_][j]
                    cs = slice(c * 128, (c + 1) * 128)
                    L = issue_loads(c)
                    qTc, Bq = L["q"]
                    kTc, Bk = L["k"]
                    ktk, Bkt = L["kt"]
                    am4, Bam = rg["am4"].get()
                    dve("tensor_tensor", [Ba, B_cst], [Bam], out=am4[:], in0=Xa[:, cs].unsqueeze(1).to_broadcast([4, 4, 128]), in1=oh4, op=ALU.mult)
                    pSa, pSb_ = PSl.get()
                    for h in range(4):
                        mm(pSa[:, 128 * h:128 * h + 128], [(kTc[:, 2 * h + kc, :], qTc[:, 2 * h + kc, :]) for kc in range(2)], [Bk, Bq], pSb_)
                    pEa, pEb_ = PSl.get()
                    for h in range(4):
                        mm(pEa[:, 128 * h:128 * h + 128], [(am4[:, h, :], oh_ones), (sel4[:, h, :], Xm[:, cs]), (id_bf[:, :], mask_bf)],
                           [Bam, Bm, B_cst, B_idbf, B_maskbf], pEb_)
                    Dta, BDa = rg["Dt"].get()
                    act(Dta[:], pEa[:, :], AF.Exp, [pEb_], [BDa])
                    SdTa, BSa = rg["SdT"].get()
                    dve("tensor_tensor", [pSb_, BDa], [BSa], out=SdTa[:], in0=pSa[:, :], in1=Dta[:], op=ALU.mult)
                    kw = {}
                    for h in range(4):
                        kw[h] = rg["kw"].get()
                        dve("tensor_scalar", [Bkt, Bn["colsb"]], [kw[h][1]], out=kw[h][0][:], in0=ktk[:, 256 * h:256 * h + 256],
                            scalar1=col(2, d_, c, h), scalar2=None, op0=ALU.mult)
                    L["SdT"] = (SdTa, BSa)
                    L["kw"] = kw
                    return L

                def midback(j, L):
                    c = orders[d_][j]
                    cs = slice(c * 128, (c + 1) * 128)
                    qTc, Bq = L["q"]
                    vc, Bv = L["v"]
                    SdTa, BSa = L["SdT"]
                    kw = L["kw"]
                    H = range(4)
                    pN, pI, t1, nd = {}, {}, {}, {}
                    for h in H:
                        pN[h] = PSl.get()
                        mm(pN[h][0][:, 0:257], [(SdTa[:, 128 * h:128 * h + 128], vc[:, h, :])], [BSa, Bv], pN[h][1])
                        pI[h] = PSl.get()
                        mm(pI[h][0][:, 0:257], [(qTc[:, 2 * h + kc, :], Cbf[:, h, kc, :]) for kc in range(2)], [Bq, BCb[h][0], BCb[h][1]], pI[h][1])
                        t1[h] = rg["t1"].get()
                        act(t1[h][0][:], pI[h][0][:, 0:257], AF.Identity, [pI[h][1], Bn["colsb"]], [t1[h][1]], scale=col(0, d_, c, h))
                        nd[h] = rg["nd"].get()
                        dve("tensor_tensor", [pN[h][1], t1[h][1]], [nd[h][1]], out=nd[h][0][:], in0=pN[h][0][:, 0:257], in1=t1[h][0][:], op=ALU.add)
                    for h in H:
                        for kc in range(2):
                            pC, pCb = PSl.get()
                            mm(pC[:, 0:257], [(kw[h][0][:, 128 * kc:128 * kc + 128], vc[:, h, :])], [kw[h][1], Bv], pCb)
                            ko = (d_ * 4 + h) * NCH + c
                            dve("scalar_tensor_tensor", [BCs[h][kc], Bn["keepb"], pCb], [BCs[h][kc]], out=Cst[:, h, kc, :], in0=Cst[:, h, kc, :],
                                scalar=keepb[:, ko:ko + 1], in1=pC[:, 0:257], op0=ALU.mult, op1=ALU.add)
                            act(Cbf[:, h, kc, :], Cst[:, h, kc, :], AF.Copy, [BCs[h][kc]], [BCb[h][kc]])
                    return (j, L, nd)

                def tail(ctx_):
                    j, L, nd = ctx_
                    c = orders[d_][j]
                    cs = slice(c * 128, (c + 1) * 128)
                    H = range(4)
                    sm, Bsm = rg["sm"].get()
                    hs, Bhs = rg["hsf" if d_ == 0 else "hsb"].get()
                    for h in H:
                        act(sm[:, h:h + 1], nd[h][0][:, 256:257], AF.Abs, [nd[h][1]], [Bsm])
                    for h in H:
                        dve("tensor_scalar", [Bsm, Bn["colsb"]], [Bsm], out=sm[:, h:h + 1], in0=sm[:, h:h + 1],
                            scalar1=col(1, d_, c, h), scalar2=None, op0=ALU.max)
                    dve("reciprocal", [Bsm], [Bsm], out=sm[:, 0:4], in_=sm[:, 0:4])
                    if d_ == 0:
                        for h in H:
                            dve("tensor_scalar", [nd[h][1], Bsm], [Bhs], out=hs[:, 256 * h:256 * h + 256], in0=nd[h][0][:, 0:256],
                                scalar1=sm[:, h:h + 1], scalar2=None, op0=ALU.mult)
                        dma("sp", d_hs[c], hs[:], reads=[Bhs], writes=[B_dhs[c]])
                    else:
                        hsf, Bhsf = L["hsf"]
                        ogc, Bog = L["og"]
                        for h in H:
                            dve("scalar_tensor_tensor", [nd[h][1], Bsm, Bhsf], [Bhs], out=hs[:, 256 * h:256 * h + 256], in0=nd[h][0][:, 0:256],
                                scalar=sm[:, h:h + 1], in1=hsf[:, 256 * h:256 * h + 256], op0=ALU.mult, op1=ALU.add)
                        sm2, Bsm2 = rg["sm"].get()
                        hn, Bhn = rg["hn"].get()
                        for h in range(4):
                            act(hn[:, 256 * h:256 * h + 256], hs[:, 256 * h:256 * h + 256], AF.Square, [Bhs], [Bhn])
                        dve("tensor_reduce", [Bhn], [Bsm2], out=sm2[:, 4:8], in_=hn[:, :].rearrange("p (h d) -> p h d", d=256),
                            axis=mybir.AxisListType.X, op=ALU.add)
                        act(sm2[:, 4:8], sm2[:, 4:8], AF.Ln, [Bsm2], [Bsm2], bias=EPS, scale=1.0 / 256)
                        act(sm2[:, 4:8], sm2[:, 4:8], AF.Exp, [Bsm2], [Bsm2], scale=-0.5)
                        for h in range(4):
                            dve("tensor_scalar", [Bhs, Bsm2], [Bhn], out=hn[:, 256 * h:256 * h + 256], in0=hs[:, 256 * h:256 * h + 256],
                                scalar1=sm2[:, 4 + h:5 + h], scalar2=None, op0=ALU.mult)
                        hmc, Bhm = rg["hmc"].get()
                        for f in range(8):
                            p.op("pe", lambda e, f=f, hn=hn: e.transpose(out=psT[:, f, :], in_=hn[:, 128 * f:128 * f + 128], identity=id_bf[:, :]),
                                 reads=[Bhn, B_idbf], writes=[B_psT])
                        for f in range(8):
                            dve("scalar_tensor_tensor", [B_psT, B_vec, Bog], [Bhm], out=hmc[:, f, :], in0=psT[:, f, :],
                                scalar=vec[:, l, V_MN + f:V_MN + f + 1], in1=ogc[:, f, :], op0=ALU.mult, op1=ALU.mult)
                        dma("sp", d_hm[:, :, cs].rearrange("f p t -> p f t"), hmc[:], reads=[Bhm])

                mask_bf = maskbf[:, 128 * d_:128 * d_ + 128]
                cur = front(0)
                pend = None
                for j in range(NCH):
                    nxtL = front(j + 1) if j + 1 < NCH else None
                    ctx_ = midback(j, cur)
                    if pend is not None:
                        tail(pend)
                    pend = ctx_
                    cur = nxtL
                tail(pend)
            p.barrier()
            es.close()

    oh_ones = cst[0:4, C_SEL + 0:C_SEL + 128]

    def phase_fourier(l):
        xf = nc.sbuf_tensor("xfa%d" % l, [128, 4, NT], BF16)
        pq = nc.sbuf_tensor("pq%d" % l, [128, NCH, 4, 256], BF16)
        yr = [nc.sbuf_tensor("yr%d_%d" % (l, i), [128, 512], BF16) for i in range(2)]
        with xf as xf_, pq as pq_, yr[0] as y0, yr[1] as y1:
            Yr = Ring([y0, y1])
            B_xf, B_pq = Buf(), [Buf() for _ in range(NCH)]
            dma("sp", xf_[:], d_xf.rearrange("g p t -> p g t"), writes=[B_xf])
            for c in range(NCH):
                for g2 in range(2):
                    ps, psb = PS.get()
                    for gg in range(2):
                        g = 2 * g2 + gg
                        mm(ps[:, 256 * gg:256 * gg + 256], [(xf_[:, g, c * 128:(c + 1) * 128], dft_cs[:, :])], [B_xf, B_dftcs], psb)
                    act(pq_[:, c, 2 * g2:2 * g2 + 2, :], ps[:, :].rearrange("p (g w) -> p g w", w=256), AF.Copy, [psb], [B_pq[c]])
            for g in range(4):
                ps, psb = PS.get()
                pairs = []
                for tc in range(2):
                    pairs.append((pq_[:, tc, g, 0:128], dftc_sb[:, 0, tc, :]))
                    pairs.append((pq_[:, tc, g, 128:256], dftc_sb[:, 1, tc, :]))
                mm(ps[:, 0:256], pairs, [B_pq[0], B_pq[1], B_dftc], psb)
                yt, yb = Yr.get()
                act(yt[:, 0:256], ps[:, 0:256], AF.Copy, [psb], [yb])
                dma("sp", d_yf[g, :, 0:256], yt[:, 0:256], reads=[yb])
            for j in range(4):
                wc, wcb = load_w([(dftl_d[0, :, 512 * j:512 * j + 512], 0, 0)])
                ws, wsb = load_w([(dftl_d[1, :, 512 * j:512 * j + 512], 0, 0)])
                for g in range(4):
                    ps, psb = PS.get()
                    pairs = []
                    for tc in range(16):
                        pairs.append((pq_[:, 2 + tc, g, 0:128], wc[:, tc, :]))
                        pairs.append((pq_[:, 2 + tc, g, 128:256], ws[:, tc, :]))
                    mm(ps[:, :], pairs, B_pq[2:] + [wcb, wsb], psb)
                    yt, yb = Yr.get()
                    act(yt[:, :], ps[:, :], AF.Copy, [psb], [yb])
                    dma("sp", d_yf[g, :, L_CTX + 512 * j:L_CTX + 512 * j + 512], yt[:, :], reads=[yb])
            p.barrier()

    def phase_merge(l):
        br = nc.sbuf_tensor("br%d" % l, [128, 16, NT], BF16)
        gr = [nc.sbuf_tensor("gr%d_%d" % (l, i), [128, 3, NT], BF16) for i in range(2)]
        mr = [nc.sbuf_tensor("mr%d_%d" % (l, i), [128, NT], BF16) for i in range(2)]
        tm = [nc.sbuf_tensor("tm%d_%d" % (l, i), [128, 512], F32) for i in range(4)]
        with br as br_, gr[0] as g0, gr[1] as g1, mr[0] as m0, mr[1] as m1, tm[0] as a0, tm[1] as a1, tm[2] as a2, tm[3] as a3:
            Gr, Mr, Tm = Ring([g0, g1]), Ring([m0, m1]), Ring([a0, a1, a2, a3])
            B_br = Buf()
            dma("sp", br_[:, 0:8, :], d_hm.rearrange("f p t -> p f t"), writes=[B_br])
            dma("sp", br_[:, 8:12, :], d_yf.rearrange("f p t -> p f t"), writes=[B_br])
            dma("sp", br_[:, 12:16, :], d_yc.rearrange("f p t -> p f t"), writes=[B_br])
            for dg in range(4):
                wt, wb = load_w([(w_pm[l, :, 512 * dg:512 * dg + 512], 0, 0), (w_pf[l, :, 512 * dg:512 * dg + 512], 8, 0),
                                 (w_pc[l, :, 512 * dg:512 * dg + 512], 12, 0)])
                for d4 in range(4):
                    dch = dg * 4 + d4
                    gt, gb = Gr.get()
                    for bi in range(3):
                        dma("sp", gt[:, bi, :], d_g[16 * bi + dch], writes=[gb])
                    mt, mb = Mr.get()
                    for ti, (t0, t1) in enumerate(TT):
                        n = t1 - t0
                        acc = None
                        for bi, (k0, k1) in enumerate(((0, 8), (8, 12), (12, 16))):
                            ps, psb = PS.get()
                            mm(ps[:, :n], [(wt[:, k, 128 * d4:128 * d4 + 128], br_[:, k, t0:t1]) for k in range(k0, k1)], [wb, B_br], psb)
                            tt, tb = Tm.get()
                            dve("tensor_tensor", [psb, gb], [tb], out=tt[:, :n], in0=ps[:, :n], in1=gt[:, bi, t0:t1], op=ALU.mult)
                            if acc is not None:
                                at, ab = acc
                                if bi == 2:
                                    dve("tensor_tensor", [tb, ab], [mb], out=mt[:, t0:t1], in0=tt[:, :n], in1=at[:, :n], op=ALU.add)
                                else:
                                    dve("tensor_tensor", [tb, ab], [tb], out=tt[:, :n], in0=tt[:, :n], in1=at[:, :n], op=ALU.add)
                            acc = (tt, tb)
                    dma("sp", d_mg[dch], mt[:, :], reads=[mb])
            p.barrier()

    def phase_resid_proj(l, src_d, wmat, gate_off, nm):
        xa = nc.sbuf_tensor("xa%s%d" % (nm, l), [128, 16, NT], BF16)
        hr = [nc.sbuf_tensor("hr%s%d_%d" % (nm, l, i), [128, NT], F32) for i in range(2)]
        with xa as xa_, hr[0] as h0, hr[1] as h1:
            Hr = Ring([h0, h1])
            B_xa = [Buf() for _ in TT]
            for ti, (t0, t1) in enumerate(TT):
                dma("sp", xa_[:, :, t0:t1], src_d[:, :, t0:t1].rearrange("f p t -> p f t"), writes=[B_xa[ti]])
            for dg in range(4):
                wt, wb = load_w([(wmat[l, :, 512 * dg:512 * dg + 512], 0, 0)])
                for d4 in range(4):
                    dch = dg * 4 + d4
                    ht, hb = Hr.get()
                    dma("sp", ht[:, :], d_h[dch], reads=B_dh[dch], writes=[hb])
                    for ti, (t0, t1) in enumerate(TT):
                        n = t1 - t0
                        seg = 1 if ti == 0 else 0
                        ps, psb = PS.get()
                        mm(ps[:, :n], [(wt[:, k, 128 * d4:128 * d4 + 128], xa_[:, k, t0:t1]) for k in range(16)], [wb, B_xa[ti]], psb)
                        dve("scalar_tensor_tensor", [psb, B_modc, hb], [hb], out=ht[:, t0:t1], in0=ps[:, :n],
                            scalar=modc[:, gate_off + dch, seg:seg + 1], in1=ht[:, t0:t1], op0=ALU.mult, op1=ALU.add)
                    dma("sp", d_h[dch], ht[:, :], reads=[hb], writes=B_dh[dch])
            p.barrier()

    def phase_ffn_up(l, xn, B_xn):
        U = [nc.sbuf_tensor("Uf%d_%d" % (l, i), [128, NT], F32) for i in range(4)]
        G = [nc.sbuf_tensor("Gf%d_%d" % (l, i), [128, NT], BF16) for i in range(2)]
        R = [nc.sbuf_tensor("Rf%d_%d" % (l, i), [128, NT], BF16) for i in range(2)]
        with U[0] as U0, U[1] as U1, U[2] as U2, U[3] as U3, G[0] as G0, G[1] as G1, R[0] as R0, R[1] as R1:
            Ur, Gr, Rr = Ring([U0, U1, U2, U3]), Ring([G0, G1]), Ring([R0, R1])
            for j4 in range(NFF // 4):
                wt, wb = load_w([(w_up[l, :, 512 * j4:512 * j4 + 512], 0, 0)])
                wt2, wb2 = load_w([(w_up[l, :, D_FF + 512 * j4:D_FF + 512 * j4 + 512], 0, 0)])
                for jj in range(4):
                    j = 4 * j4 + jj
                    Ut, Ub = Ur.get()
                    Gt, Gb = Gr.get()
                    for ti, (t0, t1) in enumerate(TT):
                        n = t1 - t0
                        ps, psb = PS.get()
                        mm(ps[:, :n], [(wt[:, k, 128 * jj:128 * jj + 128], xn[:, k, t0:t1]) for k in range(16)], [wb, B_xn[ti]], psb)
                        act(Ut[:, t0:t1], ps[:, :n], AF.Copy, [psb], [Ub])
                        ps2, psb2 = PS.get()
                        mm(ps2[:, :n], [(wt2[:, k, 128 * jj:128 * jj + 128], xn[:, k, t0:t1]) for k in range(16)], [wb2, B_xn[ti]], psb2)
                        act(Gt[:, t0:t1], ps2[:, :n], AF.Copy, [psb2], [Gb])
                    Ot, Ob = Ur.get()
                    wc = [vec[:, l, V_CFF + NFF * q + j:V_CFF + NFF * q + j + 1] for q in range(3)]
                    dve("tensor_scalar", [Ub, B_vec], [Ob], out=Ot[:, :], in0=Ut[:, :], scalar1=wc[1], scalar2=None, op0=ALU.mult)
                    for (a, b_, sh) in ((0, L_CTX, 1), (L_CTX, NT, 64)):
                        dve("scalar_tensor_tensor", [Ub, B_vec, Ob], [Ob], out=Ot[:, a + sh:b_], in0=Ut[:, a:b_ - sh],
                            scalar=wc[0], in1=Ot[:, a + sh:b_], op0=ALU.mult, op1=ALU.add)
                        dve("scalar_tensor_tensor", [Ub, B_vec, Ob], [Ob], out=Ot[:, a:b_ - sh], in0=Ut[:, a + sh:b_],
                            scalar=wc[2], in1=Ot[:, a:b_ - sh], op0=ALU.mult, op1=ALU.add)
                    act(Ot[:, :], Ot[:, :], AF.Silu, [Ob], [Ob])
                    Rt, Rb = Rr.get()
                    dve("tensor_tensor", [Ob, Gb], [Rb], out=Rt[:, :], in0=Ot[:, :], in1=Gt[:, :], op=ALU.mult)
                    dma("sp", d_hid[j], Rt[:, :], reads=[Rb])
            p.barrier()

    def phase_ffn_down(l):
        hd = [nc.sbuf_tensor("hd%d_%d" % (l, i), [128, NFF, 512], BF16) for i in range(2)]
        hr = [nc.sbuf_tensor("hq%d_%d" % (l, i), [128, 512], F32) for i in range(4)]
        with hd[0] as hd0, hd[1] as hd1, hr[0] as q0, hr[1] as q1, hr[2] as q2, hr[3] as q3:
            Hq = Ring([q0, q1, q2, q3])
            hds = [hd0, hd1]
            B_hd = [Buf(), Buf()]
            for pr in ((0, 1), (2, 3), (4,)):
                for i, ti in enumerate(pr):
                    t0, t1 = TT[ti]
                    dma("sp", hds[i][:, :, :t1 - t0], d_hid[:, :, t0:t1].rearrange("f p t -> p f t"), writes=[B_hd[i]])
                for dch in range(16):
                    wt, wb = WR.get()
                    wv = wt[:].rearrange("p k c -> p (k c)")[:, 0:NFF * 128].rearrange("p (k c) -> p k c", c=128)
                    dma("pool", wv, w_down[l, :, 128 * dch:128 * dch + 128].rearrange("(k p) w -> p k w", p=128), writes=[wb])
                    for i, ti in enumerate(pr):
                        t0, t1 = TT[ti]
                        n = t1 - t0
                        seg = 1 if ti == 0 else 0
                        ht, hb = Hq.get()
                        dma("sp", ht[:, :n], d_h[dch, :, t0:t1], reads=[B_dh[dch][ti]], writes=[hb])
                        ps, psb = PS.get()
                        mm(ps[:, :n], [(wv[:, k, :], hds[i][:, k, :n]) for k in range(NFF)], [wb, B_hd[i]], psb)
                        dve("scalar_tensor_tensor", [psb, B_modc, hb], [hb], out=ht[:, :n], in0=ps[:, :n],
                            scalar=modc[:, 80 + dch, seg:seg + 1], in1=ht[:, :n], op0=ALU.mult, op1=ALU.add)
                        dma("sp", d_h[dch, :, t0:t1], ht[:, :n], reads=[hb], writes=[B_dh[dch][ti]])
            p.barrier()

    oh_ones = cst[0:4, NCST - 128:NCST]
    stages = []
    for l in range(n_layers):
        phase_mod(l)
        xn_c = nc.sbuf_tensor("xn%d" % l, [128, 16, NT], BF16)
        with xn_c as xn:
            B_xn = [Buf() for _ in TT]
            phase_norm(l, 0, xn, B_xn)
            if stop_after == "norm":
                break
            phase_inproj(l, xn, B_xn)
        if stop_after == "inproj":
            break
        phase_mlstm(l)
        if stop_after == "mlstm":
            break
        phase_fourier(l)
        phase_merge(l)
        phase_resid_proj(l, d_mg, w_o, 32, "o")
        if stop_after == "mixer":
            break
        xn_c = nc.sbuf_tensor("xm%d" % l, [128, 16, NT], BF16)
        with xn_c as xn:
            B_xn = [Buf() for _ in TT]
            phase_norm(l, 1, xn, B_xn)
            phase_ffn_up(l, xn, B_xn)
        phase_ffn_down(l)
    if stop_after is None:
        phase_norm(0, 0, None, None, final=True)
    p.barrier()
    return p.emit()


def _host_consts():
    cst = np.zeros((128, NCST), np.float32)
    cst[:, C_ID:C_ID + 128] = np.eye(128, dtype=np.float32)
    s = np.arange(128)[:, None]
    t = np.arange(128)[None, :]
    cst[:, C_MF:C_MF + 128] = np.where(s <= t, 0.0, NEG)
    cst[:, C_MB:C_MB + 128] = np.where(s >= t, 0.0, NEG)
    for h in range(4):
        cst[h, C_SEL + 128 * h:C_SEL + 128 * h + 128] = 1.0
        cst[h, C_OH + 128 * h:C_OH + 128 * h + 128] = 1.0
    cst[0:4, C_I4:C_I4 + 4] = np.eye(4, dtype=np.float32)
    ang = 2 * np.pi * (np.arange(128)[:, None] * np.arange(128)[None, :] % 128) / 128.0
    cst[:, C_DFT:C_DFT + 128] = np.cos(ang)
    cst[:, C_DFT + 128:C_DFT + 256] = np.sin(ang)
    cst[0:4, NCST - 128:NCST] = 1.0

    def seq_dft(T):
        idx = (np.arange(T, dtype=np.int64)[:, None] * np.arange(T, dtype=np.int64)[None, :]) % T
        a = 2 * np.pi * idx / T
        sc = 1.0 / np.sqrt(T * 128.0)
        return np.stack([np.cos(a) * sc, -np.sin(a) * sc]).astype(np.float32)
    return cst, seq_dft(SEQ), seq_dft(L_CTX)


def _pcols(v):
    return np.ascontiguousarray(v.reshape(-1, 128).T)


def _host_vec(inp):
    vec = np.zeros((DEPTH, 128, NV), np.float32)
    for l in range(DEPTH):
        vec[l, :, V_N1:V_N1 + 16] = _pcols(inp["norm1_w"][l])
        vec[l, :, V_N2:V_N2 + 16] = _pcols(inp["norm2_w"][l])
        vec[l, :, V_BMOD:V_BMOD + 96] = _pcols(inp["b_mod"][l])
        for n, (kind, i, off) in enumerate(FM):
            vec[l, :, V_BIN + n] = inp["b_in"][l, off:off + 128]
        for j in range(3):
            vec[l, :, V_CQ + 8 * j:V_CQ + 8 * j + 8] = _pcols(inp["conv_q_w"][l, j])
            vec[l, :, V_CK + 8 * j:V_CK + 8 * j + 8] = _pcols(inp["conv_k_w"][l, j])
            vec[l, :, V_CC + 4 * j:V_CC + 4 * j + 4] = _pcols(inp["conv_c_w"][l, j])
            vec[l, :, V_CFF + NFF * j:V_CFF + NFF * j + NFF] = _pcols(inp["conv_ff_w"][l, j])
        vec[l, :, V_MN:V_MN + 8] = _pcols(inp["mlstm_norm_w"][l])
        vec[l, :, V_FN:V_FN + 16] = _pcols(inp["final_norm_w"])
    bvrep = np.ascontiguousarray(np.broadcast_to(inp["b_in"][:, None, 1024:2048], (DEPTH, 128, 1024))).astype(np.float32)
    bgt = np.ascontiguousarray(inp["b_in"][:, OFF_GATES:OFF_GATES + 16].reshape(DEPTH, 4, 4).transpose(0, 2, 1)).astype(np.float32)
    return vec, bvrep, bgt


def make_in_maps(inp, cores):
    cst, dftl, dftc = _host_consts()
    vec, bvrep, bgt = _host_vec(inp)
    shared = {"vec": vec, "bvrep": bvrep, "bgt": bgt, "cst": cst, "dft_lat": dftl, "dft_ctx": dftc}
    for k in ("w_mod", "w_in", "w_pm", "w_pf", "w_pc", "w_o", "w_up", "w_down"):
        shared[k] = np.ascontiguousarray(inp[k], dtype=np.float32)
    maps = []
    for b in cores:
        hT0 = np.ascontiguousarray(np.concatenate([inp["ctx"][b], inp["x"][b]], axis=0).T)
        cc2 = np.stack([inp["c"][b], inp["c_ctx"]], axis=1)
        ccp = np.ascontiguousarray(cc2.reshape(16, 128, 2).transpose(1, 0, 2).reshape(128, 32))
        m = dict(shared)
        m["hT0"] = hT0.astype(np.float32)
        m["ccp"] = ccp.astype(np.float32)
        maps.append(m)
    return maps


def kernel(**inputs):
    inp = {k: np.asarray(v) for k, v in inputs.items()}
    nc = build()
    cores = [0, 1, 2, 3, 0, 1, 2, 3]
    maps = make_in_maps(inp, cores)
    res = run_bass_kernel_spmd(nc, maps, core_ids=list(range(8)))
    out = np.stack([res.results[b]["outT"].T for b in range(4)], axis=0)
    return np.ascontiguousarray(out.astype(np.float32))
```

```python
import contextlib
import numpy as np
import concourse.bass as bass
import concourse.mybir as mybir
from concourse.bass_utils import run_bass_kernel_spmd

F32 = mybir.dt.float32
BF16 = mybir.dt.bfloat16
AF = mybir.ActivationFunctionType
ALU = mybir.AluOpType

ENGS = ["pe", "act", "dve", "pool", "sp"]

D = 2048
L_CTX = 256
SEQ = 2048
NT = L_CTX + SEQ
NCH = NT // 128
DEPTH = 4
D_M = 1024
D_F = 512
D_C = 512
D_FF = 5632
NFF = D_FF // 128
OFF_GATES = 2048
OFF_Q = 2064
OFF_O = OFF_Q + 1024
OFF_F = OFF_O + 1024
OFF_C = OFF_F + 512
OFF_G = OFF_C + 3 * 512
N_IN = OFF_G + 3 * D
EPS = 1e-6
TT = [(0, 256), (256, 768), (768, 1280), (1280, 1792), (1792, 2304)]
NEG = -30000.0

V_N1, V_N2, V_BMOD, V_BIN, V_CQ, V_CK, V_MN, V_CC, V_CFF, V_FN = 0, 16, 32, 128, 216, 240, 264, 272, 284, 416
NV = 432
FM = []
for i in range(8):
    FM.append(("k", i, 0 + 128 * i))
for i in range(8):
    FM.append(("q", i, OFF_Q + 128 * i))
for i in range(8):
    FM.append(("o", i, OFF_O + 128 * i))
for i in range(4):
    FM.append(("xf", i, OFF_F + 128 * i))
for i in range(4):
    FM.append(("cb", i, OFF_C + 128 * i))
for i in range(4):
    FM.append(("cc", i, OFF_C + 512 + 128 * i))
for i in range(4):
    FM.append(("cx", i, OFF_C + 1024 + 128 * i))
for i in range(48):
    FM.append(("gm", i, OFF_G + 128 * i))
FMIDX = {(k, i): n for n, (k, i, _) in enumerate(FM)}
C_ID, C_MF, C_MB, C_SEL, C_I4, C_DFT, C_OH = 0, 128, 256, 384, 896, 900, 1156
NCST = 1156 + 512 + 128


class Buf:
    __slots__ = ("name", "w", "r")

    def __init__(self, name=""):
        self.name = name
        self.w = None
        self.r = {}


class Prog:
    def __init__(self, ndma=28):
        self.nc = bass.Bass("TRN2", target_bir_lowering=False)
        self.ndma = ndma
        self.streams = {e: [] for e in ENGS}
        self.cnt = {e: 0 for e in ENGS}
        self.seen = {e: {} for e in ENGS}
        self.snap = {}
        self.dma_cnt = [0] * ndma
        self.dma_rr = 0
        self.dma_rr2 = 0
        self.waited = {}
        self.es = contextlib.ExitStack()
        self.n_ops = 0

    def sb(self, name, shape, dt):
        return self.es.enter_context(self.nc.sbuf_tensor("sb_" + name, list(shape), dt))

    def ps(self, name, shape, dt=F32):
        return self.es.enter_context(self.nc.psum_tensor("pp_" + name, list(shape), dt))

    def dram(self, name, shape, dt, kind="Internal"):
        return self.nc.dram_tensor(name, list(shape), dt, kind=kind)

    def op(self, eng, fn, reads=(), writes=(), dma=False):
        deps = {}

        def add(d):
            if d is None:
                return
            tl, s = d
            if deps.get(tl, 0) < s:
                deps[tl] = s

        for b in reads:
            add(b.w)
        for b in writes:
            add(b.w)
            for tl, s in b.r.items():
                add((tl, s))
        k = None
        if dma:
            if eng == "pool":
                k = self.ndma - 8 + self.dma_rr2
                self.dma_rr2 = (self.dma_rr2 + 1) % 8
            else:
                k = self.dma_rr
                self.dma_rr = (k + 1) % (self.ndma - 8)
            if self.dma_cnt[k] > 0:
                add(("d%d" % k, self.dma_cnt[k]))
        seen = self.seen[eng]
        waits = []
        for tl, s in deps.items():
            if tl == eng and eng == "pe":
                continue
            if seen.get(tl, 0) >= s:
                continue
            waits.append((tl, s))
        for tl, s in waits:
            sn = self.snap.get((tl, s))
            if sn:
                for t2, s2 in sn.items():
                    if seen.get(t2, 0) < s2:
                        seen[t2] = s2
            if seen.get(tl, 0) < s:
                seen[tl] = s
            self.waited.setdefault(tl, set()).add(s)
        if dma:
            self.dma_cnt[k] += 1
            done = ("d%d" % k, self.dma_cnt[k])
        else:
            self.cnt[eng] += 1
            done = (eng, self.cnt[eng])
        self.snap[done] = dict(seen)
        self.streams[eng].append((waits, fn, done))
        for b in reads:
            if b.r.get(done[0], 0) < done[1]:
                b.r[done[0]] = done[1]
        for b in writes:
            b.w = done
            b.r = {}
        self.n_ops += 1
        return done

    def barrier(self, engs=("pe", "act", "dve", "sp", "pool")):
        targets = []
        for k in range(self.ndma):
            if self.dma_cnt[k] > 0:
                targets.append(("d%d" % k, self.dma_cnt[k]))
        for e in ENGS:
            if self.cnt[e] > 0:
                targets.append((e, self.cnt[e]))
        for eng in engs:
            seen = self.seen[eng]
            waits = []
            for tl, s in targets:
                if tl == eng or seen.get(tl, 0) >= s:
                    continue
                waits.append((tl, s))
                seen[tl] = s
                if not tl.startswith("d"):
                    self.waited.setdefault(tl, set()).add(s)
            for tl, s in waits:
                sn = self.snap.get((tl, s))
                if sn:
                    for t2, s2 in sn.items():
                        if seen.get(t2, 0) < s2:
                            seen[t2] = s2
            if eng != "pe":
                seen[eng] = self.cnt[eng]
            self.streams[eng].append((waits, None, None))

    def emit(self):
        nc = self.nc
        self.snap = None
        sems = {}
        for e in ENGS:
            sems[e] = self.es.enter_context(nc.semaphore("s_" + e))
        for k in range(self.ndma):
            sems["d%d" % k] = self.es.enter_context(nc.semaphore("s_d%d" % k))
        rank = {}
        for tl, ss in self.waited.items():
            if tl.startswith("d"):
                continue
            rank[tl] = {s: i + 1 for i, s in enumerate(sorted(ss))}

        def val(tl, s):
            if tl.startswith("d"):
                return 16 * s
            return rank[tl][s]

        def replay(ename, eng):
            for waits, fn, done in self.streams[ename]:
                for tl, s in waits:
                    eng.wait_ge(sems[tl], val(tl, s))
                if fn is None:
                    continue
                inst = fn(eng)
                tl, s = done
                if tl.startswith("d"):
                    inst.then_inc(sems[tl], 16)
                elif tl in rank and s in rank[tl]:
                    inst.then_inc(sems[tl], 1)

        block = self.es.enter_context(nc.Block())

        @block.tensor
        def _(eng):
            replay("pe", eng)

        @block.scalar
        def _(eng):
            replay("act", eng)

        @block.vector
        def _(eng):
            replay("dve", eng)

        @block.gpsimd
        def _(eng):
            replay("pool", eng)

        @block.sync
        def _(eng):
            replay("sp", eng)

        self.es.close()
        return nc


class Ring:
    def __init__(self, tiles):
        self.tiles = tiles
        self.bufs = [Buf() for _ in tiles]
        self.i = 0

    def get(self):
        i = self.i
        self.i = (i + 1) % len(self.tiles)
        return self.tiles[i], self.bufs[i]


def build(n_layers=DEPTH, debug=False, stop_after=None):
    p = Prog()
    nc = p.nc
    okind = "ExternalOutput" if debug else "Internal"

    def din(name, shape):
        return p.dram(name, shape, F32, kind="ExternalInput").ap()

    hT0 = din("hT0", [D, NT])
    ccp = din("ccp", [128, 32])
    vec_d = din("vec", [DEPTH, 128, NV])
    bvrep_d = din("bvrep", [DEPTH, 128, 1024])
    bgt_d = din("bgt", [DEPTH, 4, 4])
    cst_d = din("cst", [128, NCST])
    dftl_d = din("dft_lat", [2, SEQ, SEQ])
    dftc_d = din("dft_ctx", [2, L_CTX, L_CTX])
    w_mod = din("w_mod", [DEPTH, D, 6 * D])
    w_in = din("w_in", [DEPTH, D, N_IN])
    w_pm = din("w_pm", [DEPTH, D_M, D])
    w_pf = din("w_pf", [DEPTH, D_F, D])
    w_pc = din("w_pc", [DEPTH, D_C, D])
    w_o = din("w_o", [DEPTH, D, D])
    w_up = din("w_up", [DEPTH, D, 2 * D_FF])
    w_down = din("w_down", [DEPTH, D_FF, D])
    out_d = p.dram("outT", [D, SEQ], F32, kind="ExternalOutput").ap()

    def scratch(name, shape, dt):
        return p.dram(name, shape, dt, kind=okind).ap()

    d_h = scratch("d_h", [16, 128, NT], F32)
    d_kT = scratch("d_kT", [8, 128, NT], BF16)
    d_qT = scratch("d_qT", [8, 128, NT], BF16)
    d_og = scratch("d_og", [8, 128, NT], BF16)
    d_ktok = scratch("d_ktok", [NCH, 128, 1024], BF16)
    d_v = scratch("d_v", [NCH, 128, 4 * 257], BF16)
    d_xf = scratch("d_xf", [4, 128, NT], BF16)
    d_yc = scratch("d_yc", [4, 128, NT], BF16)
    d_yf = scratch("d_yf", [4, 128, NT], BF16)
    d_g = scratch("d_g", [48, 128, NT], BF16)
    d_hs = scratch("d_hs", [NCH, 128, 1024], F32)
    d_hm = scratch("d_hm", [8, 128, NT], BF16)
    d_mg = scratch("d_mg", [16, 128, NT], BF16)
    d_hid = scratch("d_hid", [NFF, 128, NT], BF16)
    if debug:
        d_xn = scratch("d_xn", [16, 128, NT], BF16)
        d_rows = scratch("d_rows", [16, 4, NT], F32)
        d_mod = scratch("d_mod", [128, 192], F32)
    B_dh = [[Buf() for _ in TT] for _ in range(16)]

    cst = p.sb("cst", [128, NCST], F32)
    vec = p.sb("vec", [128, DEPTH, NV], F32)
    ones_bf = p.sb("ones_bf", [128, 128], BF16)
    id_bf = p.sb("id_bf", [128, 128], BF16)
    dft_cs = p.sb("dft_cs", [128, 256], BF16)
    dftc_sb = p.sb("dftc_sb", [128, 2, 2, 256], BF16)
    scc = p.sb("scc", [128, 16, 2], F32)
    modc = p.sb("modc", [128, 96, 2], F32)
    amod = p.sb("amod", [128, 2, 16, 2], F32)
    B_cst, B_vec, B_ones, B_idbf, B_dftcs, B_dftc, B_scc, B_modc, B_amod = (Buf() for _ in range(9))
    WR = Ring([p.sb("wr%d" % i, [128, 16, 512], BF16) for i in range(3)])
    PS = Ring([p.ps("ps%d" % i, [128, 512]) for i in range(7)])
    psT = p.ps("psT", [128, 8, 128], BF16)
    B_psT = Buf()

    def dma(q, out, in_, reads=(), writes=(), **kw):
        p.op(q, lambda e: e.dma_start(out=out, in_=in_, **kw), reads, writes, dma=True)

    def load_w(pieces):
        t, b = WR.get()
        for src, k0, c0 in pieces:
            nk = src.shape[0] // 128
            w = src.shape[1]
            dma("pool", t[:, k0:k0 + nk, c0:c0 + w], src.rearrange("(k p) w -> p k w", p=128), writes=[b])
        return t, b

    def mm(ps_ap, pairs, reads, psb):
        n = len(pairs)
        for i, (l, r) in enumerate(pairs):
            p.op("pe", lambda e, l=l, r=r, i=i: e.matmul(ps_ap, lhsT=l, rhs=r, start=(i == 0), stop=(i == n - 1)),
                 reads=reads, writes=[psb])

    def act(out, in_, func, reads, writes, bias=None, scale=None):
        kw = {}
        if bias is not None:
            kw["bias"] = bias
        if scale is not None:
            kw["scale"] = scale
        p.op("act", lambda e: e.activation(out=out, in_=in_, func=func, **kw), reads, writes)

    def dve(name, reads, writes, **kw):
        p.op("dve", lambda e: getattr(e, name)(**kw), reads, writes)

    dma("sp", cst[:], cst_d, writes=[B_cst])
    dma("sp", vec[:], vec_d.rearrange("l p v -> p l v"), writes=[B_vec])
    dma("sp", scc[:], ccp.rearrange("p (k c) -> p k c", c=2), writes=[B_scc])
    act(scc[:], scc[:], AF.Silu, [B_scc], [B_scc])
    maskbf = p.sb("maskbf", [128, 256], BF16)
    B_maskbf = Buf()
    dve("tensor_copy", [B_cst], [B_maskbf], out=maskbf[:], in_=cst[:, C_MF:C_MF + 256])
    scc_bf = p.sb("scc_bf", [128, 16, 2], BF16)
    B_sccbf = Buf()
    dve("tensor_copy", [B_scc], [B_sccbf], out=scc_bf[:], in_=scc[:])
    dve("memset", [], [B_ones], ap=ones_bf[:], constant=1.0)
    dve("tensor_copy", [B_cst], [B_idbf], out=id_bf[:], in_=cst[:, C_ID:C_ID + 128])
    dve("tensor_copy", [B_cst], [B_dftcs], out=dft_cs[:], in_=cst[:, C_DFT:C_DFT + 256])
    for j in range(2):
        dma("pool", dftc_sb[:, j, :, :], dftc_d[j].rearrange("(k p) w -> p k w", p=128), writes=[B_dftc])
    for k in range(16):
        for ti, (t0, t1) in enumerate(TT):
            dma("sp", d_h[k, :, t0:t1], hT0[k * 128:(k + 1) * 128, t0:t1], writes=[B_dh[k][ti]])

    ident = cst[:, C_ID:C_ID + 128]

    def phase_mod(l):
        if True:
            pm, pmb = PS.get()
            for grp in range(24):
                t, b = load_w([(w_mod[l, :, grp * 512:(grp + 1) * 512], 0, 0)])
                for j4 in range(4):
                    j = grp * 4 + j4
                    mm(pm[:, 2 * j:2 * j + 2], [(t[:, k, j4 * 128:(j4 + 1) * 128], scc_bf[:, k, :]) for k in range(16)],
                       [b, B_sccbf], pmb)
            dve("tensor_tensor", [pmb, B_vec], [B_modc], out=modc[:],
                in0=pm[:, 0:192].rearrange("p (j c) -> p j c", c=2),
                in1=vec[:, l, V_BMOD:V_BMOD + 96].unsqueeze(2).to_broadcast([128, 96, 2]), op=ALU.add)
            for w in range(2):
                sc = modc[:, 16 + 48 * w:32 + 48 * w, :]
                nw = vec[:, l, (V_N1 if w == 0 else V_N2):(V_N1 if w == 0 else V_N2) + 16]
                dve("scalar_tensor_tensor", [B_modc, B_vec], [B_amod], out=amod[:, w, :, :], in0=sc, scalar=1.0,
                    in1=nw.unsqueeze(2).to_broadcast([128, 16, 2]), op0=ALU.add, op1=ALU.mult)
            if debug:
                dma("sp", d_mod, modc[:].rearrange("p j c -> p (j c)"), reads=[B_modc])
            p.barrier()

    def phase_norm(l, w, xn, B_xn, final=False):
        fx = "F" if final else ""
        hA = [nc.sbuf_tensor("hA%s%d_%d_%d" % (fx, l, w, i), [128, 16, 256], F32) for i in range(3)]
        sqt = [nc.sbuf_tensor("sq%s%d_%d_%d" % (fx, l, w, i), [128, 16, 256], BF16) for i in range(2)]
        rst = [nc.sbuf_tensor("rs%s%d_%d_%d" % (fx, l, w, i), [128, 256], F32) for i in range(2)]
        with hA[0] as h0, hA[1] as h1, hA[2] as h2, sqt[0] as sq0, sqt[1] as sq1, rst[0] as rs0, rst[1] as rs1:
            ring = Ring([h0, h1, h2])
            sqr = Ring([sq0, sq1])
            rsr = Ring([rs0, rs1])
            n = 256
            subs = [s9 for s9 in range(NT // 256) if not (final and s9 == 0)]

            def stage1(s9):
                t0, t1 = 256 * s9, 256 * s9 + 256
                ti = 0 if s9 == 0 else 1 + (s9 - 1) // 2
                t, b = ring.get()
                sq, B_sq = sqr.get()
                rs, B_rs = rsr.get()
                dma("sp", t[:, :, :n], d_h[:, :, t0:t1].rearrange("k p t -> p k t"),
                    reads=[B_dh[k][ti] for k in range(16)], writes=[b])
                act(sq[:, :, :n], t[:, :, :n], AF.Square, [b], [B_sq])
                ps, psb = PS.get()
                mm(ps[:, :n], [(ones_bf[:, :], sq[:, k, :n]) for k in range(16)], [B_ones, B_sq], psb)
                act(rs[:, :n], ps[:, :n], AF.Ln, [psb], [B_rs], bias=EPS, scale=1.0 / D)
                act(rs[:, :n], rs[:, :n], AF.Exp, [B_rs], [B_rs], scale=-0.5)
                return (s9, t, b, rs, B_rs)

            def stage2(st):
                s9, t, b, rs, B_rs = st
                t0, t1 = 256 * s9, 256 * s9 + 256
                ti = 0 if s9 == 0 else 1 + (s9 - 1) // 2
                seg = 1 if ti == 0 else 0
                for k in range(16):
                    if final:
                        sc_ap = vec[:, 0, V_FN + k:V_FN + k + 1]
                    else:
                        sc_ap = amod[:, w, k, seg:seg + 1]
                    dve("scalar_tensor_tensor", [b, B_rs, B_amod, B_vec], [b], out=t[:, k, :n], in0=t[:, k, :n],
                        scalar=sc_ap, in1=rs[:, :n], op0=ALU.mult, op1=ALU.mult)
                    if not final:
                        act(xn[:, k, t0:t1], t[:, k, :n], AF.Identity, [b, B_modc], [B_xn[ti]],
                            bias=modc[:, 48 * w + k, seg:seg + 1])
                if final:
                    dma("sp", out_d.rearrange("(k p) t -> p k t", p=128)[:, :, t0 - L_CTX:t1 - L_CTX], t[:, :, :n], reads=[b])

            prev = None
            for s9 in subs:
                cur = stage1(s9)
                if prev is not None:
                    stage2(prev)
                prev = cur
            stage2(prev)
            if debug and not final and w == 0:
                dma("sp", d_xn.rearrange("k p t -> p k t"), xn[:], reads=B_xn)
            p.barrier()

    def conv3(O, U, wcols, segs, shift, B_O, B_U):
        dve("tensor_scalar", [B_U, B_vec], [B_O], out=O[:, :], in0=U[:, :], scalar1=wcols[1], scalar2=None, op0=ALU.mult)
        for (a, b_, rows) in segs:
            if rows is None:
                dve("scalar_tensor_tensor", [B_U, B_vec, B_O], [B_O], out=O[:, a + shift:b_], in0=U[:, a:b_ - shift],
                    scalar=wcols[0], in1=O[:, a + shift:b_], op0=ALU.mult, op1=ALU.add)
                dve("scalar_tensor_tensor", [B_U, B_vec, B_O], [B_O], out=O[:, a:b_ - shift], in0=U[:, a + shift:b_],
                    scalar=wcols[2], in1=O[:, a:b_ - shift], op0=ALU.mult, op1=ALU.add)
            else:
                Ov = O[:, a:b_].rearrange("p (r c) -> p r c", c=rows)
                Uv = U[:, a:b_].rearrange("p (r c) -> p r c", c=rows)
                dve("scalar_tensor_tensor", [B_U, B_vec, B_O], [B_O], out=Ov[:, :, 1:rows], in0=Uv[:, :, 0:rows - 1],
                    scalar=wcols[0], in1=Ov[:, :, 1:rows], op0=ALU.mult, op1=ALU.add)
                dve("scalar_tensor_tensor", [B_U, B_vec, B_O], [B_O], out=Ov[:, :, 0:rows - 1], in0=Uv[:, :, 1:rows],
                    scalar=wcols[2], in1=Ov[:, :, 0:rows - 1], op0=ALU.mult, op1=ALU.add)

    SEG_SEQ = [(0, L_CTX, None), (L_CTX, NT, None)]

    def phase_inproj(l, xn, B_xn):
        U = [nc.sbuf_tensor("U%d_%d" % (l, i), [128, NT], F32) for i in range(3)]
        R = [nc.sbuf_tensor("R%d_%d" % (l, i), [128, NT], BF16) for i in range(2)]
        Tt = nc.sbuf_tensor("Tt%d" % l, [128, NCH, 128], BF16)
        Vs = [nc.sbuf_tensor("Vs%d_%d" % (l, i), [128, 2, 257], BF16) for i in range(3)]
        bv = nc.sbuf_tensor("bv%d" % l, [128, 1024], F32)
        wg = nc.sbuf_tensor("wg%d" % l, [128, 16, 16], BF16)
        bg = nc.sbuf_tensor("bg%d" % l, [4, 4], F32)
        grow = nc.sbuf_tensor("grow%d" % l, [4, NT], F32)
        with U[0] as U0, U[1] as U1, U[2] as U2, R[0] as R0, R[1] as R1, Tt as Tt_, Vs[0] as V0, Vs[1] as V1, Vs[2] as V2, \
                bv as bv_, wg as wg_, bg as bg_, grow as grow_:
            Ur = Ring([U0, U1, U2])
            Rr = Ring([R0, R1])
            Vr = Ring([V0, V1, V2])
            B_Tt, B_bv, B_wg, B_bg, B_grow = Buf(), Buf(), Buf(), Buf(), Buf()
            dma("sp", bv_[:], bvrep_d[l], writes=[B_bv])
            dma("sp", bg_[:], bgt_d[l], writes=[B_bg])
            dma("pool", wg_[:], w_in[l, :, OFF_GATES:OFF_GATES + 16].rearrange("(k p) g -> p k g", p=128), writes=[B_wg])
            for vt, vb in zip(Vr.tiles, Vr.bufs):
                dve("memset", [], [vb], ap=vt[:, :, 256:257], constant=1.0)

            def proj_rows(wt, wb, c0, bias_idx, dst, dstb, func=AF.Identity, scale=None):
                for ti, (t0, t1) in enumerate(TT):
                    n = t1 - t0
                    ps, psb = PS.get()
                    mm(ps[:, :n], [(wt[:, k, c0:c0 + 128], xn[:, k, t0:t1]) for k in range(16)], [wb, B_xn[ti]], psb)
                    act(dst[:, t0:t1], ps[:, :n], func, [psb, B_vec], [dstb],
                        bias=vec[:, l, V_BIN + bias_idx:V_BIN + bias_idx + 1], scale=scale)

            for g in range(4):
                for ti, (t0, t1) in enumerate(TT):
                    n = t1 - t0
                    ps, psb = PS.get()
                    mm(ps[0:4, :n], [(wg_[:, k, 4 * g:4 * g + 4], xn[:, k, t0:t1]) for k in range(16)], [B_wg, B_xn[ti]], psb)
                    act(grow_[:, t0:t1], ps[0:4, :n], AF.Identity, [psb, B_bg], [B_grow], bias=bg_[:, g:g + 1])
                dma("sp", d_gr[g], grow_[:], reads=[B_grow])

            for kind, dT, cw, qs in (("k", d_kT, V_CK, None), ("q", d_qT, V_CQ, 1.0 / 16.0)):
                for half in range(2):
                    off0 = (0 if kind == "k" else OFF_Q) + 512 * half
                    wt, wb = load_w([(w_in[l, :, off0:off0 + 512], 0, 0)])
                    for i4 in range(4):
                        i = half * 4 + i4
                        Ut, Ub = Ur.get()
                        proj_rows(wt, wb, 128 * i4, FMIDX[(kind, i)], Ut, Ub)
                        Ot, Ob = Ur.get()
                        wc = [vec[:, l, cw + 8 * j + i:cw + 8 * j + i + 1] for j in range(3)]
                        conv3(Ot, Ut, wc, SEG_SEQ, 1, Ob, Ub)
                        Rt, Rb = Rr.get()
                        if qs is None:
                            act(Rt[:, :], Ot[:, :], AF.Silu, [Ob], [Rb])
                        else:
                            act(Ot[:, :], Ot[:, :], AF.Silu, [Ob], [Ob])
                            act(Rt[:, :], Ot[:, :], AF.Copy, [Ob], [Rb], scale=qs)
                        dma("sp", dT[i], Rt[:, :], reads=[Rb])
                        if kind == "k":
                            for c8 in range(0, NCH, 8):
                                nn = min(8, NCH - c8)
                                for c in range(c8, c8 + nn):
                                    p.op("pe", lambda e, c=c, c8=c8, Rt=Rt: e.transpose(out=psT[:, c - c8, :], in_=Rt[:, c * 128:(c + 1) * 128],
                                                                                         identity=id_bf[:, :]),
                                         reads=[Rb, B_idbf], writes=[B_psT])
                                dve("tensor_copy", [B_psT], [B_Tt], out=Tt_[:, c8:c8 + nn, :], in_=psT[:, 0:nn, :])
                            dma("sp", d_ktok[:, :, 128 * i:128 * (i + 1)].rearrange("c p f -> p c f"), Tt_[:, :, :], reads=[B_Tt])

            for half in range(2):
                wt, wb = load_w([(w_in[l, :, 1024 + 512 * half:1536 + 512 * half], 0, 0)])
                for c in range(NCH):
                    ti = 0 if c < 2 else 1 + (c - 2) // 4
                    ps, psb = PS.get()
                    mm(ps[:, :], [(xn[:, k, c * 128:(c + 1) * 128], wt[:, k, :]) for k in range(16)], [wb, B_xn[ti]], psb)
                    vt, vb = Vr.get()
                    dve("tensor_tensor", [psb, B_bv], [vb], out=vt[:, :, 0:256], in0=ps[:, :].rearrange("p (h d) -> p h d", d=256),
                        in1=bv_[:, 512 * half:512 * half + 512].rearrange("p (h d) -> p h d", d=256), op=ALU.add)
                    dma("sp", d_v[c].rearrange("p (h d) -> p h d", d=257)[:, 2 * half:2 * half + 2, :], vt[:, :, :], reads=[vb])

            for half in range(2):
                wt, wb = load_w([(w_in[l, :, OFF_O + 512 * half:OFF_O + 512 * half + 512], 0, 0)])
                for i4 in range(4):
                    i = half * 4 + i4
                    Rt, Rb = Rr.get()
                    proj_rows(wt, wb, 128 * i4, FMIDX[("o", i)], Rt, Rb, func=AF.Sigmoid)
                    dma("sp", d_og[i], Rt[:, :], reads=[Rb])
            wt, wb = load_w([(w_in[l, :, OFF_F:OFF_F + 512], 0, 0)])
            for i in range(4):
                Rt, Rb = Rr.get()
                proj_rows(wt, wb, 128 * i, FMIDX[("xf", i)], Rt, Rb)
                dma("sp", d_xf[i], Rt[:, :], reads=[Rb])
            for i in range(4):
                wt, wb = load_w([(w_in[l, :, OFF_C + 512 * j + 128 * i:OFF_C + 512 * j + 128 * i + 128], 0, 128 * j) for j in range(3)])
                Ucc, Bcc = Ur.get()
                proj_rows(wt, wb, 128, FMIDX[("cc", i)], Ucc, Bcc)
                Ucx, Bcx = Ur.get()
                proj_rows(wt, wb, 256, FMIDX[("cx", i)], Ucx, Bcx)
                dve("tensor_tensor", [Bcc, Bcx], [Bcx], out=Ucx[:, :], in0=Ucx[:, :], in1=Ucc[:, :], op=ALU.mult)
                wc = [vec[:, l, V_CC + 4 * j + i:V_CC + 4 * j + i + 1] for j in range(3)]
                conv3(Ucc, Ucx, wc, [(0, L_CTX, None), (L_CTX, NT, 64)], 1, Bcc, Bcx)
                Ucb, Bcb = Ur.get()
                proj_rows(wt, wb, 0, FMIDX[("cb", i)], Ucb, Bcb)
                Rt, Rb = Rr.get()
                dve("tensor_tensor", [Bcc, Bcb], [Rb], out=Rt[:, :], in0=Ucc[:, :], in1=Ucb[:, :], op=ALU.mult)
                dma("sp", d_yc[i], Rt[:, :], reads=[Rb])
            for grp in range(12):
                wt, wb = load_w([(w_in[l, :, OFF_G + 512 * grp:OFF_G + 512 * grp + 512], 0, 0)])
                for i4 in range(4):
                    i = grp * 4 + i4
                    Rt, Rb = Rr.get()
                    proj_rows(wt, wb, 128 * i4, FMIDX[("gm", i)], Rt, Rb, func=AF.Sigmoid)
                    dma("sp", d_g[i], Rt[:, :], reads=[Rb])
            p.barrier()

    d_gr = scratch("d_gr", [4, 4, NT], F32)

    def phase_mlstm(l):
        es = contextlib.ExitStack()
        es_pre = contextlib.ExitStack()
        T = {}
        Bn = {}

        def alloc(stack, nm, shp, dt):
            T[nm] = stack.enter_context(nc.sbuf_tensor("%s_L%d" % (nm, l), shp, dt))
            Bn[nm] = Buf(nm)
        for d_ in range(2):
            for nm in ("Xa", "Xm"):
                alloc(es, nm + str(d_), [4, NT], F32)
        alloc(es, "colsb", [128, 432], F32)
        alloc(es, "keepb", [128, 144], F32)
        for nm in ("X1", "X2", "X3", "X4"):
            alloc(es_pre, nm, [4, NT], F32)
        alloc(es_pre, "mrow", [4, 2, 20], F32)
        alloc(es_pre, "mlast", [4, 2, 18], F32)
        alloc(es_pre, "klog", [4, 2, 18], F32)
        if True:
            colsb, keepb, mrow, mlast, klog = T["colsb"], T["keepb"], T["mrow"], T["mlast"], T["klog"]
            X1, X2, X3, X4 = T["X1"], T["X2"], T["X3"], T["X4"]
            sel4 = cst[0:4, C_SEL:C_SEL + 512].rearrange("p (h s) -> p h s", s=128)
            oh4 = cst[0:4, C_OH:C_OH + 512].rearrange("p (h s) -> p h s", s=128)
            id4 = cst[0:4, C_I4:C_I4 + 4]
            pcol, pcolb = PS.get()
            pkeep, pkeepb = PS.get()
            dve("memset", [], [Bn["mrow"]], ap=mrow[:], constant=0.0)
            orders = [list(range(NCH)), [1, 0] + list(range(NCH - 1, 1, -1))]
            for d_ in range(2):
                Xa, Xm = T["Xa%d" % d_], T["Xm%d" % d_]
                Ba, Bm = Bn["Xa%d" % d_], Bn["Xm%d" % d_]
                B1, B2, B3, B4 = Bn["X1"], Bn["X2"], Bn["X3"], Bn["X4"]
                dma("sp", Xa[:], d_gr[2 * d_], writes=[Ba])
                dma("sp", X1[:], d_gr[2 * d_ + 1], writes=[B1])
                act(X1[:], X1[:], AF.Exp, [B1], [B1], scale=-1.0)
                act(X1[:], X1[:], AF.Ln, [B1], [B1], bias=1.0)
                dve("tensor_scalar", [B1], [B1], out=X1[:], in0=X1[:], scalar1=-1.0, scalar2=None, op0=ALU.mult)
                dve("memset", [], [B3], ap=X3[:], constant=1.0)

                def sl(c):
                    if d_ == 0:
                        return slice(c * 128, (c + 1) * 128)
                    return slice((c + 1) * 128 - 1, (c * 128 - 1) if c > 0 else None, -1)

                def last(c):
                    t = (c + 1) * 128 - 1 if d_ == 0 else c * 128
                    return slice(t, t + 1)
                for c in range(NCH):
                    dve("tensor_tensor_scan", [B1, B3], [B2], out=X2[:, sl(c)], data0=X3[:, sl(c)], data1=X1[:, sl(c)],
                        initial=0.0, op0=ALU.mult, op1=ALU.add)
                dve("tensor_tensor", [Ba, B2], [Ba], out=Xa[:], in0=Xa[:], in1=X2[:], op=ALU.subtract)
                for c in range(NCH):
                    dve("tensor_tensor_scan", [Ba], [Bm], out=Xm[:, sl(c)], data0=Xa[:, sl(c)], data1=Xa[:, sl(c)],
                        initial=-1e30, op0=ALU.max, op1=ALU.max)
                for j, c in enumerate(orders[d_]):
                    cs = slice(c * 128, (c + 1) * 128)
                    mp = mrow[:, d_, j:j + 1]
                    dve("tensor_scalar", [Bm, Bn["mrow"]], [Bm], out=Xm[:, cs], in0=Xm[:, cs], scalar1=mp, scalar2=None, op0=ALU.max)
                    dve("tensor_tensor", [Bm, B2], [Bn["mrow"]], out=mrow[:, d_, j + 1:j + 2], in0=Xm[:, last(c)], in1=X2[:, last(c)], op=ALU.add)
                    dve("tensor_tensor", [Bm, Bn["mrow"]], [Bn["klog"]], out=klog[:, d_, c:c + 1], in0=mp, in1=Xm[:, last(c)], op=ALU.subtract)
                    dve("tensor_scalar", [Bm], [Bn["mlast"]], out=mlast[:, d_, c:c + 1], in0=Xm[:, last(c)], scalar1=-1.0, scalar2=None, op0=ALU.mult)
                    act(X4[:, cs], Xm[:, cs], AF.Exp, [Bm, Bn["mrow"]], [B4], bias=mp, scale=-1.0)
                    act(X3[:, cs], Xa[:, cs], AF.Exp, [Ba, Bn["mlast"]], [B3], bias=mlast[:, d_, c:c + 1])
                dve("tensor_tensor", [B2, Bm], [B2], out=X2[:], in0=X2[:], in1=Xm[:], op=ALU.add)
                act(X2[:], X2[:], AF.Exp, [B2], [B2], scale=-1.0)
                act(klog[:, d_, :], klog[:, d_, :], AF.Exp, [Bn["klog"]], [Bn["klog"]])
                dve("tensor_scalar", [Bm], [Bm], out=Xm[:], in0=Xm[:], scalar1=-1.0, scalar2=None, op0=ALU.mult)
                if debug:
                    for qi, (xt, xb) in enumerate(((Xa, Ba), (Xm, Bm), (X4, B4), (X2, B2), (X3, B3))):
                        dma("sp", d_rows[d_ * 5 + qi], xt[:], reads=[xb])
                for qi, (xt, xb) in enumerate(((X4, B4), (X2, B2), (X3, B3))):
                    for c in range(NCH):
                        o = ((qi * 2 + d_) * NCH + c) * 4
                        p.op("pe", lambda e, xt=xt, c=c, o=o: e.matmul(pcol[:, o:o + 4], lhsT=xt[:, c * 128:(c + 1) * 128], rhs=id4,
                                                                       start=True, stop=True), reads=[xb, B_cst], writes=[pcolb])
                for h in range(4):
                    o = (d_ * 4 + h) * NCH
                    p.op("pe", lambda e, h=h, o=o, d_=d_: e.matmul(pkeep[:, o:o + NCH], lhsT=sel4[:, h, :], rhs=klog[:, d_, :], start=True, stop=True),
                         reads=[Bn["klog"], B_cst], writes=[pkeepb])
            act(colsb[:], pcol[:, 0:432], AF.Copy, [pcolb], [Bn["colsb"]])
            act(keepb[:], pkeep[:, 0:144], AF.Copy, [pkeepb], [Bn["keepb"]])

            def col(qi, d_, c, h):
                o = ((qi * 2 + d_) * NCH + c) * 4 + h
                return colsb[:, o:o + 1]

            p.barrier()
            es_pre.close()
            per = [("qTc", [128, 8, 128], BF16, 2), ("kTc", [128, 8, 128], BF16, 2), ("ktk", [128, 1024], BF16, 2), ("vc", [128, 4, 257], BF16, 2),
                   ("ogc", [128, 8, 128], BF16, 3), ("hsf", [128, 1024], F32, 3), ("hsb", [128, 1024], F32, 2), ("Dt", [128, 512], F32, 2),
                   ("SdT", [128, 512], BF16, 2), ("t1", [128, 257], F32, 8), ("nd", [128, 257], F32, 8), ("kw", [128, 256], BF16, 8),
                   ("hn", [128, 1024], BF16, 2), ("hmc", [128, 8, 128], BF16, 2), ("sm", [128, 8], F32, 6), ("am4", [4, 4, 128], F32, 2)]
            rg = {}
            for nm, shp, dt, dep in per:
                tl = []
                for i in range(dep):
                    alloc(es, "%s%d" % (nm, i), shp, dt)
                    tl.append(T["%s%d" % (nm, i)])
                rg[nm] = Ring(tl)
            alloc(es, "Cst", [128, 4, 2, 257], F32)
            alloc(es, "Cbf", [128, 4, 2, 257], BF16)
            Cst, Cbf = T["Cst"], T["Cbf"]
            BCs = [[Buf(), Buf()] for _ in range(4)]
            BCb = [[Buf(), Buf()] for _ in range(4)]
            B_dhs = [Buf() for _ in range(NCH)]
            PSl = PS
            for d_ in range(2):
                Xa, Xm = T["Xa%d" % d_], T["Xm%d" % d_]
                Ba, Bm = Bn["Xa%d" % d_], Bn["Xm%d" % d_]
                mask = cst[:, (C_MF if d_ == 0 else C_MB):(C_MF if d_ == 0 else C_MB) + 128]
                dve("memset", [], [x for y in BCs for x in y], ap=Cst[:], constant=0.0)
                dve("memset", [], [x for y in BCb for x in y], ap=Cbf[:], constant=0.0)
                def issue_loads(c_):
                    cs_ = slice(c_ * 128, (c_ + 1) * 128)
                    L = {}
                    L["q"] = rg["qTc"].get()
                    L["k"] = rg["kTc"].get()
                    L["kt"] = rg["ktk"].get()
                    L["v"] = rg["vc"].get()
                    dma("sp", L["q"][0][:], d_qT[:, :, cs_].rearrange("f p t -> p f t"), writes=[L["q"][1]])
                    dma("sp", L["k"][0][:], d_kT[:, :, cs_].rearrange("f p t -> p f t"), writes=[L["k"][1]])
                    dma("sp", L["kt"][0][:], d_ktok[c_], writes=[L["kt"][1]])
                    dma("sp", L["v"][0][:], d_v[c_].rearrange("p (h d) -> p h d", d=257), writes=[L["v"][1]])
                    if d_ == 1:
                        L["hsf"] = rg["hsf"].get()
                        dma("sp", L["hsf"][0][:], d_hs[c_], reads=[B_dhs[c_]], writes=[L["hsf"][1]])
                        L["og"] = rg["ogc"].get()
                        dma("sp", L["og"][0][:], d_og[:, :, cs_].rearrange("f p t -> p f t"), writes=[L["og"][1]])
                    return L
                def front(j):
                    c = orders[d_][j]
                    cs = slice(c * 128, (c + 1) * 128)
                    L = issue_loads(c)
                    qTc, Bq = L["q"]
                    kTc, Bk = L["k"]
                    ktk, Bkt = L["kt"]
                    am4, Bam = rg["am4"].get()
                    dve("tensor_tensor", [Ba, B_cst], [Bam], out=am4[:], in0=Xa[:, cs].unsqueeze(1).to_broadcast([4, 4, 128]), in1=oh4, op=ALU.mult)
                    pSa, pSb_ = PSl.get()
                    for h in range(4):
                        mm(pSa[:, 128 * h:128 * h + 128], [(kTc[:, 2 * h + kc, :], qTc[:, 2 * h + kc, :]) for kc in range(2)], [Bk, Bq], pSb_)
                    pEa, pEb_ = PSl.get()
                    for h in range(4):
                        mm(pEa[:, 128 * h:128 * h + 128], [(am4[:, h, :], oh_ones), (sel4[:, h, :], Xm[:, cs]), (id_bf[:, :], mask_bf)],
                           [Bam, Bm, B_cst, B_idbf, B_maskbf], pEb_)
                    Dta, BDa = rg["Dt"].get()
                    act(Dta[:], pEa[:, :], AF.Exp, [pEb_], [BDa])
                    SdTa, BSa = rg["SdT"].get()
                    dve("tensor_tensor", [pSb_, BDa], [BSa], out=SdTa[:], in0=pSa[:, :], in1=Dta[:], op=ALU.mult)
                    kw = {}
                    for h in range(4):
                        kw[h] = rg["kw"].get()
                        dve("tensor_scalar", [Bkt, Bn["colsb"]], [kw[h][1]], out=kw[h][0][:], in0=ktk[:, 256 * h:256 * h + 256],
                            scalar1=col(2, d_, c, h), scalar2=None, op0=ALU.mult)
                    L["SdT"] = (SdTa, BSa)
                    L["kw"] = kw
                    return L

                def midback(j, L):
                    c = orders[d_][j]
                    cs = slice(c * 128, (c + 1) * 128)
                    qTc, Bq = L["q"]
                    vc, Bv = L["v"]
                    SdTa, BSa = L["SdT"]
                    kw = L["kw"]
                    H = range(4)
                    pN, pI, t1, nd = {}, {}, {}, {}
                    for h in H:
                        pN[h] = PSl.get()
                        mm(pN[h][0][:, 0:257], [(SdTa[:, 128 * h:128 * h + 128], vc[:, h, :])], [BSa, Bv], pN[h][1])
                        pI[h] = PSl.get()
                        mm(pI[h][0][:, 0:257], [(qTc[:, 2 * h + kc, :], Cbf[:, h, kc, :]) for kc in range(2)], [Bq, BCb[h][0], BCb[h][1]], pI[h][1])
                        t1[h] = rg["t1"].get()
                        act(t1[h][0][:], pI[h][0][:, 0:257], AF.Identity, [pI[h][1], Bn["colsb"]], [t1[h][1]], scale=col(0, d_, c, h))
                        nd[h] = rg["nd"].get()
                        dve("tensor_tensor", [pN[h][1], t1[h][1]], [nd[h][1]], out=nd[h][0][:], in0=pN[h][0][:, 0:257], in1=t1[h][0][:], op=ALU.add)
                    for h in H:
                        for kc in range(2):
                            pC, pCb = PSl.get()
                            mm(pC[:, 0:257], [(kw[h][0][:, 128 * kc:128 * kc + 128], vc[:, h, :])], [kw[h][1], Bv], pCb)
                            ko = (d_ * 4 + h) * NCH + c
                            dve("scalar_tensor_tensor", [BCs[h][kc], Bn["keepb"], pCb], [BCs[h][kc]], out=Cst[:, h, kc, :], in0=Cst[:, h, kc, :],
                                scalar=keepb[:, ko:ko + 1], in1=pC[:, 0:257], op0=ALU.mult, op1=ALU.add)
                            act(Cbf[:, h, kc, :], Cst[:, h, kc, :], AF.Copy, [BCs[h][kc]], [BCb[h][kc]])
                    return (j, L, nd)

                def tail(ctx_):
                    j, L, nd = ctx_
                    c = orders[d_][j]
                    cs = slice(c * 128, (c + 1) * 128)
                    H = range(4)
                    sm, Bsm = rg["sm"].get()
                    hs, Bhs = rg["hsf" if d_ == 0 else "hsb"].get()
                    for h in H:
                        act(sm[:, h:h + 1], nd[h][0][:, 256:257], AF.Abs, [nd[h][1]], [Bsm])
                    for h in H:
                        dve("tensor_scalar", [Bsm, Bn["colsb"]], [Bsm], out=sm[:, h:h + 1], in0=sm[:, h:h + 1],
                            scalar1=col(1, d_, c, h), scalar2=None, op0=ALU.max)
                    dve("reciprocal", [Bsm], [Bsm], out=sm[:, 0:4], in_=sm[:, 0:4])
                    if d_ == 0:
                        for h in H:
                            dve("tensor_scalar", [nd[h][1], Bsm], [Bhs], out=hs[:, 256 * h:256 * h + 256], in0=nd[h][0][:, 0:256],
                                scalar1=sm[:, h:h + 1], scalar2=None, op0=ALU.mult)
                        dma("sp", d_hs[c], hs[:], reads=[Bhs], writes=[B_dhs[c]])
                    else:
                        hsf, Bhsf = L["hsf"]
                        ogc, Bog = L["og"]
                        for h in H:
                            dve("scalar_tensor_tensor", [nd[h][1], Bsm, Bhsf], [Bhs], out=hs[:, 256 * h:256 * h + 256], in0=nd[h][0][:, 0:256],
                                scalar=sm[:, h:h + 1], in1=hsf[:, 256 * h:256 * h + 256], op0=ALU.mult, op1=ALU.add)
                        sm2, Bsm2 = rg["sm"].get()
                        hn, Bhn = rg["hn"].get()
                        for h in range(4):
                            act(hn[:, 256 * h:256 * h + 256], hs[:, 256 * h:256 * h + 256], AF.Square, [Bhs], [Bhn])
                        dve("tensor_reduce", [Bhn], [Bsm2], out=sm2[:, 4:8], in_=hn[:, :].rearrange("p (h d) -> p h d", d=256),
                            axis=mybir.AxisListType.X, op=ALU.add)
                        act(sm2[:, 4:8], sm2[:, 4:8], AF.Ln, [Bsm2], [Bsm2], bias=EPS, scale=1.0 / 256)
                        act(sm2[:, 4:8], sm2[:, 4:8], AF.Exp, [Bsm2], [Bsm2], scale=-0.5)
                        for h in range(4):
                            dve("tensor_scalar", [Bhs, Bsm2], [Bhn], out=hn[:, 256 * h:256 * h + 256], in0=hs[:, 256 * h:256 * h + 256],
                                scalar1=sm2[:, 4 + h:5 + h], scalar2=None, op0=ALU.mult)
                        hmc, Bhm = rg["hmc"].get()
                        for f in range(8):
                            p.op("pe", lambda e, f=f, hn=hn: e.transpose(out=psT[:, f, :], in_=hn[:, 128 * f:128 * f + 128], identity=id_bf[:, :]),
                                 reads=[Bhn, B_idbf], writes=[B_psT])
                        for f in range(8):
                            dve("scalar_tensor_tensor", [B_psT, B_vec, Bog], [Bhm], out=hmc[:, f, :], in0=psT[:, f, :],
                                scalar=vec[:, l, V_MN + f:V_MN + f + 1], in1=ogc[:, f, :], op0=ALU.mult, op1=ALU.mult)
                        dma("sp", d_hm[:, :, cs].rearrange("f p t -> p f t"), hmc[:], reads=[Bhm])

                mask_bf = maskbf[:, 128 * d_:128 * d_ + 128]
                cur = front(0)
                pend = None
                for j in range(NCH):
                    nxtL = front(j + 1) if j + 1 < NCH else None
                    ctx_ = midback(j, cur)
                    if pend is not None:
                        tail(pend)
                    pend = ctx_
                    cur = nxtL
                tail(pend)
            p.barrier()
            es.close()

    oh_ones = cst[0:4, C_SEL + 0:C_SEL + 128]

    def phase_fourier(l):
        xf = nc.sbuf_tensor("xfa%d" % l, [128, 4, NT], BF16)
        pq = nc.sbuf_tensor("pq%d" % l, [128, NCH, 4, 256], BF16)
        yr = [nc.sbuf_tensor("yr%d_%d" % (l, i), [128, 512], BF16) for i in range(2)]
        with xf as xf_, pq as pq_, yr[0] as y0, yr[1] as y1:
            Yr = Ring([y0, y1])
            B_xf, B_pq = Buf(), [Buf() for _ in range(NCH)]
            dma("sp", xf_[:], d_xf.rearrange("g p t -> p g t"), writes=[B_xf])
            for c in range(NCH):
                for g2 in range(2):
                    ps, psb = PS.get()
                    for gg in range(2):
                        g = 2 * g2 + gg
                        mm(ps[:, 256 * gg:256 * gg + 256], [(xf_[:, g, c * 128:(c + 1) * 128], dft_cs[:, :])], [B_xf, B_dftcs], psb)
                    act(pq_[:, c, 2 * g2:2 * g2 + 2, :], ps[:, :].rearrange("p (g w) -> p g w", w=256), AF.Copy, [psb], [B_pq[c]])
            for g in range(4):
                ps, psb = PS.get()
                pairs = []
                for tc in range(2):
                    pairs.append((pq_[:, tc, g, 0:128], dftc_sb[:, 0, tc, :]))
                    pairs.append((pq_[:, tc, g, 128:256], dftc_sb[:, 1, tc, :]))
                mm(ps[:, 0:256], pairs, [B_pq[0], B_pq[1], B_dftc], psb)
                yt, yb = Yr.get()
                act(yt[:, 0:256], ps[:, 0:256], AF.Copy, [psb], [yb])
                dma("sp", d_yf[g, :, 0:256], yt[:, 0:256], reads=[yb])
            for j in range(4):
                wc, wcb = load_w([(dftl_d[0, :, 512 * j:512 * j + 512], 0, 0)])
                ws, wsb = load_w([(dftl_d[1, :, 512 * j:512 * j + 512], 0, 0)])
                for g in range(4):
                    ps, psb = PS.get()
                    pairs = []
                    for tc in range(16):
                        pairs.append((pq_[:, 2 + tc, g, 0:128], wc[:, tc, :]))
                        pairs.append((pq_[:, 2 + tc, g, 128:256], ws[:, tc, :]))
                    mm(ps[:, :], pairs, B_pq[2:] + [wcb, wsb], psb)
                    yt, yb = Yr.get()
                    act(yt[:, :], ps[:, :], AF.Copy, [psb], [yb])
                    dma("sp", d_yf[g, :, L_CTX + 512 * j:L_CTX + 512 * j + 512], yt[:, :], reads=[yb])
            p.barrier()

    def phase_merge(l):
        br = nc.sbuf_tensor("br%d" % l, [128, 16, NT], BF16)
        gr = [nc.sbuf_tensor("gr%d_%d" % (l, i), [128, 3, NT], BF16) for i in range(2)]
        mr = [nc.sbuf_tensor("mr%d_%d" % (l, i), [128, NT], BF16) for i in range(2)]
        tm = [nc.sbuf_tensor("tm%d_%d" % (l, i), [128, 512], F32) for i in range(4)]
        with br as br_, gr[0] as g0, gr[1] as g1, mr[0] as m0, mr[1] as m1, tm[0] as a0, tm[1] as a1, tm[2] as a2, tm[3] as a3:
            Gr, Mr, Tm = Ring([g0, g1]), Ring([m0, m1]), Ring([a0, a1, a2, a3])
            B_br = Buf()
            dma("sp", br_[:, 0:8, :], d_hm.rearrange("f p t -> p f t"), writes=[B_br])
            dma("sp", br_[:, 8:12, :], d_yf.rearrange("f p t -> p f t"), writes=[B_br])
            dma("sp", br_[:, 12:16, :], d_yc.rearrange("f p t -> p f t"), writes=[B_br])
            for dg in range(4):
                wt, wb = load_w([(w_pm[l, :, 512 * dg:512 * dg + 512], 0, 0), (w_pf[l, :, 512 * dg:512 * dg + 512], 8, 0),
                                 (w_pc[l, :, 512 * dg:512 * dg + 512], 12, 0)])
                for d4 in range(4):
                    dch = dg * 4 + d4
                    gt, gb = Gr.get()
                    for bi in range(3):
                        dma("sp", gt[:, bi, :], d_g[16 * bi + dch], writes=[gb])
                    mt, mb = Mr.get()
                    for ti, (t0, t1) in enumerate(TT):
                        n = t1 - t0
                        acc = None
                        for bi, (k0, k1) in enumerate(((0, 8), (8, 12), (12, 16))):
                            ps, psb = PS.get()
                            mm(ps[:, :n], [(wt[:, k, 128 * d4:128 * d4 + 128], br_[:, k, t0:t1]) for k in range(k0, k1)], [wb, B_br], psb)
                            tt, tb = Tm.get()
                            dve("tensor_tensor", [psb, gb], [tb], out=tt[:, :n], in0=ps[:, :n], in1=gt[:, bi, t0:t1], op=ALU.mult)
                            if acc is not None:
                                at, ab = acc
                                if bi == 2:
                                    dve("tensor_tensor", [tb, ab], [mb], out=mt[:, t0:t1], in0=tt[:, :n], in1=at[:, :n], op=ALU.add)
                                else:
                                    dve("tensor_tensor", [tb, ab], [tb], out=tt[:, :n], in0=tt[:, :n], in1=at[:, :n], op=ALU.add)
                            acc = (tt, tb)
                    dma("sp", d_mg[dch], mt[:, :], reads=[mb])
            p.barrier()

    def phase_resid_proj(l, src_d, wmat, gate_off, nm):
        xa = nc.sbuf_tensor("xa%s%d" % (nm, l), [128, 16, NT], BF16)
        hr = [nc.sbuf_tensor("hr%s%d_%d" % (nm, l, i), [128, NT], F32) for i in range(2)]
        with xa as xa_, hr[0] as h0, hr[1] as h1:
            Hr = Ring([h0, h1])
            B_xa = [Buf() for _ in TT]
            for ti, (t0, t1) in enumerate(TT):
                dma("sp", xa_[:, :, t0:t1], src_d[:, :, t0:t1].rearrange("f p t -> p f t"), writes=[B_xa[ti]])
            for dg in range(4):
                wt, wb = load_w([(wmat[l, :, 512 * dg:512 * dg + 512], 0, 0)])
                for d4 in range(4):
                    dch = dg * 4 + d4
                    ht, hb = Hr.get()
                    dma("sp", ht[:, :], d_h[dch], reads=B_dh[dch], writes=[hb])
                    for ti, (t0, t1) in enumerate(TT):
                        n = t1 - t0
                        seg = 1 if ti == 0 else 0
                        ps, psb = PS.get()
                        mm(ps[:, :n], [(wt[:, k, 128 * d4:128 * d4 + 128], xa_[:, k, t0:t1]) for k in range(16)], [wb, B_xa[ti]], psb)
                        dve("scalar_tensor_tensor", [psb, B_modc, hb], [hb], out=ht[:, t0:t1], in0=ps[:, :n],
                            scalar=modc[:, gate_off + dch, seg:seg + 1], in1=ht[:, t0:t1], op0=ALU.mult, op1=ALU.add)
                    dma("sp", d_h[dch], ht[:, :], reads=[hb], writes=B_dh[dch])
            p.barrier()

    def phase_ffn_up(l, xn, B_xn):
        U = [nc.sbuf_tensor("Uf%d_%d" % (l, i), [128, NT], F32) for i in range(4)]
        G = [nc.sbuf_tensor("Gf%d_%d" % (l, i), [128, NT], BF16) for i in range(2)]
        R = [nc.sbuf_tensor("Rf%d_%d" % (l, i), [128, NT], BF16) for i in range(2)]
        with U[0] as U0, U[1] as U1, U[2] as U2, U[3] as U3, G[0] as G0, G[1] as G1, R[0] as R0, R[1] as R1:
            Ur, Gr, Rr = Ring([U0, U1, U2, U3]), Ring([G0, G1]), Ring([R0, R1])
            for j4 in range(NFF // 4):
                wt, wb = load_w([(w_up[l, :, 512 * j4:512 * j4 + 512], 0, 0)])
                wt2, wb2 = load_w([(w_up[l, :, D_FF + 512 * j4:D_FF + 512 * j4 + 512], 0, 0)])
                for jj in range(4):
                    j = 4 * j4 + jj
                    Ut, Ub = Ur.get()
                    Gt, Gb = Gr.get()
                    for ti, (t0, t1) in enumerate(TT):
                        n = t1 - t0
                        ps, psb = PS.get()
                        mm(ps[:, :n], [(wt[:, k, 128 * jj:128 * jj + 128], xn[:, k, t0:t1]) for k in range(16)], [wb, B_xn[ti]], psb)
                        act(Ut[:, t0:t1], ps[:, :n], AF.Copy, [psb], [Ub])
                        ps2, psb2 = PS.get()
                        mm(ps2[:, :n], [(wt2[:, k, 128 * jj:128 * jj + 128], xn[:, k, t0:t1]) for k in range(16)], [wb2, B_xn[ti]], psb2)
                        act(Gt[:, t0:t1], ps2[:, :n], AF.Copy, [psb2], [Gb])
                    Ot, Ob = Ur.get()
                    wc = [vec[:, l, V_CFF + NFF * q + j:V_CFF + NFF * q + j + 1] for q in range(3)]
                    dve("tensor_scalar", [Ub, B_vec], [Ob], out=Ot[:, :], in0=Ut[:, :], scalar1=wc[1], scalar2=None, op0=ALU.mult)
                    for (a, b_, sh) in ((0, L_CTX, 1), (L_CTX, NT, 64)):
                        dve("scalar_tensor_tensor", [Ub, B_vec, Ob], [Ob], out=Ot[:, a + sh:b_], in0=Ut[:, a:b_ - sh],
                            scalar=wc[0], in1=Ot[:, a + sh:b_], op0=ALU.mult, op1=ALU.add)
                        dve("scalar_tensor_tensor", [Ub, B_vec, Ob], [Ob], out=Ot[:, a:b_ - sh], in0=Ut[:, a + sh:b_],
                            scalar=wc[2], in1=Ot[:, a:b_ - sh], op0=ALU.mult, op1=ALU.add)
                    act(Ot[:, :], Ot[:, :], AF.Silu, [Ob], [Ob])
                    Rt, Rb = Rr.get()
                    dve("tensor_tensor", [Ob, Gb], [Rb], out=Rt[:, :], in0=Ot[:, :], in1=Gt[:, :], op=ALU.mult)
                    dma("sp", d_hid[j], Rt[:, :], reads=[Rb])
            p.barrier()

    def phase_ffn_down(l):
        hd = [nc.sbuf_tensor("hd%d_%d" % (l, i), [128, NFF, 512], BF16) for i in range(2)]
        hr = [nc.sbuf_tensor("hq%d_%d" % (l, i), [128, 512], F32) for i in range(4)]
        with hd[0] as hd0, hd[1] as hd1, hr[0] as q0, hr[1] as q1, hr[2] as q2, hr[3] as q3:
            Hq = Ring([q0, q1, q2, q3])
            hds = [hd0, hd1]
            B_hd = [Buf(), Buf()]
            for pr in ((0, 1), (2, 3), (4,)):
                for i, ti in enumerate(pr):
                    t0, t1 = TT[ti]
                    dma("sp", hds[i][:, :, :t1 - t0], d_hid[:, :, t0:t1].rearrange("f p t -> p f t"), writes=[B_hd[i]])
                for dch in range(16):
                    wt, wb = WR.get()
                    wv = wt[:].rearrange("p k c -> p (k c)")[:, 0:NFF * 128].rearrange("p (k c) -> p k c", c=128)
                    dma("pool", wv, w_down[l, :, 128 * dch:128 * dch + 128].rearrange("(k p) w -> p k w", p=128), writes=[wb])
                    for i, ti in enumerate(pr):
                        t0, t1 = TT[ti]
                        n = t1 - t0
                        seg = 1 if ti == 0 else 0
                        ht, hb = Hq.get()
                        dma("sp", ht[:, :n], d_h[dch, :, t0:t1], reads=[B_dh[dch][ti]], writes=[hb])
                        ps, psb = PS.get()
                        mm(ps[:, :n], [(wv[:, k, :], hds[i][:, k, :n]) for k in range(NFF)], [wb, B_hd[i]], psb)
                        dve("scalar_tensor_tensor", [psb, B_modc, hb], [hb], out=ht[:, :n], in0=ps[:, :n],
                            scalar=modc[:, 80 + dch, seg:seg + 1], in1=ht[:, :n], op0=ALU.mult, op1=ALU.add)
                        dma("sp", d_h[dch, :, t0:t1], ht[:, :n], reads=[hb], writes=[B_dh[dch][ti]])
            p.barrier()

    oh_ones = cst[0:4, NCST - 128:NCST]
    stages = []
    for l in range(n_layers):
        phase_mod(l)
        xn_c = nc.sbuf_tensor("xn%d" % l, [128, 16, NT], BF16)
        with xn_c as xn:
            B_xn = [Buf() for _ in TT]
            phase_norm(l, 0, xn, B_xn)
            if stop_after == "norm":
                break
            phase_inproj(l, xn, B_xn)
        if stop_after == "inproj":
            break
        phase_mlstm(l)
        if stop_after == "mlstm":
            break
        phase_fourier(l)
        phase_merge(l)
        phase_resid_proj(l, d_mg, w_o, 32, "o")
        if stop_after == "mixer":
            break
        xn_c = nc.sbuf_tensor("xm%d" % l, [128, 16, NT], BF16)
        with xn_c as xn:
            B_xn = [Buf() for _ in TT]
            phase_norm(l, 1, xn, B_xn)
            phase_ffn_up(l, xn, B_xn)
        phase_ffn_down(l)
    if stop_after is None:
        phase_norm(0, 0, None, None, final=True)
    p.barrier()
    return p.emit()


def _host_consts():
    cst = np.zeros((128, NCST), np.float32)
    cst[:, C_ID:C_ID + 128] = np.eye(128, dtype=np.float32)
    s = np.arange(128)[:, None]
    t = np.arange(128)[None, :]
    cst[:, C_MF:C_MF + 128] = np.where(s <= t, 0.0, NEG)
    cst[:, C_MB:C_MB + 128] = np.where(s >= t, 0.0, NEG)
    for h in range(4):
        cst[h, C_SEL + 128 * h:C_SEL + 128 * h + 128] = 1.0
        cst[h, C_OH + 128 * h:C_OH + 128 * h + 128] = 1.0
    cst[0:4, C_I4:C_I4 + 4] = np.eye(4, dtype=np.float32)
    ang = 2 * np.pi * (np.arange(128)[:, None] * np.arange(128)[None, :] % 128) / 128.0
    cst[:, C_DFT:C_DFT + 128] = np.cos(ang)
    cst[:, C_DFT + 128:C_DFT + 256] = np.sin(ang)
    cst[0:4, NCST - 128:NCST] = 1.0

    def seq_dft(T):
        idx = (np.arange(T, dtype=np.int64)[:, None] * np.arange(T, dtype=np.int64)[None, :]) % T
        a = 2 * np.pi * idx / T
        sc = 1.0 / np.sqrt(T * 128.0)
        return np.stack([np.cos(a) * sc, -np.sin(a) * sc]).astype(np.float32)
    return cst, seq_dft(SEQ), seq_dft(L_CTX)


def _pcols(v):
    return np.ascontiguousarray(v.reshape(-1, 128).T)


def _host_vec(inp):
    vec = np.zeros((DEPTH, 128, NV), np.float32)
    for l in range(DEPTH):
        vec[l, :, V_N1:V_N1 + 16] = _pcols(inp["norm1_w"][l])
        vec[l, :, V_N2:V_N2 + 16] = _pcols(inp["norm2_w"][l])
        vec[l, :, V_BMOD:V_BMOD + 96] = _pcols(inp["b_mod"][l])
        for n, (kind, i, off) in enumerate(FM):
            vec[l, :, V_BIN + n] = inp["b_in"][l, off:off + 128]
        for j in range(3):
            vec[l, :, V_CQ + 8 * j:V_CQ + 8 * j + 8] = _pcols(inp["conv_q_w"][l, j])
            vec[l, :, V_CK + 8 * j:V_CK + 8 * j + 8] = _pcols(inp["conv_k_w"][l, j])
            vec[l, :, V_CC + 4 * j:V_CC + 4 * j + 4] = _pcols(inp["conv_c_w"][l, j])
            vec[l, :, V_CFF + NFF * j:V_CFF + NFF * j + NFF] = _pcols(inp["conv_ff_w"][l, j])
        vec[l, :, V_MN:V_MN + 8] = _pcols(inp["mlstm_norm_w"][l])
        vec[l, :, V_FN:V_FN + 16] = _pcols(inp["final_norm_w"])
    bvrep = np.ascontiguousarray(np.broadcast_to(inp["b_in"][:, None, 1024:2048], (DEPTH, 128, 1024))).astype(np.float32)
    bgt = np.ascontiguousarray(inp["b_in"][:, OFF_GATES:OFF_GATES + 16].reshape(DEPTH, 4, 4).transpose(0, 2, 1)).astype(np.float32)
    return vec, bvrep, bgt


def make_in_maps(inp, cores):
    cst, dftl, dftc = _host_consts()
    vec, bvrep, bgt = _host_vec(inp)
    shared = {"vec": vec, "bvrep": bvrep, "bgt": bgt, "cst": cst, "dft_lat": dftl, "dft_ctx": dftc}
    for k in ("w_mod", "w_in", "w_pm", "w_pf", "w_pc", "w_o", "w_up", "w_down"):
        shared[k] = np.ascontiguousarray(inp[k], dtype=np.float32)
    maps = []
    for b in cores:
        hT0 = np.ascontiguousarray(np.concatenate([inp["ctx"][b], inp["x"][b]], axis=0).T)
        cc2 = np.stack([inp["c"][b], inp["c_ctx"]], axis=1)
        ccp = np.ascontiguousarray(cc2.reshape(16, 128, 2).transpose(1, 0, 2).reshape(128, 32))
        m = dict(shared)
        m["hT0"] = hT0.astype(np.float32)
        m["ccp"] = ccp.astype(np.float32)
        maps.append(m)
    return maps


def kernel(**inputs):
    inp = {k: np.asarray(v) for k, v in inputs.items()}
    nc = build()
    real = make_in_maps(inp, [0, 1, 2, 3])
    idle = {k: np.zeros_like(v) for k, v in real[0].items()}
    maps = []
    for b in range(4):
        maps.append(real[b])
        maps.append(idle)
    res = run_bass_kernel_spmd(nc, maps, core_ids=list(range(8)))
    out = np.stack([res.results[2 * b]["outT"].T for b in range(4)], axis=0)
    return np.ascontiguousarray(out.astype(np.float32))
```
